# Optimizing a Trainium2 kernel written in Bass

```python
import jax, jax.numpy as jnp
from jax import lax
import numpy as np

D_MODEL = 2048
BATCH = 8
SEQ = 2048
DEPTH = 2

GRID_W = 64
CTX_LEN = 256
D_CONV = D_MODEL // 2
CONV_W = 3
HG_DK = 128
HG_DV = 128
D_F = D_MODEL // 2
D_V = D_MODEL // 2
HG_HEADS = D_F // HG_DK
CHUNK = 64
PEER_HEADS = 8
N_KEYS = 128
N_EXPERTS = N_KEYS * N_KEYS
D_QUERY = 256
TOPK_HALF = 16
TOPK = 16
PEER_BLOCK = 128
IN_WIDTH = 3 * D_CONV + 3 * D_F + 2 * D_V + 2 * D_MODEL
ALPHA = (2.0 * DEPTH) ** 0.25
BETA = (8.0 * DEPTH) ** -0.25
EPS = 1e-6
F_MIN = 1e-30

kernel_name = 'hybrid_conv_hgrn2_peer_diffusion_block'


def _layer_norm(x, g=None, b=None):
    xf = x.astype(jnp.float32)
    mu = jnp.mean(xf, axis=-1, keepdims=True)
    var = jnp.mean(jnp.square(xf - mu), axis=-1, keepdims=True)
    y = (xf - mu) * lax.rsqrt(var + EPS)
    if g is not None:
        y = y * g.astype(jnp.float32) + b.astype(jnp.float32)
    return y.astype(x.dtype)


def _modulate(x, shift, scale):
    return _layer_norm(x) * (1 + scale) + shift


def _split_in(p):
    sizes = (D_CONV,) * 3 + (D_F,) * 3 + (D_V,) * 2 + (D_MODEL,) * 2
    cuts = [int(s) for s in np.cumsum(sizes)[:-1]]
    return jnp.split(p, cuts, axis=-1)


def _short_conv(u, w, b):
    pad = [(0, 0)] * (u.ndim - 2) + [(1, 1), (0, 0)]
    up = jnp.pad(u, pad)
    return up[..., :-2, :] * w[0] + up[..., 1:-1, :] * w[1] + up[..., 2:, :] * w[2] + b


def _conv_branch(cb, cc, cv, w, b, grid):
    bsz, t, ch = cv.shape
    u = cc * cv
    if grid:
        rows = t // GRID_W
        u = u.reshape(bsz, rows, GRID_W, ch)
    y = _short_conv(u, w, b).reshape(bsz, t, ch)
    return cb * y


def _heads(a, dh):
    bsz, t, _ = a.shape
    return a.reshape(bsz, t, -1, dh).transpose(0, 2, 1, 3)


def _forget_terms(z, lb):
    zf = z.astype(jnp.float32)
    f = lb + (1.0 - lb) * jax.nn.sigmoid(zf)
    logf = jnp.log(jnp.maximum(f, F_MIN))
    k = (1.0 - lb) * jax.nn.sigmoid(-zf)
    return _heads(logf, HG_DK), _heads(k, HG_DK)


def _chunk_scan(q, k, v, logf, s0):
    bsz, nh, t, dk = q.shape
    n = t // CHUNK

    def to_chunks(a):
        return jnp.moveaxis(a.reshape(bsz, nh, n, CHUNK, a.shape[-1]), 2, 0)

    mask = jnp.tril(jnp.ones((CHUNK, CHUNK), dtype=bool))[:, :, None]

    def step(s, inp):
        qb, kb, vb, gb = inp
        bcum = jnp.cumsum(gb.astype(jnp.float32), axis=2)
        diff = bcum[:, :, :, None, :] - bcum[:, :, None, :, :]
        decay = jnp.where(mask, jnp.exp(jnp.where(mask, diff, 0.0)), 0.0)
        scores = jnp.einsum('bhtk,bhsk,bhtsk->bhts', qb, kb, decay)
        o = (jnp.einsum('bhts,bhsv->bhtv', scores, vb)
             + jnp.einsum('bhtk,bhkv->bhtv', qb * jnp.exp(bcum), s))
        blast = bcum[:, :, -1:, :]
        s_new = (jnp.exp(blast[:, :, 0, :])[..., None] * s
                 + jnp.einsum('bhsk,bhsv->bhkv', kb * jnp.exp(blast - bcum), vb))
        return s_new, o

    s_fin, oc = lax.scan(step, s0, (to_chunks(q), to_chunks(k), to_chunks(v), to_chunks(logf)))
    o = jnp.moveaxis(oc, 0, 2).reshape(bsz, nh, t, v.shape[-1])
    return o, s_fin


def _scan_dir(q, k, v, logf, s0, reverse):
    if reverse:
        q, k, v, logf = (jnp.flip(a, axis=2) for a in (q, k, v, logf))
    o, s = _chunk_scan(q, k, v, logf, s0)
    if reverse:
        o = jnp.flip(o, axis=2)
    return o, s


def _hgrn_readout(o, og, norm_g):
    o = o * lax.rsqrt(jnp.mean(jnp.square(o), axis=-1, keepdims=True) + EPS) * norm_g.astype(jnp.float32)
    bsz, nh, t, dv = o.shape
    o = o.transpose(0, 2, 1, 3).reshape(bsz, t, nh * dv)
    return (o * jax.nn.silu(og.astype(jnp.float32))).astype(og.dtype)


def _merge(ya, yb, ga, gb, w_pa, w_pb, w_o):
    m = jax.nn.sigmoid(ga) * (ya @ w_pa) + jax.nn.sigmoid(gb) * (yb @ w_pb)
    return m @ w_o


def _token_mixer(h, hc, w_in, conv_w, conv_b, lb, norm_g, w_pa, w_pb, w_o, ctx_out):
    cb, cc, cv, q, zf, zb, vi, og, ga, gb = _split_in(h @ w_in)
    ccb, ccc, ccv, qc, zfc, zbc, vic, ogc, gac, gbc = _split_in(hc @ w_in)
    qh = _heads(jax.nn.silu(q) * HG_DK ** -0.5, HG_DK)
    vh = _heads(vi, HG_DV)
    qhc = _heads(jax.nn.silu(qc) * HG_DK ** -0.5, HG_DK)
    vhc = _heads(vic, HG_DV)
    s0 = jnp.zeros((h.shape[0], HG_HEADS, HG_DK, HG_DV), jnp.float32)
    outs_lat, outs_ctx = [], []
    for d, (z_lat, z_ctx) in enumerate(((zf, zfc), (zb, zbc))):
        rev = d == 1
        logf_c, k_c = _forget_terms(z_ctx, lb[d])
        o_c, s_c = _scan_dir(qhc, k_c, vhc, logf_c, s0, rev)
        logf_l, k_l = _forget_terms(z_lat, lb[d])
        o_l, _ = _scan_dir(qh, k_l, vh, logf_l, s_c, rev)
        outs_lat.append(o_l)
        outs_ctx.append(o_c)
    y_b = _hgrn_readout(outs_lat[0] + outs_lat[1], og, norm_g)
    y_a = _conv_branch(cb, cc, cv, conv_w, conv_b, grid=True)
    y = _merge(y_a, y_b, ga, gb, w_pa, w_pb, w_o)
    if not ctx_out:
        return y, None
    y_bc = _hgrn_readout(outs_ctx[0] + outs_ctx[1], ogc, norm_g)
    y_ac = _conv_branch(ccb, ccc, ccv, conv_w, conv_b, grid=False)
    yc = _merge(y_ac, y_bc, gac, gbc, w_pa, w_pb, w_o)
    return y, yc


def _peer_route(xf, w_q, keys):
    n = xf.shape[0]
    q = (xf @ w_q).reshape(n, PEER_HEADS, 2, D_QUERY // 2)
    s = jnp.einsum('nhpd,hpkd->nhpk', q, keys).astype(jnp.float32)
    sv, si = lax.top_k(s, TOPK_HALF)
    cand = sv[:, :, 0, :, None] + sv[:, :, 1, None, :]
    cand_idx = si[:, :, 0, :, None] * N_KEYS + si[:, :, 1, None, :]
    top_v, top_pos = lax.top_k(cand.reshape(n, PEER_HEADS, TOPK_HALF * TOPK_HALF), TOPK)
    idx = jnp.take_along_axis(cand_idx.reshape(n, PEER_HEADS, TOPK_HALF * TOPK_HALF), top_pos, axis=-1)
    g = jax.nn.softmax(top_v, axis=-1)
    return idx.reshape(n, PEER_HEADS * TOPK), g.reshape(n, PEER_HEADS * TOPK)


def _peer_ffn(x, w_q, keys, u, v):
    shape = x.shape
    xf = x.reshape(-1, D_MODEL)
    idx, g = _peer_route(xf, w_q, keys)
    nb = xf.shape[0] // PEER_BLOCK

    def block(args):
        xb, ib, gb = args
        ue = u[ib]
        act = jax.nn.gelu(jnp.einsum('nd,nkd->nk', xb, ue).astype(jnp.float32), approximate=False)
        return jnp.einsum('nk,nkd->nd', (gb * act).astype(xb.dtype), v[ib])

    out = lax.map(block, (xf.reshape(nb, PEER_BLOCK, D_MODEL),
                          idx.reshape(nb, PEER_BLOCK, -1),
                          g.reshape(nb, PEER_BLOCK, -1)))
    return out.reshape(shape)


def setup_inputs(seed: int = 0) -> dict:
    key = jax.random.key(seed)
    ks = jax.random.split(key, 22)
    D = D_MODEL

    def nrm(k, shape, s):
        return jax.random.normal(k, shape, jnp.float32) * s

    return {
        'x': nrm(ks[0], (BATCH, SEQ, D), 1.0),
        'c': nrm(ks[1], (BATCH, D), 1.0),
        'ctx': nrm(ks[2], (BATCH, CTX_LEN, D), 1.0),
        'c_ctx': nrm(ks[3], (D,), 1.0),
        'w_mod': nrm(ks[4], (DEPTH, D, 6 * D), 0.5 * D ** -0.5),
        'b_mod': nrm(ks[5], (DEPTH, 6 * D), 0.01),
        'w_in': nrm(ks[6], (DEPTH, D, IN_WIDTH), D ** -0.5),
        'conv_w': nrm(ks[7], (DEPTH, CONV_W, D_CONV), CONV_W ** -0.5),
        'conv_b': nrm(ks[8], (DEPTH, D_CONV), 0.01),
        'lb_raw': 1.0 + nrm(ks[9], (DEPTH, 2, D_F), 0.1),
        'hg_norm_g': 1.0 + nrm(ks[10], (DEPTH, HG_DV), 0.02),
        'w_pa': nrm(ks[11], (DEPTH, D_CONV, D), BETA * D_CONV ** -0.5),
        'w_pb': nrm(ks[12], (DEPTH, D_V, D), BETA * D_V ** -0.5),
        'w_o': nrm(ks[13], (DEPTH, D, D), BETA * D ** -0.5),
        'ln1_g': 1.0 + nrm(ks[14], (DEPTH, D), 0.02),
        'ln1_b': nrm(ks[15], (DEPTH, D), 0.01),
        'peer_wq': nrm(ks[16], (DEPTH, D, PEER_HEADS * D_QUERY), D ** -0.5),
        'peer_keys': nrm(ks[17], (DEPTH, PEER_HEADS, 2, N_KEYS, D_QUERY // 2), (D_QUERY // 2) ** -0.5),
        'peer_u': nrm(ks[18], (DEPTH, N_EXPERTS, D), D ** -0.5),
        'peer_v': nrm(ks[19], (DEPTH, N_EXPERTS, D), BETA * PEER_HEADS ** -0.5),
        'ln2_g': 1.0 + nrm(ks[20], (DEPTH, D), 0.02),
        'ln2_b': nrm(ks[21], (DEPTH, D), 0.01),
    }


def reference(x, c, ctx, c_ctx, w_mod, b_mod, w_in, conv_w, conv_b, lb_raw, hg_norm_g,
              w_pa, w_pb, w_o, ln1_g, ln1_b, peer_wq, peer_keys, peer_u, peer_v, ln2_g, ln2_b):
    p = jax.nn.softmax(lb_raw.astype(jnp.float32), axis=0)
    lower_bounds = jnp.cumsum(p, axis=0) - p[:1]
    sc = jax.nn.silu(c)
    scc = jax.nn.silu(c_ctx)
    for l in range(DEPTH):
        last = l == DEPTH - 1
        mod = (sc @ w_mod[l] + b_mod[l])[:, None, :]
        mod_c = scc @ w_mod[l] + b_mod[l]
        sh1, s1, g1, sh2, s2, g2 = jnp.split(mod, 6, axis=-1)
        csh1, cs1, cg1, csh2, cs2, cg2 = jnp.split(mod_c, 6, axis=-1)
        y, yc = _token_mixer(_modulate(x, sh1, s1), _modulate(ctx, csh1, cs1), w_in[l],
                             conv_w[l], conv_b[l], lower_bounds[l], hg_norm_g[l],
                             w_pa[l], w_pb[l], w_o[l], ctx_out=not last)
        x = _layer_norm(ALPHA * x + g1 * y, ln1_g[l], ln1_b[l])
        f = _peer_ffn(_modulate(x, sh2, s2), peer_wq[l], peer_keys[l], peer_u[l], peer_v[l])
        x = _layer_norm(ALPHA * x + g2 * f, ln2_g[l], ln2_b[l])
        if not last:
            ctx = _layer_norm(ALPHA * ctx + cg1 * yc, ln1_g[l], ln1_b[l])
            fc = _peer_ffn(_modulate(ctx, csh2, cs2), peer_wq[l], peer_keys[l], peer_u[l], peer_v[l])
            ctx = _layer_norm(ALPHA * ctx + cg2 * fc, ln2_g[l], ln2_b[l])
    return x
```

```python
import contextlib
import numpy as np
import concourse.bass as bass
import concourse.mybir as mybir
from concourse.bass_utils import run_bass_kernel_spmd

DT = mybir.dt
F32 = DT.float32
BF16 = DT.bfloat16
I32 = DT.int32
U32 = DT.uint32
ALU = mybir.AluOpType
AF = mybir.ActivationFunctionType
AX = mybir.AxisListType

D = 2048
SEQ = 2048
CTXL = 256
TT = SEQ + CTXL
DEPTH = 2
NCORE = 8
DC = 1024
NH = 8
DK = 128
CH = 64
NCHUNK = TT // CH
INW = 12288
ALPHA = (2.0 * DEPTH) ** 0.25
EPS = 1e-6
NEXP = 16384
FM_CB, FM_CC, FM_CV, FM_Q, FM_ZF, FM_ZB, FM_OG, FM_GA, FM_GB = 0, 1024, 2048, 3072, 4096, 5120, 6144, 7168, 9216
FM_ROWS = 11264

ENGS = ("tensor", "vector", "scalar", "gpsimd", "sync")
SAME_ENG_GAP = 2


class Prog:
    def __init__(self, nc, n_dma_slots=12):
        self.nc = nc
        self.stack = contextlib.ExitStack()
        self.q = {e: [] for e in ENGS}
        self.cnt = {e: 0 for e in ENGS}
        self.pending = {e: False for e in ENGS}
        self.seen = {e: {} for e in ENGS}
        self.last_w = {}
        self.readers = {}
        self.sems = {}
        for e in ENGS:
            if e != "sync":
                self.sems[e] = self.stack.enter_context(nc.semaphore("s_" + e))
        self.dma_slots = {}
        self.dma_rr = {}
        for qn in ("sync", "gpsimd", "scalar"):
            self.dma_slots[qn] = []
            self.dma_rr[qn] = 0
            for i in range(n_dma_slots):
                key = ("dma", qn, i)
                self.sems[key] = self.stack.enter_context(nc.semaphore(f"d_{qn}_{i}"))
                self.dma_slots[qn].append([key, 0])
        self.n_inst = 0
        self.stage_stack = None

    def begin_stage(self):
        self.stage_id = getattr(self, "stage_id", 0) + 1
        self.stage_stack = contextlib.ExitStack()
        return self.stage_stack

    def sb(self, name, shape, dtype):
        return self.stage_stack.enter_context(self.nc.sbuf_tensor(f"{name}_s{self.stage_id}", list(shape), dtype))

    def ps(self, name, shape, dtype=F32):
        return self.stage_stack.enter_context(self.nc.psum_tensor(f"{name}_s{self.stage_id}", list(shape), dtype))

    def _need(self, eng, tok, waits):
        if tok is None:
            return
        key, val = tok
        if key == eng and eng == "tensor":
            return
        if key == eng and self.cnt[eng] - val >= SAME_ENG_GAP:
            return
        if self.seen[eng].get(key, 0) >= val:
            return
        self.seen[eng][key] = val
        waits.append((key, val))

    def _deps(self, eng, reads, writes):
        waits = []
        for r in reads:
            self._need(eng, self.last_w.get(r), waits)
        for w in writes:
            self._need(eng, self.last_w.get(w), waits)
            for t in self.readers.get(w, ()):
                self._need(eng, t, waits)
        return waits

    def _commit(self, tok, reads, writes):
        for r in reads:
            self.readers.setdefault(r, []).append(tok)
        for w in writes:
            self.last_w[w] = tok
            self.readers[w] = []

    def op(self, eng, fn, reads=(), writes=(), inc=True):
        waits = self._deps(eng, reads, writes)
        tok = (eng, self.cnt[eng] + 1)
        if inc:
            self.cnt[eng] += 1
            self.pending[eng] = False
        else:
            self.pending[eng] = True
        self.q[eng].append((waits, fn, eng if inc else None, 1))
        self._commit(tok, reads, writes)
        self.n_inst += 1
        return tok

    def dma(self, qn, fn, reads=(), writes=()):
        slots = self.dma_slots[qn]
        i = self.dma_rr[qn]
        self.dma_rr[qn] = (i + 1) % len(slots)
        slot = slots[i]
        waits = self._deps(qn, reads, writes)
        if slot[1] > 0:
            self._need(qn, (slot[0], slot[1]), waits)
        slot[1] += 16
        tok = (slot[0], slot[1])
        self.q[qn].append((waits, fn, slot[0], 16))
        self._commit(tok, reads, writes)
        self.n_inst += 1
        return tok

    def barrier(self):
        for e in ENGS:
            waits = []
            for x in ENGS:
                if x != "sync" and x != e and self.cnt[x] > 0:
                    self._need(e, (x, self.cnt[x]), waits)
            if e != "sync" and e != "tensor" and self.cnt[e] > 0:
                self._need(e, (e, self.cnt[e]), waits)
            for qn in self.dma_slots:
                for key, val in self.dma_slots[qn]:
                    if val > 0:
                        self._need(e, (key, val), waits)
            self.q[e].append((waits, None, None, 0))

    def end_stage(self):
        self.barrier()
        for e in ENGS:
            assert not self.pending[e], f"engine {e} ends stage with un-inc'ed instruction"
        nc = self.nc
        q = self.q
        sems = self.sems
        with nc.Block() as block:
            def run(ename, engine):
                for waits, fn, inc_key, amt in q[ename]:
                    for key, val in waits:
                        engine.wait_ge(sems[key], val)
                    if fn is None:
                        continue
                    ins = fn(engine)
                    if inc_key is not None:
                        ins.then_inc(sems[inc_key], amt)

            @block.tensor
            def _(e):
                run("tensor", e)

            @block.vector
            def _(e):
                run("vector", e)

            @block.scalar
            def _(e):
                run("scalar", e)

            @block.gpsimd
            def _(e):
                run("gpsimd", e)

            @block.sync
            def _(e):
                run("sync", e)
        self.q = {e: [] for e in ENGS}
        self.last_w = {}
        self.readers = {}
        self.stage_stack.close()
        self.stage_stack = None

    def close(self):
        self.stack.close()


def _layer_stats(P, pref, xt, xkey, bst, mv, rs):
    for j in range(4):
        P.op("vector", lambda e, j=j: e.bn_stats(out=bst[:, j * 6:(j + 1) * 6], in_=xt[:, j * 512:(j + 1) * 512]),
             reads=[xkey], writes=[pref + "bst"])
    P.op("vector", lambda e: e.bn_aggr(out=mv[:], in_=bst[:]), reads=[pref + "bst"], writes=[pref + "mv"])
    P.op("vector", lambda e: e.tensor_scalar(out=rs[:], in0=mv[:, 1:2], scalar1=EPS, scalar2=None, op0=ALU.add),
         reads=[pref + "mv"], writes=[pref + "rs"])
    P.op("scalar", lambda e: e.activation(out=rs[:], in_=rs[:], func=AF.Ln), reads=[pref + "rs"], writes=[pref + "rs"])
    P.op("scalar", lambda e: e.activation(out=rs[:], in_=rs[:], func=AF.Exp, scale=-0.5),
         reads=[pref + "rs"], writes=[pref + "rs"])


def _make_ident(P, ident_f, ident_b):
    P.op("gpsimd", lambda e: e.memset(ident_f[:], 1.0), writes=["ident_f"])
    P.op("gpsimd", lambda e: e.affine_select(out=ident_f[:], in_=ident_f[:], pattern=[[-1, 128]],
                                             compare_op=ALU.is_equal, fill=0.0, base=0, channel_multiplier=1),
         reads=["ident_f"], writes=["ident_f"])
    if ident_b is not None:
        P.op("vector", lambda e: e.tensor_copy(out=ident_b[:], in_=ident_f[:]), reads=["ident_f"], writes=["ident_b"])


def stage_mod(P, l, T):
    P.begin_stage()
    craw = P.sb("m_craw", [128, 2, 16], F32)
    sig = P.sb("m_sig", [128, 2, 16], F32)
    c2 = P.sb("m_c2", [128, 16, 2], F32)
    bm = P.sb("m_bm", [2, INW], F32)
    res = P.sb("m_res", [2, INW], F32)
    wt = [P.sb(f"m_wt{i}", [128, 16, 512], F32) for i in range(2)]
    pss = [P.ps(f"m_ps{i}", [2, 512]) for i in range(2)]
    P.dma("sync", lambda e: e.dma_start(out=craw[:, 0, :], in_=T["c"].rearrange("(p k) -> p k", k=16)), writes=["craw"])
    P.dma("sync", lambda e: e.dma_start(out=craw[:, 1, :], in_=T["c_ctx"].rearrange("(p k) -> p k", k=16)), writes=["craw"])
    P.dma("sync", lambda e: e.dma_start(out=bm[:], in_=T["b_mod"][l:l + 1, :].partition_broadcast(2)), writes=["bm"])
    P.op("scalar", lambda e: e.activation(out=sig[:], in_=craw[:], func=AF.Sigmoid), reads=["craw"], writes=["sig"])
    P.op("vector", lambda e: e.tensor_tensor(out=c2[:].rearrange("p k j -> p j k"), in0=craw[:], in1=sig[:], op=ALU.mult),
         reads=["craw", "sig"], writes=["c2"])
    wsrc = T["w_mod"][l].rearrange("(p k) n -> p k n", k=16)
    for n in range(24):
        w = wt[n % 2]
        wk = f"wt{n % 2}"
        P.dma("sync", lambda e, w=w, n=n: e.dma_start(out=w[:], in_=wsrc[:, :, n * 512:(n + 1) * 512]), writes=[wk])
        ps = pss[n % 2]
        pk = f"mps{n % 2}"
        for k in range(16):
            P.op("tensor", lambda e, ps=ps, w=w, k=k: e.matmul(ps[:], lhsT=c2[:, k, :], rhs=w[:, k, :],
                                                              start=(k == 0), stop=(k == 15)),
                 reads=["c2", wk], writes=[pk], inc=(k == 15))
        P.op("vector", lambda e, ps=ps, n=n: e.tensor_tensor(out=res[:, n * 512:(n + 1) * 512], in0=ps[:],
                                                            in1=bm[:, n * 512:(n + 1) * 512], op=ALU.add),
             reads=[pk, "bm"], writes=["res"])
    P.dma("sync", lambda e: e.dma_start(out=T["modv"][l], in_=res[:]), reads=["res"], writes=["modv"])
    P.end_stage()


def _tok_src(T, l, i):
    if l == 0:
        if i < 2:
            return T["ctx"][i * 128:(i + 1) * 128, :]
        return T["x"][(i - 2) * 128:(i - 1) * 128, :]
    return T["xcur"][i * 128:(i + 1) * 128, :]


def stage_A(P, l, T):
    P.begin_stage()
    NT = TT // 128
    hT = P.sb("a_hT", [128, 16, TT], BF16)
    ident_f = P.sb("a_idf", [128, 128], F32)
    ident_b = P.sb("a_idb", [128, 128], BF16)
    xbuf = [P.sb(f"a_x{i}", [128, D], F32) for i in range(2)]
    hb = P.sb("a_hb", [128, D], BF16)
    bc = {k: P.sb("a_bc_" + k, [128, D], F32) for k in ("s_c", "sh_c", "s_l", "sh_l")}
    bst = P.sb("a_bst", [128, 24], F32)
    mv = P.sb("a_mv", [128, 2], F32)
    rs = P.sb("a_rs", [128, 1], F32)
    ptr = P.ps("a_ptr", [128, D], BF16)
    _make_ident(P, ident_f, ident_b)
    modv = T["modv"][l]
    for nm, row, col in (("sh_c", 1, 0), ("s_c", 1, 1), ("sh_l", 0, 0), ("s_l", 0, 1)):
        P.dma("sync", lambda e, nm=nm, row=row, col=col: e.dma_start(
            out=bc[nm][:], in_=modv[row:row + 1, col * D:(col + 1) * D].partition_broadcast(128)),
            reads=["modv"], writes=["bc" + nm])
    for nm in ("s_c", "s_l"):
        P.op("gpsimd", lambda e, nm=nm: e.tensor_scalar(out=bc[nm][:], in0=bc[nm][:], scalar1=1.0, scalar2=None, op0=ALU.add),
             reads=["bc" + nm], writes=["bc" + nm])
    for i in range(NT):
        xt = xbuf[i % 2]
        xk = f"ax{i % 2}"
        P.dma("sync", lambda e, xt=xt, i=i: e.dma_start(out=xt[:], in_=_tok_src(T, l, i)), reads=["xcur"], writes=[xk])
        _layer_stats(P, "a_", xt, xk, bst, mv, rs)
        sfx = "c" if i < 2 else "l"
        P.op("vector", lambda e, xt=xt: e.tensor_scalar(out=xt[:], in0=xt[:], scalar1=mv[:, 0:1], scalar2=rs[:, 0:1],
                                                       op0=ALU.subtract, op1=ALU.mult),
             reads=[xk, "a_mv", "a_rs"], writes=[xk])
        P.op("vector", lambda e, xt=xt, sfx=sfx: e.tensor_tensor(out=xt[:], in0=xt[:], in1=bc["s_" + sfx][:], op=ALU.mult),
             reads=[xk, "bcs_" + sfx], writes=[xk])
        P.op("vector", lambda e, xt=xt, sfx=sfx: e.tensor_tensor(out=hb[:], in0=xt[:], in1=bc["sh_" + sfx][:], op=ALU.add),
             reads=[xk, "bcsh_" + sfx], writes=["hb"])
        for k in range(16):
            P.op("tensor", lambda e, k=k: e.transpose(ptr[:, k * 128:(k + 1) * 128], hb[:, k * 128:(k + 1) * 128], ident_b[:]),
                 reads=["hb", "ident_b"], writes=["ptr"], inc=(k == 15))
        P.op("scalar", lambda e, i=i: e.copy(out=hT[:, :, i * 128:(i + 1) * 128],
                                             in_=ptr[:].rearrange("p (k t) -> p k t", t=128)),
             reads=["ptr"], writes=["hT"])
    if T.get("dbg_h") is not None:
        P.dma("sync", lambda e: e.dma_start(out=T["dbg_h"].rearrange("(k p) t -> p k t", p=128), in_=hT[:]),
              reads=["hT"], writes=["dbg_h"])
    CW = 256
    wst = [P.sb(f"a_wst{i}", [128, 16, CW], F32) for i in range(2)]
    wbf = [P.sb(f"a_wbf{i}", [128, 16, CW], BF16) for i in range(2)]
    stg = [P.sb(f"a_stg{i}", [128, TT], F32) for i in range(2)]
    tmp = [P.sb(f"a_tmp{i}", [128, 512], F32) for i in range(2)]
    vst = [P.sb(f"a_vst{i}", [128, CW], BF16) for i in range(2)]
    pss = [P.ps(f"a_ps{i}", [128, 512]) for i in range(4)]
    wsrc = T["w_in"][l].rearrange("(k p) n -> p k n", p=128)
    tchunks = [(0, 256)] + [(256 + 512 * j, 512) for j in range(4)]
    fam_of_col = [("cb", FM_CB), ("cc", FM_CC), ("cv", FM_CV), ("q", FM_Q), ("zf", FM_ZF), ("zb", FM_ZB),
                  ("v", None), ("og", FM_OG), ("ga", FM_GA), ("ga", FM_GA + 1024), ("gb", FM_GB), ("gb", FM_GB + 1024)]
    pi = 0
    si = 0
    ti = 0
    for n in range(INW // CW):
        ws = wst[n % 2]
        wb = wbf[n % 2]
        wsk, wbk = f"wst{n % 2}", f"wbf{n % 2}"
        P.dma("sync", lambda e, ws=ws, n=n: e.dma_start(out=ws[:], in_=wsrc[:, :, n * CW:(n + 1) * CW]), writes=[wsk])
        P.op("gpsimd", lambda e, ws=ws, wb=wb: e.tensor_copy(out=wb[:], in_=ws[:]), reads=[wsk], writes=[wbk])
        fam, row0 = fam_of_col[(n * CW) // 1024]
        coff = (n * CW) % 1024
        if fam == "v":
            for i in range(NT):
                ps = pss[pi % 4]
                pk = f"aps{pi % 4}"
                pi += 1
                for k in range(16):
                    P.op("tensor", lambda e, ps=ps, wb=wb, k=k, i=i: e.matmul(
                        ps[:, 0:CW], lhsT=hT[:, k, i * 128:(i + 1) * 128], rhs=wb[:, k, :], start=(k == 0), stop=(k == 15)),
                        reads=["hT", wbk], writes=[pk], inc=(k == 15))
                vs = vst[i % 2]
                vk = f"vst{i % 2}"
                P.op("scalar", lambda e, ps=ps, vs=vs: e.copy(out=vs[:], in_=ps[:, 0:CW]), reads=[pk], writes=[vk])
                P.dma("scalar", lambda e, vs=vs, i=i, coff=coff: e.dma_start(
                    out=T["VT"][i * 128:(i + 1) * 128, coff:coff + CW], in_=vs[:]), reads=[vk], writes=["VT"])
            continue
        for sub in range(CW // 128):
            sg = stg[si % 2]
            sk = f"stg{si % 2}"
            si += 1
            for (t0, tn) in tchunks:
                ps = pss[pi % 4]
                pk = f"aps{pi % 4}"
                pi += 1
                for k in range(16):
                    P.op("tensor", lambda e, ps=ps, wb=wb, k=k, sub=sub, t0=t0, tn=tn: e.matmul(
                        ps[:, 0:tn], lhsT=wb[:, k, sub * 128:(sub + 1) * 128], rhs=hT[:, k, t0:t0 + tn],
                        start=(k == 0), stop=(k == 15)),
                        reads=["hT", wbk], writes=[pk], inc=(k == 15))
                dst = sg[:, t0:t0 + tn]
                if fam in ("cb", "cc", "cv"):
                    P.op("scalar", lambda e, ps=ps, dst=dst, tn=tn: e.copy(out=dst, in_=ps[:, 0:tn]), reads=[pk], writes=[sk])
                elif fam in ("zf", "zb", "ga", "gb"):
                    P.op("scalar", lambda e, ps=ps, dst=dst, tn=tn: e.activation(out=dst, in_=ps[:, 0:tn], func=AF.Sigmoid),
                         reads=[pk], writes=[sk])
                else:
                    tm = tmp[ti % 2]
                    tk = f"atmp{ti % 2}"
                    ti += 1
                    P.op("scalar", lambda e, ps=ps, tm=tm, tn=tn: e.activation(out=tm[:, 0:tn], in_=ps[:, 0:tn], func=AF.Sigmoid),
                         reads=[pk], writes=[tk])
                    scl = DK ** -0.5 if fam == "q" else 1.0
                    P.op("vector", lambda e, ps=ps, tm=tm, dst=dst, tn=tn, scl=scl: e.scalar_tensor_tensor(
                        out=dst, in0=ps[:, 0:tn], scalar=scl, in1=tm[:, 0:tn], op0=ALU.mult, op1=ALU.mult),
                        reads=[pk, tk], writes=[sk])
            r0 = row0 + coff + sub * 128
            P.dma("scalar", lambda e, sg=sg, r0=r0: e.dma_start(out=T["FM"][r0:r0 + 128, :], in_=sg[:]),
                  reads=[sk], writes=["FM"])
    P.end_stage()


def stage_B(P, l, T):
    P.begin_stage()
    W3 = lambda ap: ap.rearrange("p (c s) -> p c s", s=CH)
    ident_f = P.sb("b_idf", [128, 128], F32)
    ident_b = P.sb("b_idb", [128, 128], BF16)
    ones_f = P.sb("b_ones", [128, 128], F32)
    maskc = P.sb("b_maskc", [128, TT], F32)
    msk = {"f": P.sb("b_mskf", [64, 64], F32), "b": P.sb("b_mskb", [64, 64], F32)}
    lbr = P.sb("b_lbr", [128, 2, 2, NH], F32)
    lb = P.sb("b_lb", [128, 2, NH], F32)
    oml = P.sb("b_oml", [128, 2, NH], F32)
    noml = P.sb("b_noml", [128, 2, NH], F32)
    gn = P.sb("b_gn", [128, 1], F32)
    qs = P.sb("b_qs", [128, TT], F32)
    sgt = {"f": P.sb("b_sgf", [128, TT], F32), "b": P.sb("b_sgb", [128, TT], F32)}
    ogs = P.sb("b_ogs", [128, TT], F32)
    Vc = P.sb("b_Vc", [64, NCHUNK, 128], BF16)
    oT = P.sb("b_oT", [128, TT], F32)
    wkbig = P.sb("b_wkbig", [128, 6, TT], F32)
    wk = [wkbig[:, i, :] for i in range(6)]
    tot = P.sb("b_tot", [128, NCHUNK], F32)
    ebl = {d: P.sb("b_ebl" + d, [128, NCHUNK], F32) for d in "fb"}
    qt = {d: P.sb("b_qt" + d, [128, TT], BF16) for d in "fb"}
    kt = {d: P.sb("b_kt" + d, [128, TT], BF16) for d in "fb"}
    kf = {d: P.sb("b_kf" + d, [128, TT], BF16) for d in "fb"}
    qb = {d: P.sb("b_qb" + d, [128, TT], BF16) for d in "fb"}
    kd = {d: P.sb("b_kd" + d, [128, TT], BF16) for d in "fb"}
    kdT = {d: P.sb("b_kdT" + d, [64, NCHUNK, 128], BF16) for d in "fb"}
    S = {d: [P.sb(f"b_S{d}{i}", [128, 128], F32) for i in range(2)] for d in "fb"}
    scm_all = {d: P.sb("b_scm" + d, [64, NCHUNK, CH], BF16) for d in "fb"}
    upd_all = {"f": wkbig[:, 0:2, :].rearrange("p a t -> p (a t)").rearrange("p (c k) -> p c k", k=128),
               "b": wkbig[:, 2:4, :].rearrange("p a t -> p (a t)").rearrange("p (c k) -> p c k", k=128)}
    upd_keys = {"f": ["wk0", "wk1"], "b": ["wk2", "wk3"]}
    Sb_all = {"f": wkbig[:, 4, :].bitcast(BF16).rearrange("p (c k) -> p c k", k=128),
              "b": wkbig[:, 5, :].bitcast(BF16).rearrange("p (c k) -> p c k", k=128)}
    Sb_keys = {"f": "wk4", "b": "wk5"}
    rsb = P.sb("b_rsb", [128, 512], F32)
    ybt = P.sb("b_ybt", [128, TT], BF16)
    ps_scl = [P.ps(f"b_pssc{i}", [64, 512]) for i in range(2)]
    ps_o = [P.ps("b_pso0", [128, 512])]
    ps_upl = [P.ps(f"b_psup{i}", [128, 512]) for i in range(2)]
    ps_trl = [P.ps(f"b_pstr{i}", [64, 4 * 128], BF16) for i in range(2)]
    ps_ss = P.ps("b_psss", [128, 512])

    _make_ident(P, ident_f, ident_b)
    P.op("gpsimd", lambda e: e.memset(ones_f[:], 1.0), writes=["ones_f"])
    P.op("gpsimd", lambda e: e.memset(maskc[:], 1.0), writes=["maskc"])
    P.op("gpsimd", lambda e: e.memset(W3(maskc[:])[:, :, 0:1], 0.0), reads=["maskc"], writes=["maskc"])
    for d, pat, cm in (("f", 1, -1), ("b", -1, 1)):
        P.op("gpsimd", lambda e, d=d: e.memset(msk[d][:], 1.0), writes=["msk" + d])
        P.op("gpsimd", lambda e, d=d, pat=pat, cm=cm: e.affine_select(
            out=msk[d][:], in_=msk[d][:], pattern=[[pat, 64]], compare_op=ALU.is_ge, fill=0.0, base=0,
            channel_multiplier=cm), reads=["msk" + d], writes=["msk" + d])
    for j in range(2):
        for dd in range(2):
            P.dma("sync", lambda e, j=j, dd=dd: e.dma_start(
                out=lbr[:, j, dd, :], in_=T["lb_raw"][j, dd, :].rearrange("(h p) -> p h", p=128),
                allow_slow_non_contiguous=True), writes=["lbr"])
    P.dma("sync", lambda e: e.dma_start(out=gn[:], in_=T["hg_norm_g"][l, :].rearrange("(p o) -> p o", o=1)), writes=["gn"])
    if l == 0:
        P.op("vector", lambda e: e.memset(lb[:], 0.0), writes=["lb"])
    else:
        P.op("vector", lambda e: e.tensor_tensor(out=lb[:], in0=lbr[:, 0, :, :], in1=lbr[:, 1, :, :], op=ALU.subtract),
             reads=["lbr"], writes=["lb"])
        P.op("scalar", lambda e: e.activation(out=lb[:], in_=lb[:], func=AF.Sigmoid), reads=["lb"], writes=["lb"])
    P.op("vector", lambda e: e.tensor_scalar(out=oml[:], in0=lb[:], scalar1=-1.0, scalar2=1.0, op0=ALU.mult, op1=ALU.add),
         reads=["lb"], writes=["oml"])
    P.op("vector", lambda e: e.tensor_scalar(out=noml[:], in0=oml[:], scalar1=-1.0, scalar2=None, op0=ALU.mult),
         reads=["oml"], writes=["noml"])

    fwd_order = list(range(NCHUNK))
    bwd_order = [3, 2, 1, 0] + list(range(NCHUNK - 1, 3, -1))
    for hd in range(NH):
        r = hd * 128
        P.dma("sync", lambda e, r=r: e.dma_start(out=qs[:], in_=T["FM"][FM_Q + r:FM_Q + r + 128, :]), reads=["FM"], writes=["qs"])
        P.dma("sync", lambda e, r=r: e.dma_start(out=sgt["f"][:], in_=T["FM"][FM_ZF + r:FM_ZF + r + 128, :]), reads=["FM"], writes=["sgf"])
        P.dma("sync", lambda e, r=r: e.dma_start(out=sgt["b"][:], in_=T["FM"][FM_ZB + r:FM_ZB + r + 128, :]), reads=["FM"], writes=["sgb"])
        P.dma("sync", lambda e, r=r: e.dma_start(out=ogs[:], in_=T["FM"][FM_OG + r:FM_OG + r + 128, :]), reads=["FM"], writes=["ogs"])
        P.dma("sync", lambda e, r=r: e.dma_start(out=Vc[:], in_=T["VT"][:, r:r + 128].rearrange("(c s) v -> s c v", s=CH)),
              reads=["VT"], writes=["Vc"])
        for jj in range(32):
            cjob = hd * 32 + jj
            tb, c = cjob % 2, cjob // 2
            src_tab = T["peer_u"] if tb == 0 else T["peer_v"]
            P.dma("gpsimd", lambda e, src_tab=src_tab, tb=tb, c=c: e.dma_start(
                out=T["UVb"][l][c * 128:(c + 1) * 128, tb, :], in_=src_tab[l, c * 128:(c + 1) * 128, :]), writes=[("UVb", cjob)])
        for di, d in enumerate("fb"):
            sg = sgt[d]
            sgk = "sg" + d
            lf, bcs, cc, kk, e1, e2 = wk
            a_oml, a_lb, a_noml = oml[:, di, hd:hd + 1], lb[:, di, hd:hd + 1], noml[:, di, hd:hd + 1]
            P.op("vector", lambda e, sg=sg, a_oml=a_oml, a_lb=a_lb: e.tensor_scalar(
                out=lf[:], in0=sg[:], scalar1=a_oml, scalar2=a_lb, op0=ALU.mult, op1=ALU.add),
                reads=[sgk, "oml", "lb"], writes=["wk0"])
            P.op("scalar", lambda e: e.activation(out=lf[:], in_=lf[:], func=AF.Ln), reads=["wk0"], writes=["wk0"])
            P.op("vector", lambda e: e.tensor_tensor_scan(out=bcs[:], data0=maskc[:], data1=lf[:], initial=0.0,
                                                          op0=ALU.mult, op1=ALU.add),
                 reads=["wk0", "maskc"], writes=["wk1"])
            P.op("vector", lambda e: e.tensor_copy(out=tot[:], in_=W3(bcs[:])[:, :, CH - 1]), reads=["wk1"], writes=["tot"])
            totb = tot[:].unsqueeze(2).to_broadcast([128, NCHUNK, CH])
            if d == "f":
                c_ap, ck = bcs, "wk1"
            else:
                P.op("vector", lambda e: e.tensor_tensor(out=cc[:], in0=lf[:], in1=bcs[:], op=ALU.subtract),
                     reads=["wk0", "wk1"], writes=["wk2"])
                P.op("vector", lambda e, totb=totb: e.tensor_tensor(out=W3(cc[:]), in0=W3(cc[:]), in1=totb, op=ALU.add),
                     reads=["wk2", "tot"], writes=["wk2"])
                c_ap, ck = cc, "wk2"
            P.op("scalar", lambda e, d=d: e.activation(out=ebl[d][:], in_=tot[:], func=AF.Exp), reads=["tot"], writes=["ebl" + d])
            P.op("vector", lambda e, sg=sg, a_noml=a_noml, a_oml=a_oml: e.tensor_scalar(
                out=kk[:], in0=sg[:], scalar1=a_noml, scalar2=a_oml, op0=ALU.mult, op1=ALU.add),
                reads=[sgk, "oml", "noml"], writes=["wk3"])
            W4 = lambda ap: ap.rearrange("p (c h s) -> p c h s", h=2, s=CH // 2)
            h1, h2, i1, ib = (0, 1, 15, 31) if d == "f" else (1, 0, 47, 32)
            c4, c3 = W4(c_ap[:]), W3(c_ap[:])
            r1b = c3[:, :, i1:i1 + 1].to_broadcast([128, NCHUNK, CH // 2])
            rbb = c3[:, :, ib:ib + 1].to_broadcast([128, NCHUNK, CH // 2])
            rbb64 = c3[:, :, ib:ib + 1].to_broadcast([128, NCHUNK, CH])
            P.op("vector", lambda e, c4=c4, r1b=r1b, h1=h1: e.tensor_tensor(out=W4(e1[:])[:, :, h1, :], in0=c4[:, :, h1, :], in1=r1b, op=ALU.subtract),
                 reads=[ck], writes=["wk4"])
            P.op("vector", lambda e, c4=c4, rbb=rbb, h2=h2: e.tensor_tensor(out=W4(e1[:])[:, :, h2, :], in0=c4[:, :, h2, :], in1=rbb, op=ALU.subtract),
                 reads=[ck], writes=["wk4"])
            P.op("gpsimd", lambda e, h2=h2: e.memset(W4(e2[:])[:, :, h2, :], 0.0), writes=["wk5"])
            P.op("scalar", lambda e, h1=h1: e.activation(out=W4(e2[:])[:, :, h1, :], in_=W4(e1[:])[:, :, h1, :], func=AF.Exp, scale=-1.0),
                 reads=["wk4"], writes=["wk5"])
            P.op("vector", lambda e, d=d: e.tensor_tensor(out=kt[d][:], in0=kk[:], in1=e2[:], op=ALU.mult),
                 reads=["wk3", "wk5"], writes=["kt" + d])
            P.op("scalar", lambda e: e.activation(out=e1[:], in_=e1[:], func=AF.Exp), reads=["wk4"], writes=["wk4"])
            P.op("vector", lambda e, d=d: e.tensor_tensor(out=qt[d][:], in0=qs[:], in1=e1[:], op=ALU.mult),
                 reads=["qs", "wk4"], writes=["qt" + d])
            P.op("vector", lambda e, c3=c3, rbb64=rbb64: e.tensor_tensor(out=W3(e2[:]), in0=rbb64, in1=c3, op=ALU.subtract),
                 reads=[ck, "kt" + d], writes=["wk5"])
            P.op("scalar", lambda e: e.activation(out=e2[:], in_=e2[:], func=AF.Exp), reads=["wk5"], writes=["wk5"])
            P.op("vector", lambda e, d=d: e.tensor_tensor(out=kf[d][:], in0=kk[:], in1=e2[:], op=ALU.mult),
                 reads=["wk3", "wk5"], writes=["kf" + d])
            P.op("scalar", lambda e, c_ap=c_ap: e.activation(out=lf[:], in_=c_ap[:], func=AF.Exp), reads=[ck], writes=["wk0"])
            P.op("vector", lambda e, d=d: e.tensor_tensor(out=qb[d][:], in0=qs[:], in1=lf[:], op=ALU.mult),
                 reads=["qs", "wk0"], writes=["qb" + d])
            P.op("vector", lambda e, c_ap=c_ap, totb=totb: e.tensor_tensor(out=W3(e1[:]), in0=totb, in1=W3(c_ap[:]), op=ALU.subtract),
                 reads=[ck, "tot", "qt" + d], writes=["wk4"])
            P.op("scalar", lambda e: e.activation(out=e1[:], in_=e1[:], func=AF.Exp), reads=["wk4"], writes=["wk4"])
            P.op("vector", lambda e, d=d: e.tensor_tensor(out=kd[d][:], in0=kk[:], in1=e1[:], op=ALU.mult),
                 reads=["wk3", "wk4"], writes=["kd" + d])
            for g4 in range(NCHUNK // 4):
                ps_tr, ptk = ps_trl[g4 % 2], f"ps_tr{g4 % 2}"
                for j in range(4):
                    c = g4 * 4 + j
                    P.op("tensor", lambda e, d=d, c=c, j=j, ps_tr=ps_tr: e.transpose(ps_tr[:, j * 128:(j + 1) * 128],
                                                                       kd[d][:, c * CH:(c + 1) * CH], ident_b[:]),
                         reads=["kd" + d, "ident_b"], writes=[ptk], inc=(j == 3))
                P.op("scalar", lambda e, d=d, g4=g4, ps_tr=ps_tr: e.copy(out=kdT[d][:, g4 * 4:(g4 + 1) * 4, :],
                                                          in_=ps_tr[:].rearrange("s (j k) -> s j k", k=128)),
                     reads=[ptk], writes=["kdT" + d])
        order = {"f": fwd_order, "b": bwd_order}
        posn = {d: {c: k for k, c in enumerate(order[d])} for d in "fb"}
        groups8 = [(g0, min(8, NCHUNK - g0)) for g0 in range(0, NCHUNK, 8)]
        for d in "fb":
            for g4 in range(NCHUNK // 4):
                pu, puk = ps_upl[g4 % 2], f"ps_up{g4 % 2}"
                for j in range(4):
                    c = g4 * 4 + j
                    P.op("tensor", lambda e, d=d, c=c, j=j, pu=pu: e.matmul(pu[:, j * 128:(j + 1) * 128], lhsT=kdT[d][:, c, :], rhs=Vc[:, c, :],
                                                                   start=True, stop=True),
                         reads=["kdT" + d, "Vc"], writes=[puk], inc=(j == 3))
                P.op("scalar", lambda e, d=d, g4=g4, pu=pu: e.copy(out=upd_all[d][:, g4 * 4:(g4 + 1) * 4, :],
                                                          in_=pu[:].rearrange("p (j k) -> p j k", k=128)),
                     reads=[puk], writes=upd_keys[d])
        def p1_group(d, g0, gcnt, sci):
            psc, psk = ps_scl[sci % 2], f"ps_sc{sci % 2}"
            kA, kB = (kt[d], kf[d]) if d == "f" else (kf[d], kt[d])
            for j in range(gcnt):
                c = g0 + j
                cs = slice(c * CH, (c + 1) * CH)
                hA = slice(c * CH, c * CH + CH // 2)
                hB = slice(c * CH + CH // 2, (c + 1) * CH)
                P.op("tensor", lambda e, cs=cs, hA=hA, j=j: e.matmul(psc[:, j * CH:j * CH + CH // 2], lhsT=kA[:, cs], rhs=qt[d][:, hA], start=True, stop=True),
                     reads=["kt" + d, "kf" + d, "qt" + d], writes=[psk], inc=False)
                P.op("tensor", lambda e, cs=cs, hB=hB, j=j: e.matmul(psc[:, j * CH + CH // 2:(j + 1) * CH], lhsT=kB[:, cs], rhs=qt[d][:, hB], start=True, stop=True),
                     reads=["kt" + d, "kf" + d, "qt" + d], writes=[psk], inc=(j == gcnt - 1))
            P.op("vector", lambda e: e.tensor_tensor(
                out=scm_all[d][:, g0:g0 + gcnt, :], in0=psc[:, 0:gcnt * CH].rearrange("s (j t) -> s j t", t=CH),
                in1=msk[d][:].unsqueeze(1).to_broadcast([64, gcnt, CH]), op=ALU.mult),
                reads=[psk, "msk" + d], writes=["scm" + d])
        p1_jobs = [(d, g0, gcnt) for (g0, gcnt) in groups8 for d in "fb"]
        p1_done = 0
        for d in "fb":
            P.op("gpsimd", lambda e, d=d: e.memset(S[d][1][:], 0.0), writes=[("S", d, 1)])
            P.op("gpsimd", lambda e, d=d: e.memset(Sb_all[d][:, 0, :], 0.0), writes=[Sb_keys[d]])
        for k in range(NCHUNK - 1):
            if k % 3 == 0 and p1_done < len(p1_jobs):
                p1_group(*p1_jobs[p1_done], p1_done)
                p1_done += 1
            for d in "fb":
                c = order[d][k]
                So, Sn = S[d][(k + 1) % 2], S[d][k % 2]
                P.op("vector", lambda e, d=d, c=c, So=So, Sn=Sn: e.scalar_tensor_tensor(out=Sn[:], in0=So[:], scalar=ebl[d][:, c:c + 1],
                                                                                   in1=upd_all[d][:, c, :], op0=ALU.mult, op1=ALU.add),
                     reads=[("S", d, (k + 1) % 2), "ebl" + d] + upd_keys[d], writes=[("S", d, k % 2)])
                P.op("scalar", lambda e, d=d, k=k, Sn=Sn: e.copy(out=Sb_all[d][:, k + 1, :], in_=Sn[:]),
                     reads=[("S", d, k % 2)], writes=[Sb_keys[d]])
        while p1_done < len(p1_jobs):
            p1_group(*p1_jobs[p1_done], p1_done)
            p1_done += 1
        for gi_, (g0, gcnt) in enumerate(groups8):
            po, pk = ps_o[0], "ps_o0"
            for j in range(gcnt):
                c = g0 + j
                cs = slice(c * CH, (c + 1) * CH)
                dst = po[:, j * CH:(j + 1) * CH]
                P.op("tensor", lambda e, c=c, dst=dst: e.matmul(dst, lhsT=Vc[:, c, :], rhs=scm_all["f"][:, c, :], start=True, stop=False),
                     reads=["Vc", "scmf"], writes=[pk], inc=False)
                P.op("tensor", lambda e, c=c, dst=dst: e.matmul(dst, lhsT=Vc[:, c, :], rhs=scm_all["b"][:, c, :], start=False, stop=False),
                     reads=["Vc", "scmb"], writes=[pk], inc=False)
                P.op("tensor", lambda e, c=c, cs=cs, dst=dst: e.matmul(dst, lhsT=Sb_all["f"][:, posn["f"][c], :], rhs=qb["f"][:, cs], start=False, stop=False),
                     reads=[Sb_keys["f"], "qbf"], writes=[pk], inc=False)
                P.op("tensor", lambda e, c=c, cs=cs, dst=dst: e.matmul(dst, lhsT=Sb_all["b"][:, posn["b"][c], :], rhs=qb["b"][:, cs], start=False, stop=True),
                     reads=[Sb_keys["b"], "qbb"], writes=[pk], inc=(j == gcnt - 1))
            P.op("scalar", lambda e, po=po, g0=g0, gcnt=gcnt: e.copy(out=oT[:, g0 * CH:(g0 + gcnt) * CH], in_=po[:, 0:gcnt * CH]),
                 reads=[pk], writes=[("oT", g0)])
        okeys = [("oT", g0) for g0 in range(0, NCHUNK, 8)]
        osq = wk[0]
        P.op("scalar", lambda e: e.activation(out=osq[:], in_=oT[:], func=AF.Square), reads=okeys, writes=["wk0"])
        for (t0, tn) in [(0, 256)] + [(256 + 512 * j, 512) for j in range(4)]:
            P.op("tensor", lambda e, t0=t0, tn=tn: e.matmul(ps_ss[:, 0:tn], lhsT=ones_f[:], rhs=osq[:, t0:t0 + tn], start=True, stop=True),
                 reads=["ones_f", "wk0"], writes=["ps_ss"])
            P.op("vector", lambda e, tn=tn: e.tensor_scalar(out=rsb[:, 0:tn], in0=ps_ss[:, 0:tn], scalar1=1.0 / 128, scalar2=EPS,
                                                           op0=ALU.mult, op1=ALU.add), reads=["ps_ss"], writes=["rsb"])
            P.op("scalar", lambda e, tn=tn: e.activation(out=rsb[:, 0:tn], in_=rsb[:, 0:tn], func=AF.Ln), reads=["rsb"], writes=["rsb"])
            P.op("scalar", lambda e, tn=tn: e.activation(out=rsb[:, 0:tn], in_=rsb[:, 0:tn], func=AF.Exp, scale=-0.5),
                 reads=["rsb"], writes=["rsb"])
            P.op("vector", lambda e, t0=t0, tn=tn: e.tensor_tensor(out=rsb[:, 0:tn], in0=rsb[:, 0:tn], in1=oT[:, t0:t0 + tn], op=ALU.mult),
                 reads=["rsb"] + okeys, writes=["rsb"])
            P.op("vector", lambda e, t0=t0, tn=tn: e.scalar_tensor_tensor(out=ybt[:, t0:t0 + tn], in0=rsb[:, 0:tn], scalar=gn[:, 0:1],
                                                                        in1=ogs[:, t0:t0 + tn], op0=ALU.mult, op1=ALU.mult),
                 reads=["rsb", "gn", "ogs"], writes=["ybt"])
        P.dma("scalar", lambda e, r=r: e.dma_start(out=T["YB"][r:r + 128, :], in_=ybt[:]), reads=["ybt"], writes=["YB"])
        if T.get("dbg_o") is not None:
            P.dma("scalar", lambda e, r=r: e.dma_start(out=T["dbg_o"][r:r + 128, :], in_=oT[:]), reads=okeys, writes=["dbg_o"])
    P.end_stage()


def _load_cast_weight(P, pref, dst, src, nk, stage, nq=2, stage_keys=None):
    view = src.rearrange("(k p) n -> p k n", p=128)
    for j in range(nk // nq):
        st = stage[j % 2]
        sk = stage_keys[j % 2] if stage_keys else f"wstage{j % 2}"
        P.dma("sync", lambda e, st=st, j=j: e.dma_start(out=st[:], in_=view[:, j * nq:(j + 1) * nq, :]), writes=[sk])
        eng = "gpsimd" if j % 2 == 0 else "scalar"
        if eng == "gpsimd":
            P.op("gpsimd", lambda e, st=st, j=j: e.tensor_copy(out=dst[:, j * nq:(j + 1) * nq, :], in_=st[:]), reads=[sk], writes=[pref])
        else:
            P.op("scalar", lambda e, st=st, j=j: e.copy(out=dst[:, j * nq:(j + 1) * nq, :], in_=st[:]), reads=[sk], writes=[pref])


def stage_C(P, l, T):
    last = l == DEPTH - 1
    P.begin_stage()
    Wpa = P.sb("c_wpa", [128, 8, D], BF16)
    Wpb = P.sb("c_wpb", [128, 8, D], BF16)
    stage = [P.sb(f"c_stage{i}", [128, 2, D], F32) for i in range(2)]
    cw = P.sb("c_cw", [128, 3, 8], F32)
    cbias = P.sb("c_cb", [128, 8], F32)
    cin = {k: [P.sb(f"c_in_{k}{i}", [128, 512], F32) for i in range(2)] for k in ("cb", "cc", "cv")}
    u = P.sb("c_u", [128, 512], F32)
    y = P.sb("c_y", [128, 512], F32)
    yaT = P.sb("c_yaT", [128, 8, 512], BF16)
    ybT = P.sb("c_ybT", [128, 8, 512], BF16)
    sga = [P.sb(f"c_sga{i}", [128, 512], F32) for i in range(2)]
    sgb = [P.sb(f"c_sgb{i}", [128, 512], F32) for i in range(2)]
    m1 = [P.sb(f"c_m1{i}", [128, 512], F32) for i in range(2)]
    m2 = [P.sb(f"c_m2{i}", [128, 512], F32) for i in range(2)]
    mT = P.sb("c_mT", [128, 16, 512], BF16)
    ps1 = [P.ps(f"c_ps1{i}", [128, 512]) for i in range(4)]
    ps2 = [P.ps(f"c_ps2{i}", [128, 512]) for i in range(4)]
    _load_cast_weight(P, "Wpa", Wpa, T["w_pa"][l], 8, stage)
    _load_cast_weight(P, "Wpb", Wpb, T["w_pb"][l], 8, stage)
    for k in range(3):
        P.dma("sync", lambda e, k=k: e.dma_start(out=cw[:, k, :], in_=T["conv_w"][l, k, :].rearrange("(c p) -> p c", p=128),
                                                allow_slow_non_contiguous=True), writes=["cw"])
    P.dma("sync", lambda e: e.dma_start(out=cbias[:], in_=T["conv_b"][l, :].rearrange("(c p) -> p c", p=128),
                                        allow_slow_non_contiguous=True), writes=["cbias"])
    tchunks = [(0, 256, 256)] + [(256 + 512 * j, 512, 64) for j in range(4)]
    if last:
        tchunks = tchunks[1:]
    it = 0
    for (t0, tn, rowlen) in tchunks:
        R3 = lambda ap, tn=tn, rowlen=rowlen: ap[:, 0:tn].rearrange("p (r s) -> p r s", s=rowlen)
        for ct in range(8):
            bufs = {}
            for k, base in (("cb", FM_CB), ("cc", FM_CC), ("cv", FM_CV)):
                b = cin[k][it % 2]
                bufs[k] = (b, f"cin{k}{it % 2}")
                P.dma("sync", lambda e, b=b, base=base, ct=ct, t0=t0, tn=tn: e.dma_start(
                    out=b[:, 0:tn], in_=T["FM"][base + ct * 128:base + (ct + 1) * 128, t0:t0 + tn]),
                    reads=["FM"], writes=[bufs[k][1]])
            it += 1
            (bcb, kcb), (bcc, kcc), (bcv, kcv) = bufs["cb"], bufs["cc"], bufs["cv"]
            P.op("vector", lambda e, bcc=bcc, bcv=bcv, tn=tn: e.tensor_tensor(out=u[:, 0:tn], in0=bcc[:, 0:tn], in1=bcv[:, 0:tn], op=ALU.mult),
                 reads=[kcc, kcv], writes=["u"])
            P.op("vector", lambda e, ct=ct, tn=tn: e.tensor_scalar(out=y[:, 0:tn], in0=u[:, 0:tn], scalar1=cw[:, 1, ct:ct + 1],
                                                                 scalar2=cbias[:, ct:ct + 1], op0=ALU.mult, op1=ALU.add),
                 reads=["u", "cw", "cbias"], writes=["y"])
            P.op("vector", lambda e, ct=ct, R3=R3, rowlen=rowlen: e.scalar_tensor_tensor(
                out=R3(y)[:, :, 1:rowlen], in0=R3(u)[:, :, 0:rowlen - 1], scalar=cw[:, 0, ct:ct + 1], in1=R3(y)[:, :, 1:rowlen],
                op0=ALU.mult, op1=ALU.add), reads=["u", "cw", "y"], writes=["y"])
            P.op("vector", lambda e, ct=ct, R3=R3, rowlen=rowlen: e.scalar_tensor_tensor(
                out=R3(y)[:, :, 0:rowlen - 1], in0=R3(u)[:, :, 1:rowlen], scalar=cw[:, 2, ct:ct + 1], in1=R3(y)[:, :, 0:rowlen - 1],
                op0=ALU.mult, op1=ALU.add), reads=["u", "cw", "y"], writes=["y"])
            P.op("vector", lambda e, ct=ct, bcb=bcb, tn=tn: e.tensor_tensor(out=yaT[:, ct, 0:tn], in0=bcb[:, 0:tn], in1=y[:, 0:tn], op=ALU.mult),
                 reads=[kcb, "y"], writes=["yaT"])
        P.dma("sync", lambda e, t0=t0, tn=tn: e.dma_start(out=ybT[:, :, 0:tn], in_=T["YB"][:, t0:t0 + tn].rearrange("(k p) t -> p k t", p=128)),
              reads=["YB"], writes=["ybT"])
        for mc in range(16):
            j = mc % 2
            jp = mc % 4
            P.dma("sync", lambda e, mc=mc, j=j, t0=t0, tn=tn: e.dma_start(
                out=sga[j][:, 0:tn], in_=T["FM"][FM_GA + mc * 128:FM_GA + (mc + 1) * 128, t0:t0 + tn]), reads=["FM"], writes=[f"sga{j}"])
            P.dma("sync", lambda e, mc=mc, j=j, t0=t0, tn=tn: e.dma_start(
                out=sgb[j][:, 0:tn], in_=T["FM"][FM_GB + mc * 128:FM_GB + (mc + 1) * 128, t0:t0 + tn]), reads=["FM"], writes=[f"sgb{j}"])
            for k in range(8):
                P.op("tensor", lambda e, mc=mc, k=k, jp=jp, tn=tn: e.matmul(ps1[jp][:, 0:tn], lhsT=Wpa[:, k, mc * 128:(mc + 1) * 128],
                                                                       rhs=yaT[:, k, 0:tn], start=(k == 0), stop=(k == 7)),
                     reads=["Wpa", "yaT"], writes=[f"cps1{jp}"], inc=(k == 7))
            for k in range(8):
                P.op("tensor", lambda e, mc=mc, k=k, jp=jp, tn=tn: e.matmul(ps2[jp][:, 0:tn], lhsT=Wpb[:, k, mc * 128:(mc + 1) * 128],
                                                                       rhs=ybT[:, k, 0:tn], start=(k == 0), stop=(k == 7)),
                     reads=["Wpb", "ybT"], writes=[f"cps2{jp}"], inc=(k == 7))
            P.op("vector", lambda e, j=j, jp=jp, tn=tn: e.tensor_tensor(out=m1[j][:, 0:tn], in0=ps1[jp][:, 0:tn], in1=sga[j][:, 0:tn], op=ALU.mult),
                 reads=[f"cps1{jp}", f"sga{j}"], writes=[f"m1{j}"])
            P.op("vector", lambda e, j=j, jp=jp, tn=tn: e.tensor_tensor(out=m2[j][:, 0:tn], in0=ps2[jp][:, 0:tn], in1=sgb[j][:, 0:tn], op=ALU.mult),
                 reads=[f"cps2{jp}", f"sgb{j}"], writes=[f"m2{j}"])
            P.op("gpsimd", lambda e, j=j, mc=mc, tn=tn: e.tensor_tensor(out=mT[:, mc, 0:tn], in0=m1[j][:, 0:tn], in1=m2[j][:, 0:tn], op=ALU.add),
                 reads=[f"m1{j}", f"m2{j}"], writes=["mT"])
        P.dma("scalar", lambda e, t0=t0, tn=tn: e.dma_start(out=T["MT"][:, t0:t0 + tn].rearrange("(k p) t -> p k t", p=128), in_=mT[:, :, 0:tn]),
              reads=["mT"], writes=["MT"])
        if T.get("dbg_ya") is not None:
            P.dma("scalar", lambda e, t0=t0, tn=tn: e.dma_start(out=T["dbg_ya"][:, t0:t0 + tn].rearrange("(k p) t -> p k t", p=128), in_=yaT[:, :, 0:tn]),
                  reads=["yaT"], writes=["dbg_ya"])
    P.end_stage()


def _bcast_load(P, dst, key, src_row):
    P.dma("sync", lambda e: e.dma_start(out=dst[:], in_=src_row.partition_broadcast(128)), reads=["modv"], writes=[key])


def _residual_ln(P, pref, src_ps_or_sb, src_key, xt, xk, gb, gkey, lng, lnb, r, rkey, bst, mv, rs, out_ap, out_key, out_dram, aff_eng="gpsimd", res_eng="vector"):
    P.op("vector", lambda e: e.tensor_tensor(out=r[:], in0=src_ps_or_sb, in1=gb[:], op=ALU.mult), reads=[src_key, gkey], writes=[rkey])
    P.op(res_eng, lambda e: e.scalar_tensor_tensor(out=r[:], in0=xt[:], scalar=ALPHA, in1=r[:], op0=ALU.mult, op1=ALU.add),
         reads=[xk, rkey], writes=[rkey])
    _layer_stats(P, pref, r, rkey, bst, mv, rs)
    P.op("vector", lambda e: e.tensor_scalar(out=r[:], in0=r[:], scalar1=mv[:, 0:1], scalar2=rs[:, 0:1], op0=ALU.subtract, op1=ALU.mult),
         reads=[rkey, pref + "mv", pref + "rs"], writes=[rkey])
    P.op(aff_eng, lambda e: e.tensor_tensor(out=r[:], in0=r[:], in1=lng[:], op=ALU.mult), reads=[rkey, "lng"], writes=[rkey])
    P.op(aff_eng, lambda e: e.tensor_tensor(out=out_ap, in0=r[:], in1=lnb[:], op=ALU.add), reads=[rkey, "lnb"], writes=[out_key])
    P.dma("scalar", lambda e: e.dma_start(out=out_dram, in_=out_ap), reads=[out_key], writes=["xout"])


def stage_D(P, l, T):
    last = l == DEPTH - 1
    P.begin_stage()
    Wo = P.sb("d_wo", [128, 16, D], BF16)
    stage = [P.sb(f"d_stage{i}", [128, 2, D], F32) for i in range(2)]
    mTc = [P.sb(f"d_mT{i}", [128, 16, 512], BF16) for i in range(2)]
    xb = [P.sb(f"d_x{i}", [128, D], F32) for i in range(2)]
    rrb = [P.sb(f"d_r{i}", [128, D], F32) for i in range(2)]
    ob = [P.sb(f"d_o{i}", [128, D], F32) for i in range(2)]
    g1b = P.sb("d_g1", [128, D], F32)
    lng = P.sb("d_lng", [128, D], F32)
    lnb = P.sb("d_lnb", [128, D], F32)
    bst = P.sb("d_bst", [128, 24], F32)
    mv = P.sb("d_mv", [128, 2], F32)
    rs = P.sb("d_rs", [128, 1], F32)
    psy = [P.ps(f"d_psy{i}", [128, D]) for i in range(2)]
    _load_cast_weight(P, "Wo", Wo, T["w_o"][l], 16, stage)
    P.dma("sync", lambda e: e.dma_start(out=lng[:], in_=T["ln1_g"][l:l + 1, :].partition_broadcast(128)), writes=["lng"])
    P.dma("sync", lambda e: e.dma_start(out=lnb[:], in_=T["ln1_b"][l:l + 1, :].partition_broadcast(128)), writes=["lnb"])
    modv = T["modv"][l]
    tchunks = [(0, 256)] + [(256 + 512 * j, 512) for j in range(4)]
    if last:
        tchunks = tchunks[1:]
    ti = 0
    for ci, (t0, tn) in enumerate(tchunks):
        if t0 == 0:
            _bcast_load(P, g1b, "g1b", modv[1:2, 2 * D:3 * D])
        elif t0 == 256:
            _bcast_load(P, g1b, "g1b", modv[0:1, 2 * D:3 * D])
        mt = mTc[ci % 2]
        mk = f"dmT{ci % 2}"
        P.dma("sync", lambda e, mt=mt, t0=t0, tn=tn: e.dma_start(out=mt[:, :, 0:tn], in_=T["MT"][:, t0:t0 + tn].rearrange("(k p) t -> p k t", p=128)),
              reads=["MT"], writes=[mk])
        for tt in range(tn // 128):
            i = (t0 + tt * 128) // 128
            xt = xb[ti % 2]
            xk = f"dx{ti % 2}"
            ot = ob[ti % 2]
            ok = f"do{ti % 2}"
            py = psy[ti % 2]
            pk = f"dpsy{ti % 2}"
            ti += 1
            P.dma("sync", lambda e, xt=xt, i=i: e.dma_start(out=xt[:], in_=_tok_src(T, l, i)), reads=["xcur"], writes=[xk])
            for nq in range(4):
                for k in range(16):
                    P.op("tensor", lambda e, py=py, mt=mt, tt=tt, k=k, nq=nq: e.matmul(
                        py[:, nq * 512:(nq + 1) * 512], lhsT=mt[:, k, tt * 128:(tt + 1) * 128], rhs=Wo[:, k, nq * 512:(nq + 1) * 512],
                        start=(k == 0), stop=(k == 15)), reads=[mk, "Wo"], writes=[pk], inc=(k == 15 and nq == 3))
            rr, rrk = rrb[(ti - 1) % 2], f"d_r{(ti - 1) % 2}"
            _residual_ln(P, "d_", py[:], pk, xt, xk, g1b, "g1b", lng, lnb, rr, rrk, bst, mv, rs, ot[:], ok,
                         T["x1"][i * 128:(i + 1) * 128, :])
    P.end_stage()


def _top16_multi(P, jobs):
    for (src, sk, wkb, wkeys, vals, vkey, idxs, ikey) in jobs:
        P.op("vector", lambda e, vals=vals, src=src: e.max(out=vals[:, 0:8], in_=src), reads=[sk], writes=[vkey])
        yield
    for (src, sk, wkb, wkeys, vals, vkey, idxs, ikey) in jobs:
        P.op("vector", lambda e, vals=vals, src=src, idxs=idxs: e.max_index(out=idxs[:, 0:8], in_max=vals[:, 0:8], in_values=src),
             reads=[sk, vkey], writes=[ikey])
        yield
    for (src, sk, wkb, wkeys, vals, vkey, idxs, ikey) in jobs:
        P.op("vector", lambda e, vals=vals, src=src, wkb=wkb: e.match_replace(out=wkb, in_to_replace=vals[:, 0:8], in_values=src, imm_value=-1e30),
             reads=[sk, vkey], writes=wkeys)
        yield
    for (src, sk, wkb, wkeys, vals, vkey, idxs, ikey) in jobs:
        P.op("vector", lambda e, vals=vals, wkb=wkb: e.max(out=vals[:, 8:16], in_=wkb), reads=wkeys, writes=[vkey])
        yield
    for (src, sk, wkb, wkeys, vals, vkey, idxs, ikey) in jobs:
        P.op("vector", lambda e, vals=vals, wkb=wkb, idxs=idxs: e.max_index(out=idxs[:, 8:16], in_max=vals[:, 8:16], in_values=wkb),
             reads=wkeys + [vkey], writes=[ikey])
        yield


def stage_T(P, l, T):
    P.begin_stage()
    for cjob in range(256):
        tb, c = cjob % 2, cjob // 2
        src_tab = T["peer_u"] if tb == 0 else T["peer_v"]
        P.dma("gpsimd", lambda e, src_tab=src_tab, tb=tb, c=c: e.dma_start(
            out=T["UVb"][l][c * 128:(c + 1) * 128, tb, :], in_=src_tab[l, c * 128:(c + 1) * 128, :]), writes=[("UVb", cjob)])
    P.end_stage()


def stage_Q(P, l, T):
    last = l == DEPTH - 1
    P.begin_stage()
    NT = TT // 128
    Wq = P.sb("q_wq", [128, 16, D], BF16)
    keysT = P.sb("q_keysT", [128, 16, 128], BF16)
    ident_f = P.sb("q_idf", [128, 128], F32)
    ident_b = P.sb("q_idb", [128, 128], BF16)
    bc = {k: P.sb("q_bc_" + k, [128, D], F32) for k in ("sh", "s")}
    x1b = [P.sb(f"q_x1{i}", [128, D], F32) for i in range(2)]
    tmpA = P.sb("q_tmpA", [128, D], F32)
    xmbb = [P.sb(f"q_xmb{i}", [128, D], BF16) for i in range(2)]
    scb = [P.sb(f"q_sc{i}", [128, D], F32) for i in range(2)]
    xmT = P.sb("q_xmT", [128, 16, 128], BF16)
    qT = P.sb("q_qT", [128, 16, 128], BF16)
    bst = P.sb("q_bst", [128, 24], F32)
    mv = P.sb("q_mv", [128, 2], F32)
    rs = P.sb("q_rs", [128, 1], F32)
    ps_t = P.ps("q_pst", [128, D], BF16)
    ps_q = P.ps("q_psq", [128, D])
    ps_s = P.ps("q_pss", [128, D // 2])
    _make_ident(P, ident_f, ident_b)
    _load_cast_weight(P, "Wq", Wq, T["peer_wq"][l], 16, [x1b[0][:].rearrange("p (a n) -> p a n", a=1), x1b[1][:].rearrange("p (a n) -> p a n", a=1)],
                      nq=1, stage_keys=["x1b0", "x1b1"])
    kraw = scb[0][:].rearrange("p (b d) -> p b d", d=128)
    P.dma("sync", lambda e: e.dma_start(out=kraw, in_=T["peer_keys"][l].rearrange("h p k d -> k (h p) d")), writes=["sc0"])
    for blk in range(16):
        P.op("tensor", lambda e, blk=blk: e.transpose(ps_q[:, blk * 128:(blk + 1) * 128], kraw[:, blk, :], ident_f[:]),
             reads=["sc0", "ident_f"], writes=["ps_q"], inc=(blk == 15))
    P.op("vector", lambda e: e.tensor_copy(out=keysT[:], in_=ps_q[:].rearrange("p (b k) -> p b k", k=128)), reads=["ps_q"], writes=["keysT"])
    modv = T["modv"][l]
    tiles = list(range(2, NT)) if last else list(range(NT))
    def lnt(ti, i):
        par = ti % 2
        x1t, kx = x1b[par], f"x1b{par}"
        xmb, kxmb = xmbb[par], f"xmb{par}"
        xm, kxm = tmpA, "tmpA"
        if i in (0, 2):
            row = 1 if i == 0 else 0
            _bcast_load(P, bc["sh"], "bcsh", modv[row:row + 1, 3 * D:4 * D])
            _bcast_load(P, bc["s"], "bcs", modv[row:row + 1, 4 * D:5 * D])
            P.op("gpsimd", lambda e: e.tensor_scalar(out=bc["s"][:], in0=bc["s"][:], scalar1=1.0, scalar2=None, op0=ALU.add),
                 reads=["bcs"], writes=["bcs"])
        P.dma("sync", lambda e: e.dma_start(out=x1t[:], in_=T["x1"][i * 128:(i + 1) * 128, :]), reads=["x1"], writes=[kx])
        _layer_stats(P, "q_", x1t, kx, bst, mv, rs)
        P.op("vector", lambda e: e.tensor_scalar(out=xm[:], in0=x1t[:], scalar1=mv[:, 0:1], scalar2=rs[:, 0:1], op0=ALU.subtract, op1=ALU.mult),
             reads=[kx, "q_mv", "q_rs"], writes=[kxm])
        P.op("vector", lambda e: e.tensor_tensor(out=xm[:], in0=xm[:], in1=bc["s"][:], op=ALU.mult), reads=[kxm, "bcs"], writes=[kxm])
        P.op("vector", lambda e: e.tensor_tensor(out=xmb[:], in0=xm[:], in1=bc["sh"][:], op=ALU.add), reads=[kxm, "bcsh"], writes=[kxmb])
        P.dma("scalar", lambda e: e.dma_start(out=T["XMB"][i * 128:(i + 1) * 128, :], in_=xmb[:]), reads=[kxmb], writes=["XMB"])

    def tr(ti):
        xmb, kxmb = xmbb[ti % 2], f"xmb{ti % 2}"
        for k in range(16):
            P.op("tensor", lambda e, k=k: e.transpose(ps_t[:, k * 128:(k + 1) * 128], xmb[:, k * 128:(k + 1) * 128], ident_b[:]),
                 reads=[kxmb, "ident_b"], writes=["ps_t"], inc=(k == 15))
        P.op("scalar", lambda e: e.copy(out=xmT[:], in_=ps_t[:].rearrange("p (k t) -> p k t", t=128)), reads=["ps_t"], writes=["xmT"])

    lnt(0, tiles[0])
    tr(0)
    for ti, i in enumerate(tiles):
        sc, ksc = scb[ti % 2], f"sc{ti % 2}"
        if ti + 1 < len(tiles):
            lnt(ti + 1, tiles[ti + 1])
        for blk in range(16):
            for k in range(16):
                P.op("tensor", lambda e, blk=blk, k=k: e.matmul(ps_q[:, blk * 128:(blk + 1) * 128], lhsT=Wq[:, k, blk * 128:(blk + 1) * 128],
                                                              rhs=xmT[:, k, :], start=(k == 0), stop=(k == 15)),
                     reads=["Wq", "xmT"], writes=["ps_q"], inc=(k == 15 and blk == 15))
        P.op("scalar", lambda e: e.copy(out=qT[:], in_=ps_q[:].rearrange("p (b t) -> p b t", t=128)), reads=["ps_q"], writes=["qT"])
        if ti + 1 < len(tiles):
            tr(ti + 1)
        for half in range(2):
            for b8 in range(8):
                blk = half * 8 + b8
                P.op("tensor", lambda e, blk=blk, b8=b8: e.matmul(ps_s[:, b8 * 128:(b8 + 1) * 128], lhsT=qT[:, blk, :], rhs=keysT[:, blk, :],
                                                               start=True, stop=True), reads=["qT", "keysT"], writes=["ps_s"], inc=(b8 == 7))
            P.op("scalar", lambda e, sc=sc, half=half: e.copy(out=sc[:, half * 1024:(half + 1) * 1024], in_=ps_s[:]), reads=["ps_s"], writes=[ksc])
        P.dma("scalar", lambda e, i=i, sc=sc: e.dma_start(out=T["SC"][i * 128:(i + 1) * 128, :], in_=sc[:]), reads=[ksc], writes=["SC"])
    P.end_stage()


def stage_E(P, l, T):
    last = l == DEPTH - 1
    P.begin_stage()
    NT = TT // 128
    NG = 10
    ident_f = P.sb("f_idf", [128, 128], F32)
    ident_b = P.sb("f_idb", [128, 128], BF16)
    iota16 = P.sb("f_iota", [128, 16], F32)
    bc = {k: P.sb("f_bc_" + k, [128, D], F32) for k in ("g", "lng", "lnb")}
    x1b = [P.sb(f"f_x1{i}", [128, D], F32) for i in range(2)]
    rb = [P.sb(f"f_r{i}", [128, D], F32) for i in range(2)]
    xmbb = [P.sb(f"f_xmb{i}", [128, D], BF16) for i in range(2)]
    sc = P.sb("f_sc", [128, D], F32)
    wk = P.sb("f_wk", [128, D], F32)
    cand = P.sb("f_cand", [128, D], F32)
    sv = P.sb("f_sv", [128, 16, 16], F32)
    si = P.sb("f_si", [128, 16, 16], U32)
    sif = P.sb("f_sif", [128, 16, 16], F32)
    tv = P.sb("f_tv", [128, 8, 16], F32)
    pos = P.sb("f_pos", [128, 8, 16], U32)
    posu = P.sb("f_posu", [128, 8, 16], U32)
    af = P.sb("f_af", [128, 8, 16], F32)
    bf = P.sb("f_bf", [128, 8, 16], F32)
    sel0 = P.sb("f_sel0", [128, 8, 16], F32)
    sel1 = P.sb("f_sel1", [128, 8, 16], F32)
    idxf = P.sb("f_idxf", [128, 128], F32)
    zz = P.sb("f_zz", [128, 8], F32)
    idxb = [P.sb(f"f_idx{i}", [128, 128], I32) for i in range(2)]
    ggb = [P.sb(f"f_gg{i}", [128, 8, 16], F32) for i in range(2)]
    dotb = [P.sb(f"f_dot{i}", [128, 128], F32) for i in range(2)]
    gelb = [P.sb(f"f_gel{i}", [128, 128], F32) for i in range(2)]
    aab = [P.sb(f"f_aa{i}", [128, 128], F32) for i in range(2)]
    gbuf = [P.sb(f"f_g{i}", [128, 2 * D], BF16) for i in range(NG)]
    diag = [P.sb(f"f_dg{i}", [128, 128], BF16) for i in range(NG)]
    junkl = [P.sb(f"f_junk{i}", [128, D], BF16) for i in range(3)]
    bst = P.sb("f_bst", [128, 24], F32)
    mv = P.sb("f_mv", [128, 2], F32)
    rs = P.sb("f_rs", [128, 1], F32)
    ps_f = [P.ps(f"f_psf{i}", [128, D]) for i in range(2)]
    _make_ident(P, ident_f, ident_b)
    P.op("gpsimd", lambda e: e.iota(iota16[:], pattern=[[1, 16]], base=0, channel_multiplier=0,
                                    allow_small_or_imprecise_dtypes=True), writes=["iota16"])
    P.dma("sync", lambda e: e.dma_start(out=bc["lng"][:], in_=T["ln2_g"][l:l + 1, :].partition_broadcast(128)), writes=["lng"])
    P.dma("sync", lambda e: e.dma_start(out=bc["lnb"][:], in_=T["ln2_b"][l:l + 1, :].partition_broadcast(128)), writes=["lnb"])
    modv = T["modv"][l]
    tiles = list(range(2, NT)) if last else list(range(NT))
    UVflat = T["UVb"][l].rearrange("e t d -> e (t d)")

    def route(ti, i):
        par = ti % 2
        gg, kgg = ggb[par], f"gg{par}"
        ii, kii = idxb[par], f"idx{par}"
        ksc = "sc"
        P.dma("sync", lambda e: e.dma_start(out=sc[:], in_=T["SC"][i * 128:(i + 1) * 128, :]), reads=["SC"], writes=[ksc])
        yield
        yield from _top16_multi(P, [(sc[:, blk * 128:(blk + 1) * 128], ksc, wk[:, blk * 128:(blk + 1) * 128], [("wk", blk)], sv[:, blk, :], ("sv", blk),
                                     si[:, blk, :], ("si", blk)) for blk in range(16)])
        svk = [("sv", b_) for b_ in range(16)]
        sik = [("si", b_) for b_ in range(16)]
        P.op("vector", lambda e: e.tensor_copy(out=sif[:], in_=si[:]), reads=sik, writes=["sif"])
        yield
        sv4 = sv[:].rearrange("p (h t) a -> p h t a", t=2)
        sif4 = sif[:].rearrange("p (h t) a -> p h t a", t=2)
        c4 = lambda ap: ap[:].rearrange("p (h a b) -> p h a b", a=16, b=16)
        bA = lambda v4: v4[:, :, 0, :].unsqueeze(3).to_broadcast([128, 8, 16, 16])
        bB = lambda v4: v4[:, :, 1, :].unsqueeze(2).to_broadcast([128, 8, 16, 16])
        P.op("vector", lambda e: e.tensor_tensor(out=c4(cand), in0=bA(sv4), in1=bB(sv4), op=ALU.add), reads=svk, writes=["cand"])
        yield
        yield from _top16_multi(P, [(cand[:, h * 256:(h + 1) * 256], "cand", wk[:, h * 256:(h + 1) * 256], [("wk", 2 * h), ("wk", 2 * h + 1)],
                                     tv[:, h, :], ("tv", h), pos[:, h, :], ("pos", h)) for h in range(8)])
        tvk = [("tv", h) for h in range(8)]
        posk = [("pos", h) for h in range(8)]
        P.op("vector", lambda e: e.tensor_tensor(out=gg[:], in0=tv[:], in1=tv[:, :, 0:1].to_broadcast([128, 8, 16]), op=ALU.subtract),
             reads=tvk, writes=[kgg])
        yield
        P.op("scalar", lambda e: e.activation(out=gg[:], in_=gg[:], func=AF.Exp), reads=[kgg], writes=[kgg])
        P.op("vector", lambda e: e.tensor_single_scalar(out=posu[:], in_=pos[:], scalar=4, op=ALU.logical_shift_right), reads=posk, writes=["posu"])
        yield
        P.op("vector", lambda e: e.tensor_copy(out=af[:], in_=posu[:]), reads=["posu"], writes=["af"])
        yield
        P.op("vector", lambda e: e.tensor_single_scalar(out=posu[:], in_=pos[:], scalar=15, op=ALU.bitwise_and), reads=posk + ["af"], writes=["posu"])
        yield
        P.op("vector", lambda e: e.tensor_copy(out=bf[:], in_=posu[:]), reads=["posu"], writes=["bf"])
        yield
        P.op("vector", lambda e: e.reduce_sum(out=zz[:], in_=gg[:], axis=AX.X), reads=[kgg], writes=["zz"])
        yield
        P.op("vector", lambda e: e.reciprocal(out=zz[:], in_=zz[:]), reads=["zz"], writes=["zz"])
        yield
        io4 = iota16[:].unsqueeze(1).unsqueeze(1).to_broadcast([128, 8, 16, 16])
        for (src, half, dst, dk_) in ((af, 0, sel0, "sel0"), (bf, 1, sel1, "sel1")):
            oh = c4(sc)
            P.op("vector", lambda e, src=src, oh=oh: e.tensor_tensor(out=oh, in0=io4, in1=src[:].unsqueeze(3).to_broadcast([128, 8, 16, 16]),
                                                                   op=ALU.is_equal), reads=["iota16", "af", "bf"], writes=[ksc])
            yield
            P.op("vector", lambda e, half=half, oh=oh: e.tensor_tensor(
                out=oh, in0=oh, in1=sif4[:, :, half, :].unsqueeze(2).to_broadcast([128, 8, 16, 16]), op=ALU.mult),
                reads=[ksc, "sif"], writes=[ksc])
            yield
            P.op("vector", lambda e, dst=dst, oh=oh: e.tensor_reduce(out=dst[:], in_=oh, axis=AX.X, op=ALU.add), reads=[ksc], writes=[dk_])
            yield
        P.op("vector", lambda e: e.tensor_tensor(out=gg[:], in0=gg[:], in1=zz[:].unsqueeze(2).to_broadcast([128, 8, 16]), op=ALU.mult),
             reads=[kgg, "zz"], writes=[kgg])
        yield
        P.op("vector", lambda e: e.scalar_tensor_tensor(out=idxf[:].rearrange("p (h k) -> p h k", k=16), in0=sel0[:], scalar=128.0, in1=sel1[:],
                                                       op0=ALU.mult, op1=ALU.add), reads=["sel0", "sel1"], writes=["idxf"])
        yield
        P.op("vector", lambda e: e.tensor_copy(out=ii[:], in_=idxf[:]), reads=["idxf"], writes=[kii])
        yield
        if T.get("dbg_route"):
            P.dma("scalar", lambda e: e.dma_start(out=T["IDX"][i * 128:(i + 1) * 128, :], in_=ii[:]), reads=[kii], writes=["IDX"])
            P.dma("scalar", lambda e: e.dma_start(out=T["GG"][i * 128:(i + 1) * 128, :], in_=gg[:].rearrange("p h k -> p (h k)")),
                  reads=[kgg], writes=["GG"])

    for _ in route(0, tiles[0]):
        pass
    gi = 0
    for ti, i in enumerate(tiles):
        if i in (0, 2):
            row = 1 if i == 0 else 0
            _bcast_load(P, bc["g"], "bcg", modv[row:row + 1, 5 * D:6 * D])
        par = ti % 2
        x1t, kx = x1b[par], f"x1b{par}"
        xmb, kxmb = xmbb[par], f"xmb{par}"
        idx, kidx = idxb[par], f"idx{par}"
        gg, kgg = ggb[par], f"gg{par}"
        ggf = gg[:].rearrange("p h k -> p (h k)")
        dot, gel, aa = dotb[par], gelb[par], aab[par]
        pf, kpf = ps_f[par], f"psf{par}"
        r, kr = rb[par], f"r{par}"
        P.dma("sync", lambda e, i=i, xmb=xmb: e.dma_start(out=xmb[:], in_=T["XMB"][i * 128:(i + 1) * 128, :]), reads=["XMB"], writes=[kxmb])
        P.dma("sync", lambda e, i=i, x1t=x1t: e.dma_start(out=x1t[:], in_=T["x1"][i * 128:(i + 1) * 128, :]), reads=["x1"], writes=[kx])
        gen = route(ti + 1, tiles[ti + 1]) if ti + 1 < len(tiles) else None
        for ee in range(128):
            j = gi % NG
            gi += 1
            gb_, gk = gbuf[j], ("gbuf", j)
            dg, dk_ = diag[j], ("diag", j)
            P.dma("gpsimd", lambda e, gb_=gb_, ee=ee, idx=idx: e.indirect_dma_start(
                out=gb_[:], out_offset=None, in_=UVflat,
                in_offset=bass.IndirectOffsetOnAxis(ap=idx[:, ee:ee + 1], axis=0)), reads=[kidx, "UVb"], writes=[gk])
            junk, jk = junkl[gi % 3], f"junk{gi % 3}"
            P.op("vector", lambda e, gb_=gb_, ee=ee, xmb=xmb, dot=dot, junk=junk: e.scalar_tensor_tensor(
                out=junk[:], in0=gb_[:, 0:D], scalar=1.0, in1=xmb[:], op0=ALU.mult, op1=ALU.mult, accum_out=dot[:, ee:ee + 1]),
                reads=[gk, kxmb], writes=[jk, ("dot", par, ee)])
            if gen is not None:
                for _ in range(2):
                    try:
                        next(gen)
                    except StopIteration:
                        gen = None
                        break
            P.op("scalar", lambda e, ee=ee, dot=dot, gel=gel: e.activation(out=gel[:, ee:ee + 1], in_=dot[:, ee:ee + 1], func=AF.Gelu),
                 reads=[("dot", par, ee)], writes=[("gel", par, ee)])
            P.op("scalar", lambda e, ee=ee, gel=gel, aa=aa, ggf=ggf: e.activation(out=aa[:, ee:ee + 1], in_=gel[:, ee:ee + 1], func=AF.Copy,
                                                                            scale=ggf[:, ee:ee + 1]),
                 reads=[("gel", par, ee), kgg], writes=[("aa", par, ee)])
            P.op("scalar", lambda e, ee=ee, aa=aa, dg=dg: e.activation(out=dg[:], in_=ident_b[:], func=AF.Copy, scale=aa[:, ee:ee + 1]),
                 reads=[("aa", par, ee), "ident_b"], writes=[dk_])
            for nq in range(4):
                P.op("tensor", lambda e, nq=nq, dg=dg, gb_=gb_, pf=pf, ee=ee: e.matmul(
                    pf[:, nq * 512:(nq + 1) * 512], lhsT=dg[:], rhs=gb_[:, D + nq * 512:D + (nq + 1) * 512],
                    start=(ee == 0), stop=(ee == 127)), reads=[dk_, gk], writes=[kpf], inc=(nq == 3))
        if gen is not None:
            for _ in gen:
                pass
        if T.get("dbg_f") is not None:
            P.op("vector", lambda e, pf=pf, r=r: e.tensor_copy(out=r[:], in_=pf[:]), reads=[kpf], writes=[kr])
            P.dma("scalar", lambda e, i=i, r=r: e.dma_start(out=T["dbg_f"][i * 128:(i + 1) * 128, :], in_=r[:]), reads=[kr], writes=["dbg_f"])
        if last:
            dst = T["out"][(i - 2) * 128:(i - 1) * 128, :]
        else:
            dst = T["xcur"][i * 128:(i + 1) * 128, :]
        _residual_ln(P, "f_", pf[:], kpf, x1t, kx, bc["g"], "bcg", bc["lng"], bc["lnb"], r, kr, bst, mv, rs, r[:], kr, dst, aff_eng="vector", res_eng="vector")
    P.end_stage()


def build(stop_after=None, dbg=(), only=None, ext_in=()):
    nc = bass.Bass("TRN2", target_bir_lowering=False)
    T = {}

    def inp(name, shape, dtype=F32):
        T[name] = nc.dram_tensor(name, list(shape), dtype, kind="ExternalInput").ap()

    def scr(name, shape, dtype=F32):
        kind = "ExternalOutput" if name in dbg else ("ExternalInput" if name in ext_in else "Internal")
        T[name] = nc.dram_tensor(name, list(shape), dtype, kind=kind).ap()

    inp("x", [SEQ, D]); inp("c", [D]); inp("ctx", [CTXL, D]); inp("c_ctx", [D])
    inp("w_mod", [DEPTH, D, 6 * D]); inp("b_mod", [DEPTH, 6 * D]); inp("w_in", [DEPTH, D, INW])
    inp("conv_w", [DEPTH, 3, DC]); inp("conv_b", [DEPTH, DC]); inp("lb_raw", [DEPTH, 2, DC])
    inp("hg_norm_g", [DEPTH, 128]); inp("w_pa", [DEPTH, DC, D]); inp("w_pb", [DEPTH, DC, D]); inp("w_o", [DEPTH, D, D])
    inp("ln1_g", [DEPTH, D]); inp("ln1_b", [DEPTH, D]); inp("peer_wq", [DEPTH, D, D])
    inp("peer_keys", [DEPTH, 8, 2, 128, 128]); inp("peer_u", [DEPTH, NEXP, D]); inp("peer_v", [DEPTH, NEXP, D])
    inp("ln2_g", [DEPTH, D]); inp("ln2_b", [DEPTH, D])
    T["out"] = nc.dram_tensor("out", [SEQ, D], F32, kind="ExternalOutput").ap()
    scr("modv", [DEPTH, 2, INW])
    scr("FM", [FM_ROWS, TT])
    scr("VT", [TT, DC], BF16)
    scr("xcur", [TT, D])
    scr("YB", [DC, TT], BF16)
    scr("MT", [D, TT], BF16)
    scr("IDX", [TT, 128], I32)
    scr("GG", [TT, 128])
    scr("XMB", [TT, D], BF16)
    scr("SC", [TT, D])
    T["dbg_route"] = ("IDX" in dbg)
    T["UVb"] = [nc.dram_tensor(f"UVb{l}", [NEXP, 2, D], BF16, kind="Internal").ap() for l in range(DEPTH)]
    scr("x1", [TT, D])
    for nm, shp, dt_ in (("dbg_ya", [DC, TT], BF16), ("dbg_idx", [TT, 128], I32), ("dbg_g", [TT, 128], F32), ("dbg_f", [TT, D], F32)):
        if nm in dbg:
            scr(nm, shp, dt_)
    if "dbg_h" in dbg:
        scr("dbg_h", [D, TT], BF16)
    if "dbg_o" in dbg:
        scr("dbg_o", [DC, TT])

    P = Prog(nc)
    stages = []
    for l in range(DEPTH):
        stages += [("M", l), ("A", l), ("B", l), ("C", l), ("D", l), ("Q", l), ("E", l)]
    if only is not None:
        stages = list(only)
    for (s, l) in stages:
        if only is not None and (s, l) not in only:
            continue
        if s == "M":
            stage_mod(P, l, T)
        elif s == "A":
            stage_A(P, l, T)
        elif s == "B":
            stage_B(P, l, T)
        elif s == "C":
            stage_C(P, l, T)
        elif s == "D":
            stage_D(P, l, T)
        elif s == "Q":
            stage_Q(P, l, T)
        elif s == "E":
            stage_E(P, l, T)
        elif s == "T":
            stage_T(P, l, T)
        if stop_after == (s, l):
            break
    P.close()
    return nc, P


def make_in_maps(inputs):
    shared = {k: np.ascontiguousarray(inputs[k]) for k in (
        "c_ctx", "w_mod", "b_mod", "w_in", "conv_w", "conv_b", "lb_raw", "hg_norm_g", "w_pa", "w_pb", "w_o",
        "ln1_g", "ln1_b", "peer_wq", "peer_keys", "peer_u", "peer_v", "ln2_g", "ln2_b")}
    maps = []
    for b in range(NCORE):
        m = dict(shared)
        m["x"] = np.ascontiguousarray(inputs["x"][b])
        m["c"] = np.ascontiguousarray(inputs["c"][b])
        m["ctx"] = np.ascontiguousarray(inputs["ctx"][b])
        maps.append(m)
    return maps


def kernel(**inputs):
    inputs = {k: np.asarray(v, dtype=np.float32) for k, v in inputs.items()}
    nc, _ = build()
    res = run_bass_kernel_spmd(nc, make_in_maps(inputs), core_ids=list(range(NCORE)))
    return np.stack([np.asarray(r["out"], dtype=np.float32) for r in res.results], axis=0)
```

```python
import contextlib
import numpy as np
import concourse.bass as bass
import concourse.mybir as mybir
from concourse.bass_utils import run_bass_kernel_spmd

DT = mybir.dt
F32 = DT.float32
BF16 = DT.bfloat16
I32 = DT.int32
U32 = DT.uint32
ALU = mybir.AluOpType
AF = mybir.ActivationFunctionType
AX = mybir.AxisListType

D = 2048
SEQ = 2048
CTXL = 256
TT = SEQ + CTXL
DEPTH = 2
NCORE = 8
DC = 1024
NH = 8
DK = 128
CH = 64
NCHUNK = TT // CH
INW = 12288
ALPHA = (2.0 * DEPTH) ** 0.25
EPS = 1e-6
NEXP = 16384
FM_CB, FM_CC, FM_CV, FM_Q, FM_ZF, FM_ZB, FM_OG, FM_GA, FM_GB = 0, 1024, 2048, 3072, 4096, 5120, 6144, 7168, 9216
FM_ROWS = 11264

ENGS = ("tensor", "vector", "scalar", "gpsimd", "sync")
SAME_ENG_GAP = 2


class Prog:
    def __init__(self, nc, n_dma_slots=12):
        self.nc = nc
        self.stack = contextlib.ExitStack()
        self.q = {e: [] for e in ENGS}
        self.cnt = {e: 0 for e in ENGS}
        self.pending = {e: False for e in ENGS}
        self.seen = {e: {} for e in ENGS}
        self.last_w = {}
        self.readers = {}
        self.sems = {}
        for e in ENGS:
            if e != "sync":
                self.sems[e] = self.stack.enter_context(nc.semaphore("s_" + e))
        self.dma_slots = {}
        self.dma_rr = {}
        for qn in ("sync", "gpsimd", "scalar"):
            self.dma_slots[qn] = []
            self.dma_rr[qn] = 0
            for i in range(n_dma_slots):
                key = ("dma", qn, i)
                self.sems[key] = self.stack.enter_context(nc.semaphore(f"d_{qn}_{i}"))
                self.dma_slots[qn].append([key, 0])
        self.n_inst = 0
        self.stage_stack = None

    def begin_stage(self):
        self.stage_id = getattr(self, "stage_id", 0) + 1
        self.stage_stack = contextlib.ExitStack()
        return self.stage_stack

    def sb(self, name, shape, dtype):
        return self.stage_stack.enter_context(self.nc.sbuf_tensor(f"{name}_s{self.stage_id}", list(shape), dtype))

    def ps(self, name, shape, dtype=F32):
        return self.stage_stack.enter_context(self.nc.psum_tensor(f"{name}_s{self.stage_id}", list(shape), dtype))

    def _need(self, eng, tok, waits):
        if tok is None:
            return
        key, val = tok
        if key == eng and eng == "tensor":
            return
        if key == eng and self.cnt[eng] - val >= SAME_ENG_GAP:
            return
        if self.seen[eng].get(key, 0) >= val:
            return
        self.seen[eng][key] = val
        waits.append((key, val))

    def _deps(self, eng, reads, writes):
        waits = []
        for r in reads:
            self._need(eng, self.last_w.get(r), waits)
        for w in writes:
            self._need(eng, self.last_w.get(w), waits)
            for t in self.readers.get(w, ()):
                self._need(eng, t, waits)
        return waits

    def _commit(self, tok, reads, writes):
        for r in reads:
            self.readers.setdefault(r, []).append(tok)
        for w in writes:
            self.last_w[w] = tok
            self.readers[w] = []

    def op(self, eng, fn, reads=(), writes=(), inc=True):
        waits = self._deps(eng, reads, writes)
        tok = (eng, self.cnt[eng] + 1)
        if inc:
            self.cnt[eng] += 1
            self.pending[eng] = False
        else:
            self.pending[eng] = True
        self.q[eng].append((waits, fn, eng if inc else None, 1))
        self._commit(tok, reads, writes)
        self.n_inst += 1
        return tok

    def dma(self, qn, fn, reads=(), writes=()):
        slots = self.dma_slots[qn]
        i = self.dma_rr[qn]
        self.dma_rr[qn] = (i + 1) % len(slots)
        slot = slots[i]
        waits = self._deps(qn, reads, writes)
        if slot[1] > 0:
            self._need(qn, (slot[0], slot[1]), waits)
        slot[1] += 16
        tok = (slot[0], slot[1])
        self.q[qn].append((waits, fn, slot[0], 16))
        self._commit(tok, reads, writes)
        self.n_inst += 1
        return tok

    def barrier(self):
        for e in ENGS:
            waits = []
            for x in ENGS:
                if x != "sync" and x != e and self.cnt[x] > 0:
                    self._need(e, (x, self.cnt[x]), waits)
            if e != "sync" and e != "tensor" and self.cnt[e] > 0:
                self._need(e, (e, self.cnt[e]), waits)
            for qn in self.dma_slots:
                for key, val in self.dma_slots[qn]:
                    if val > 0:
                        self._need(e, (key, val), waits)
            self.q[e].append((waits, None, None, 0))

    def end_stage(self):
        self.barrier()
        for e in ENGS:
            assert not self.pending[e], f"engine {e} ends stage with un-inc'ed instruction"
        nc = self.nc
        q = self.q
        sems = self.sems
        with nc.Block() as block:
            def run(ename, engine):
                for waits, fn, inc_key, amt in q[ename]:
                    for key, val in waits:
                        engine.wait_ge(sems[key], val)
                    if fn is None:
                        continue
                    ins = fn(engine)
                    if inc_key is not None:
                        ins.then_inc(sems[inc_key], amt)

            @block.tensor
            def _(e):
                run("tensor", e)

            @block.vector
            def _(e):
                run("vector", e)

            @block.scalar
            def _(e):
                run("scalar", e)

            @block.gpsimd
            def _(e):
                run("gpsimd", e)

            @block.sync
            def _(e):
                run("sync", e)
        self.q = {e: [] for e in ENGS}
        self.last_w = {}
        self.readers = {}
        self.stage_stack.close()
        self.stage_stack = None

    def close(self):
        self.stack.close()


def _layer_stats(P, pref, xt, xkey, bst, mv, rs):
    for j in range(4):
        P.op("vector", lambda e, j=j: e.bn_stats(out=bst[:, j * 6:(j + 1) * 6], in_=xt[:, j * 512:(j + 1) * 512]),
             reads=[xkey], writes=[pref + "bst"])
    P.op("vector", lambda e: e.bn_aggr(out=mv[:], in_=bst[:]), reads=[pref + "bst"], writes=[pref + "mv"])
    P.op("vector", lambda e: e.tensor_scalar(out=rs[:], in0=mv[:, 1:2], scalar1=EPS, scalar2=None, op0=ALU.add),
         reads=[pref + "mv"], writes=[pref + "rs"])
    P.op("scalar", lambda e: e.activation(out=rs[:], in_=rs[:], func=AF.Ln), reads=[pref + "rs"], writes=[pref + "rs"])
    P.op("scalar", lambda e: e.activation(out=rs[:], in_=rs[:], func=AF.Exp, scale=-0.5),
         reads=[pref + "rs"], writes=[pref + "rs"])


def _make_ident(P, ident_f, ident_b):
    P.op("gpsimd", lambda e: e.memset(ident_f[:], 1.0), writes=["ident_f"])
    P.op("gpsimd", lambda e: e.affine_select(out=ident_f[:], in_=ident_f[:], pattern=[[-1, 128]],
                                             compare_op=ALU.is_equal, fill=0.0, base=0, channel_multiplier=1),
         reads=["ident_f"], writes=["ident_f"])
    if ident_b is not None:
        P.op("vector", lambda e: e.tensor_copy(out=ident_b[:], in_=ident_f[:]), reads=["ident_f"], writes=["ident_b"])


def stage_mod(P, l, T):
    P.begin_stage()
    craw = P.sb("m_craw", [128, 2, 16], F32)
    sig = P.sb("m_sig", [128, 2, 16], F32)
    c2 = P.sb("m_c2", [128, 16, 2], F32)
    bm = P.sb("m_bm", [2, INW], F32)
    res = P.sb("m_res", [2, INW], F32)
    wt = [P.sb(f"m_wt{i}", [128, 16, 512], F32) for i in range(2)]
    pss = [P.ps(f"m_ps{i}", [2, 512]) for i in range(2)]
    P.dma("sync", lambda e: e.dma_start(out=craw[:, 0, :], in_=T["c"].rearrange("(p k) -> p k", k=16)), writes=["craw"])
    P.dma("sync", lambda e: e.dma_start(out=craw[:, 1, :], in_=T["c_ctx"].rearrange("(p k) -> p k", k=16)), writes=["craw"])
    P.dma("sync", lambda e: e.dma_start(out=bm[:], in_=T["b_mod"][l:l + 1, :].partition_broadcast(2)), writes=["bm"])
    P.op("scalar", lambda e: e.activation(out=sig[:], in_=craw[:], func=AF.Sigmoid), reads=["craw"], writes=["sig"])
    P.op("vector", lambda e: e.tensor_tensor(out=c2[:].rearrange("p k j -> p j k"), in0=craw[:], in1=sig[:], op=ALU.mult),
         reads=["craw", "sig"], writes=["c2"])
    wsrc = T["w_mod"][l].rearrange("(p k) n -> p k n", k=16)
    for n in range(24):
        w = wt[n % 2]
        wk = f"wt{n % 2}"
        P.dma("sync", lambda e, w=w, n=n: e.dma_start(out=w[:], in_=wsrc[:, :, n * 512:(n + 1) * 512]), writes=[wk])
        ps = pss[n % 2]
        pk = f"mps{n % 2}"
        for k in range(16):
            P.op("tensor", lambda e, ps=ps, w=w, k=k: e.matmul(ps[:], lhsT=c2[:, k, :], rhs=w[:, k, :],
                                                              start=(k == 0), stop=(k == 15)),
                 reads=["c2", wk], writes=[pk], inc=(k == 15))
        P.op("vector", lambda e, ps=ps, n=n: e.tensor_tensor(out=res[:, n * 512:(n + 1) * 512], in0=ps[:],
                                                            in1=bm[:, n * 512:(n + 1) * 512], op=ALU.add),
             reads=[pk, "bm"], writes=["res"])
    P.dma("sync", lambda e: e.dma_start(out=T["modv"][l], in_=res[:]), reads=["res"], writes=["modv"])
    P.end_stage()


def _tok_src(T, l, i):
    if l == 0:
        if i < 2:
            return T["ctx"][i * 128:(i + 1) * 128, :]
        return T["x"][(i - 2) * 128:(i - 1) * 128, :]
    return T["xcur"][i * 128:(i + 1) * 128, :]


def stage_A(P, l, T):
    P.begin_stage()
    NT = TT // 128
    hT = P.sb("a_hT", [128, 16, TT], BF16)
    ident_f = P.sb("a_idf", [128, 128], F32)
    ident_b = P.sb("a_idb", [128, 128], BF16)
    xbuf = [P.sb(f"a_x{i}", [128, D], F32) for i in range(2)]
    hb = P.sb("a_hb", [128, D], BF16)
    bc = {k: P.sb("a_bc_" + k, [128, D], F32) for k in ("s_c", "sh_c", "s_l", "sh_l")}
    bst = P.sb("a_bst", [128, 24], F32)
    mv = P.sb("a_mv", [128, 2], F32)
    rs = P.sb("a_rs", [128, 1], F32)
    ptr = P.ps("a_ptr", [128, D], BF16)
    _make_ident(P, ident_f, ident_b)
    modv = T["modv"][l]
    for nm, row, col in (("sh_c", 1, 0), ("s_c", 1, 1), ("sh_l", 0, 0), ("s_l", 0, 1)):
        P.dma("sync", lambda e, nm=nm, row=row, col=col: e.dma_start(
            out=bc[nm][:], in_=modv[row:row + 1, col * D:(col + 1) * D].partition_broadcast(128)),
            reads=["modv"], writes=["bc" + nm])
    for nm in ("s_c", "s_l"):
        P.op("gpsimd", lambda e, nm=nm: e.tensor_scalar(out=bc[nm][:], in0=bc[nm][:], scalar1=1.0, scalar2=None, op0=ALU.add),
             reads=["bc" + nm], writes=["bc" + nm])
    for i in range(NT):
        xt = xbuf[i % 2]
        xk = f"ax{i % 2}"
        P.dma("sync", lambda e, xt=xt, i=i: e.dma_start(out=xt[:], in_=_tok_src(T, l, i)), reads=["xcur"], writes=[xk])
        _layer_stats(P, "a_", xt, xk, bst, mv, rs)
        sfx = "c" if i < 2 else "l"
        P.op("vector", lambda e, xt=xt: e.tensor_scalar(out=xt[:], in0=xt[:], scalar1=mv[:, 0:1], scalar2=rs[:, 0:1],
                                                       op0=ALU.subtract, op1=ALU.mult),
             reads=[xk, "a_mv", "a_rs"], writes=[xk])
        P.op("vector", lambda e, xt=xt, sfx=sfx: e.tensor_tensor(out=xt[:], in0=xt[:], in1=bc["s_" + sfx][:], op=ALU.mult),
             reads=[xk, "bcs_" + sfx], writes=[xk])
        P.op("vector", lambda e, xt=xt, sfx=sfx: e.tensor_tensor(out=hb[:], in0=xt[:], in1=bc["sh_" + sfx][:], op=ALU.add),
             reads=[xk, "bcsh_" + sfx], writes=["hb"])
        for k in range(16):
            P.op("tensor", lambda e, k=k: e.transpose(ptr[:, k * 128:(k + 1) * 128], hb[:, k * 128:(k + 1) * 128], ident_b[:]),
                 reads=["hb", "ident_b"], writes=["ptr"], inc=(k == 15))
        P.op("scalar", lambda e, i=i: e.copy(out=hT[:, :, i * 128:(i + 1) * 128],
                                             in_=ptr[:].rearrange("p (k t) -> p k t", t=128)),
             reads=["ptr"], writes=["hT"])
    if T.get("dbg_h") is not None:
        P.dma("sync", lambda e: e.dma_start(out=T["dbg_h"].rearrange("(k p) t -> p k t", p=128), in_=hT[:]),
              reads=["hT"], writes=["dbg_h"])
    CW = 256
    wst = [P.sb(f"a_wst{i}", [128, 16, CW], F32) for i in range(2)]
    wbf = [P.sb(f"a_wbf{i}", [128, 16, CW], BF16) for i in range(2)]
    stg = [P.sb(f"a_stg{i}", [128, TT], F32) for i in range(2)]
    tmp = [P.sb(f"a_tmp{i}", [128, 512], F32) for i in range(2)]
    vst = [P.sb(f"a_vst{i}", [128, CW], BF16) for i in range(2)]
    pss = [P.ps(f"a_ps{i}", [128, 512]) for i in range(4)]
    wsrc = T["w_in"][l].rearrange("(k p) n -> p k n", p=128)
    tchunks = [(0, 256)] + [(256 + 512 * j, 512) for j in range(4)]
    fam_of_col = [("cb", FM_CB), ("cc", FM_CC), ("cv", FM_CV), ("q", FM_Q), ("zf", FM_ZF), ("zb", FM_ZB),
                  ("v", None), ("og", FM_OG), ("ga", FM_GA), ("ga", FM_GA + 1024), ("gb", FM_GB), ("gb", FM_GB + 1024)]
    pi = 0
    si = 0
    ti = 0
    for n in range(INW // CW):
        ws = wst[n % 2]
        wb = wbf[n % 2]
        wsk, wbk = f"wst{n % 2}", f"wbf{n % 2}"
        P.dma("sync", lambda e, ws=ws, n=n: e.dma_start(out=ws[:], in_=wsrc[:, :, n * CW:(n + 1) * CW]), writes=[wsk])
        P.op("gpsimd", lambda e, ws=ws, wb=wb: e.tensor_copy(out=wb[:], in_=ws[:]), reads=[wsk], writes=[wbk])
        fam, row0 = fam_of_col[(n * CW) // 1024]
        coff = (n * CW) % 1024
        if fam == "v":
            for i in range(NT):
                ps = pss[pi % 4]
                pk = f"aps{pi % 4}"
                pi += 1
                for k in range(16):
                    P.op("tensor", lambda e, ps=ps, wb=wb, k=k, i=i: e.matmul(
                        ps[:, 0:CW], lhsT=hT[:, k, i * 128:(i + 1) * 128], rhs=wb[:, k, :], start=(k == 0), stop=(k == 15)),
                        reads=["hT", wbk], writes=[pk], inc=(k == 15))
                vs = vst[i % 2]
                vk = f"vst{i % 2}"
                P.op("scalar", lambda e, ps=ps, vs=vs: e.copy(out=vs[:], in_=ps[:, 0:CW]), reads=[pk], writes=[vk])
                P.dma("scalar", lambda e, vs=vs, i=i, coff=coff: e.dma_start(
                    out=T["VT"][i * 128:(i + 1) * 128, coff:coff + CW], in_=vs[:]), reads=[vk], writes=["VT"])
            continue
        for sub in range(CW // 128):
            sg = stg[si % 2]
            sk = f"stg{si % 2}"
            si += 1
            for (t0, tn) in tchunks:
                ps = pss[pi % 4]
                pk = f"aps{pi % 4}"
                pi += 1
                for k in range(16):
                    P.op("tensor", lambda e, ps=ps, wb=wb, k=k, sub=sub, t0=t0, tn=tn: e.matmul(
                        ps[:, 0:tn], lhsT=wb[:, k, sub * 128:(sub + 1) * 128], rhs=hT[:, k, t0:t0 + tn],
                        start=(k == 0), stop=(k == 15)),
                        reads=["hT", wbk], writes=[pk], inc=(k == 15))
                dst = sg[:, t0:t0 + tn]
                if fam in ("cb", "cc", "cv"):
                    P.op("scalar", lambda e, ps=ps, dst=dst, tn=tn: e.copy(out=dst, in_=ps[:, 0:tn]), reads=[pk], writes=[sk])
                elif fam in ("zf", "zb", "ga", "gb"):
                    P.op("scalar", lambda e, ps=ps, dst=dst, tn=tn: e.activation(out=dst, in_=ps[:, 0:tn], func=AF.Sigmoid),
                         reads=[pk], writes=[sk])
                else:
                    tm = tmp[ti % 2]
                    tk = f"atmp{ti % 2}"
                    ti += 1
                    P.op("scalar", lambda e, ps=ps, tm=tm, tn=tn: e.activation(out=tm[:, 0:tn], in_=ps[:, 0:tn], func=AF.Sigmoid),
                         reads=[pk], writes=[tk])
                    scl = DK ** -0.5 if fam == "q" else 1.0
                    P.op("vector", lambda e, ps=ps, tm=tm, dst=dst, tn=tn, scl=scl: e.scalar_tensor_tensor(
                        out=dst, in0=ps[:, 0:tn], scalar=scl, in1=tm[:, 0:tn], op0=ALU.mult, op1=ALU.mult),
                        reads=[pk, tk], writes=[sk])
            r0 = row0 + coff + sub * 128
            P.dma("scalar", lambda e, sg=sg, r0=r0: e.dma_start(out=T["FM"][r0:r0 + 128, :], in_=sg[:]),
                  reads=[sk], writes=["FM"])
    P.end_stage()


def stage_B(P, l, T):
    P.begin_stage()
    W3 = lambda ap: ap.rearrange("p (c s) -> p c s", s=CH)
    ident_f = P.sb("b_idf", [128, 128], F32)
    ident_b = P.sb("b_idb", [128, 128], BF16)
    ones_f = P.sb("b_ones", [128, 128], F32)
    maskc = P.sb("b_maskc", [128, TT], F32)
    msk = {"f": P.sb("b_mskf", [64, 64], F32), "b": P.sb("b_mskb", [64, 64], F32)}
    lbr = P.sb("b_lbr", [128, 2, 2, NH], F32)
    lb = P.sb("b_lb", [128, 2, NH], F32)
    oml = P.sb("b_oml", [128, 2, NH], F32)
    noml = P.sb("b_noml", [128, 2, NH], F32)
    gn = P.sb("b_gn", [128, 1], F32)
    qs = P.sb("b_qs", [128, TT], F32)
    sgt = {"f": P.sb("b_sgf", [128, TT], F32), "b": P.sb("b_sgb", [128, TT], F32)}
    ogs = P.sb("b_ogs", [128, TT], F32)
    Vc = P.sb("b_Vc", [64, NCHUNK, 128], BF16)
    oT = P.sb("b_oT", [128, TT], F32)
    wkbig = P.sb("b_wkbig", [128, 6, TT], F32)
    wk = [wkbig[:, i, :] for i in range(6)]
    tot = P.sb("b_tot", [128, NCHUNK], F32)
    ebl = {d: P.sb("b_ebl" + d, [128, NCHUNK], F32) for d in "fb"}
    qt = {d: P.sb("b_qt" + d, [128, TT], BF16) for d in "fb"}
    kt = {d: P.sb("b_kt" + d, [128, TT], BF16) for d in "fb"}
    kf = {d: P.sb("b_kf" + d, [128, TT], BF16) for d in "fb"}
    qb = {d: P.sb("b_qb" + d, [128, TT], BF16) for d in "fb"}
    kd = {d: P.sb("b_kd" + d, [128, TT], BF16) for d in "fb"}
    kdT = {d: P.sb("b_kdT" + d, [64, NCHUNK, 128], BF16) for d in "fb"}
    S = {d: [P.sb(f"b_S{d}{i}", [128, 128], F32) for i in range(2)] for d in "fb"}
    scm_all = {d: P.sb("b_scm" + d, [64, NCHUNK, CH], BF16) for d in "fb"}
    upd_all = {"f": wkbig[:, 0:2, :].rearrange("p a t -> p (a t)").rearrange("p (c k) -> p c k", k=128),
               "b": wkbig[:, 2:4, :].rearrange("p a t -> p (a t)").rearrange("p (c k) -> p c k", k=128)}
    upd_keys = {"f": ["wk0", "wk1"], "b": ["wk2", "wk3"]}
    Sb_all = {"f": wkbig[:, 4, :].bitcast(BF16).rearrange("p (c k) -> p c k", k=128),
              "b": wkbig[:, 5, :].bitcast(BF16).rearrange("p (c k) -> p c k", k=128)}
    Sb_keys = {"f": "wk4", "b": "wk5"}
    rsb = P.sb("b_rsb", [128, 512], F32)
    ybt = P.sb("b_ybt", [128, TT], BF16)
    ps_scl = [P.ps(f"b_pssc{i}", [64, 512]) for i in range(2)]
    ps_o = [P.ps("b_pso0", [128, 512])]
    ps_upl = [P.ps(f"b_psup{i}", [128, 512]) for i in range(2)]
    ps_trl = [P.ps(f"b_pstr{i}", [64, 4 * 128], BF16) for i in range(2)]
    ps_ss = P.ps("b_psss", [128, 512])

    _make_ident(P, ident_f, ident_b)
    P.op("gpsimd", lambda e: e.memset(ones_f[:], 1.0), writes=["ones_f"])
    P.op("gpsimd", lambda e: e.memset(maskc[:], 1.0), writes=["maskc"])
    P.op("gpsimd", lambda e: e.memset(W3(maskc[:])[:, :, 0:1], 0.0), reads=["maskc"], writes=["maskc"])
    for d, pat, cm in (("f", 1, -1), ("b", -1, 1)):
        P.op("gpsimd", lambda e, d=d: e.memset(msk[d][:], 1.0), writes=["msk" + d])
        P.op("gpsimd", lambda e, d=d, pat=pat, cm=cm: e.affine_select(
            out=msk[d][:], in_=msk[d][:], pattern=[[pat, 64]], compare_op=ALU.is_ge, fill=0.0, base=0,
            channel_multiplier=cm), reads=["msk" + d], writes=["msk" + d])
    for j in range(2):
        for dd in range(2):
            P.dma("sync", lambda e, j=j, dd=dd: e.dma_start(
                out=lbr[:, j, dd, :], in_=T["lb_raw"][j, dd, :].rearrange("(h p) -> p h", p=128),
                allow_slow_non_contiguous=True), writes=["lbr"])
    P.dma("sync", lambda e: e.dma_start(out=gn[:], in_=T["hg_norm_g"][l, :].rearrange("(p o) -> p o", o=1)), writes=["gn"])
    if l == 0:
        P.op("vector", lambda e: e.memset(lb[:], 0.0), writes=["lb"])
    else:
        P.op("vector", lambda e: e.tensor_tensor(out=lb[:], in0=lbr[:, 0, :, :], in1=lbr[:, 1, :, :], op=ALU.subtract),
             reads=["lbr"], writes=["lb"])
        P.op("scalar", lambda e: e.activation(out=lb[:], in_=lb[:], func=AF.Sigmoid), reads=["lb"], writes=["lb"])
    P.op("vector", lambda e: e.tensor_scalar(out=oml[:], in0=lb[:], scalar1=-1.0, scalar2=1.0, op0=ALU.mult, op1=ALU.add),
         reads=["lb"], writes=["oml"])
    P.op("vector", lambda e: e.tensor_scalar(out=noml[:], in0=oml[:], scalar1=-1.0, scalar2=None, op0=ALU.mult),
         reads=["oml"], writes=["noml"])

    fwd_order = list(range(NCHUNK))
    bwd_order = [3, 2, 1, 0] + list(range(NCHUNK - 1, 3, -1))
    for hd in range(NH):
        r = hd * 128
        P.dma("sync", lambda e, r=r: e.dma_start(out=qs[:], in_=T["FM"][FM_Q + r:FM_Q + r + 128, :]), reads=["FM"], writes=["qs"])
        P.dma("sync", lambda e, r=r: e.dma_start(out=sgt["f"][:], in_=T["FM"][FM_ZF + r:FM_ZF + r + 128, :]), reads=["FM"], writes=["sgf"])
        P.dma("sync", lambda e, r=r: e.dma_start(out=sgt["b"][:], in_=T["FM"][FM_ZB + r:FM_ZB + r + 128, :]), reads=["FM"], writes=["sgb"])
        P.dma("sync", lambda e, r=r: e.dma_start(out=ogs[:], in_=T["FM"][FM_OG + r:FM_OG + r + 128, :]), reads=["FM"], writes=["ogs"])
        P.dma("sync", lambda e, r=r: e.dma_start(out=Vc[:], in_=T["VT"][:, r:r + 128].rearrange("(c s) v -> s c v", s=CH)),
              reads=["VT"], writes=["Vc"])
        for jj in range(32):
            cjob = hd * 32 + jj
            tb, c = cjob % 2, cjob // 2
            src_tab = T["peer_u"] if tb == 0 else T["peer_v"]
            P.dma("gpsimd", lambda e, src_tab=src_tab, tb=tb, c=c: e.dma_start(
                out=T["UVb"][l][c * 128:(c + 1) * 128, tb, :], in_=src_tab[l, c * 128:(c + 1) * 128, :]), writes=[("UVb", cjob)])
        for di, d in enumerate("fb"):
            sg = sgt[d]
            sgk = "sg" + d
            lf, bcs, cc, kk, e1, e2 = wk
            a_oml, a_lb, a_noml = oml[:, di, hd:hd + 1], lb[:, di, hd:hd + 1], noml[:, di, hd:hd + 1]
            P.op("vector", lambda e, sg=sg, a_oml=a_oml, a_lb=a_lb: e.tensor_scalar(
                out=lf[:], in0=sg[:], scalar1=a_oml, scalar2=a_lb, op0=ALU.mult, op1=ALU.add),
                reads=[sgk, "oml", "lb"], writes=["wk0"])
            P.op("scalar", lambda e: e.activation(out=lf[:], in_=lf[:], func=AF.Ln), reads=["wk0"], writes=["wk0"])
            P.op("vector", lambda e: e.tensor_tensor_scan(out=bcs[:], data0=maskc[:], data1=lf[:], initial=0.0,
                                                          op0=ALU.mult, op1=ALU.add),
                 reads=["wk0", "maskc"], writes=["wk1"])
            P.op("vector", lambda e: e.tensor_copy(out=tot[:], in_=W3(bcs[:])[:, :, CH - 1]), reads=["wk1"], writes=["tot"])
            totb = tot[:].unsqueeze(2).to_broadcast([128, NCHUNK, CH])
            if d == "f":
                c_ap, ck = bcs, "wk1"
            else:
                P.op("vector", lambda e: e.tensor_tensor(out=cc[:], in0=lf[:], in1=bcs[:], op=ALU.subtract),
                     reads=["wk0", "wk1"], writes=["wk2"])
                P.op("vector", lambda e, totb=totb: e.tensor_tensor(out=W3(cc[:]), in0=W3(cc[:]), in1=totb, op=ALU.add),
                     reads=["wk2", "tot"], writes=["wk2"])
                c_ap, ck = cc, "wk2"
            P.op("scalar", lambda e, d=d: e.activation(out=ebl[d][:], in_=tot[:], func=AF.Exp), reads=["tot"], writes=["ebl" + d])
            P.op("vector", lambda e, sg=sg, a_noml=a_noml, a_oml=a_oml: e.tensor_scalar(
                out=kk[:], in0=sg[:], scalar1=a_noml, scalar2=a_oml, op0=ALU.mult, op1=ALU.add),
                reads=[sgk, "oml", "noml"], writes=["wk3"])
            W4 = lambda ap: ap.rearrange("p (c h s) -> p c h s", h=2, s=CH // 2)
            h1, h2, i1, ib = (0, 1, 15, 31) if d == "f" else (1, 0, 47, 32)
            c4, c3 = W4(c_ap[:]), W3(c_ap[:])
            r1b = c3[:, :, i1:i1 + 1].to_broadcast([128, NCHUNK, CH // 2])
            rbb = c3[:, :, ib:ib + 1].to_broadcast([128, NCHUNK, CH // 2])
            rbb64 = c3[:, :, ib:ib + 1].to_broadcast([128, NCHUNK, CH])
            P.op("vector", lambda e, c4=c4, r1b=r1b, h1=h1: e.tensor_tensor(out=W4(e1[:])[:, :, h1, :], in0=c4[:, :, h1, :], in1=r1b, op=ALU.subtract),
                 reads=[ck], writes=["wk4"])
            P.op("vector", lambda e, c4=c4, rbb=rbb, h2=h2: e.tensor_tensor(out=W4(e1[:])[:, :, h2, :], in0=c4[:, :, h2, :], in1=rbb, op=ALU.subtract),
                 reads=[ck], writes=["wk4"])
            P.op("gpsimd", lambda e, h2=h2: e.memset(W4(e2[:])[:, :, h2, :], 0.0), writes=["wk5"])
            P.op("scalar", lambda e, h1=h1: e.activation(out=W4(e2[:])[:, :, h1, :], in_=W4(e1[:])[:, :, h1, :], func=AF.Exp, scale=-1.0),
                 reads=["wk4"], writes=["wk5"])
            P.op("vector", lambda e, d=d: e.tensor_tensor(out=kt[d][:], in0=kk[:], in1=e2[:], op=ALU.mult),
                 reads=["wk3", "wk5"], writes=["kt" + d])
            P.op("scalar", lambda e: e.activation(out=e1[:], in_=e1[:], func=AF.Exp), reads=["wk4"], writes=["wk4"])
            P.op("vector", lambda e, d=d: e.tensor_tensor(out=qt[d][:], in0=qs[:], in1=e1[:], op=ALU.mult),
                 reads=["qs", "wk4"], writes=["qt" + d])
            P.op("vector", lambda e, c3=c3, rbb64=rbb64: e.tensor_tensor(out=W3(e2[:]), in0=rbb64, in1=c3, op=ALU.subtract),
                 reads=[ck, "kt" + d], writes=["wk5"])
            P.op("scalar", lambda e: e.activation(out=e2[:], in_=e2[:], func=AF.Exp), reads=["wk5"], writes=["wk5"])
            P.op("vector", lambda e, d=d: e.tensor_tensor(out=kf[d][:], in0=kk[:], in1=e2[:], op=ALU.mult),
                 reads=["wk3", "wk5"], writes=["kf" + d])
            P.op("scalar", lambda e, c_ap=c_ap: e.activation(out=lf[:], in_=c_ap[:], func=AF.Exp), reads=[ck], writes=["wk0"])
            P.op("vector", lambda e, d=d: e.tensor_tensor(out=qb[d][:], in0=qs[:], in1=lf[:], op=ALU.mult),
                 reads=["qs", "wk0"], writes=["qb" + d])
            P.op("vector", lambda e, c_ap=c_ap, totb=totb: e.tensor_tensor(out=W3(e1[:]), in0=totb, in1=W3(c_ap[:]), op=ALU.subtract),
                 reads=[ck, "tot", "qt" + d], writes=["wk4"])
            P.op("scalar", lambda e: e.activation(out=e1[:], in_=e1[:], func=AF.Exp), reads=["wk4"], writes=["wk4"])
            P.op("vector", lambda e, d=d: e.tensor_tensor(out=kd[d][:], in0=kk[:], in1=e1[:], op=ALU.mult),
                 reads=["wk3", "wk4"], writes=["kd" + d])
            for g4 in range(NCHUNK // 4):
                ps_tr, ptk = ps_trl[g4 % 2], f"ps_tr{g4 % 2}"
                for j in range(4):
                    c = g4 * 4 + j
                    P.op("tensor", lambda e, d=d, c=c, j=j, ps_tr=ps_tr: e.transpose(ps_tr[:, j * 128:(j + 1) * 128],
                                                                       kd[d][:, c * CH:(c + 1) * CH], ident_b[:]),
                         reads=["kd" + d, "ident_b"], writes=[ptk], inc=(j == 3))
                P.op("scalar", lambda e, d=d, g4=g4, ps_tr=ps_tr: e.copy(out=kdT[d][:, g4 * 4:(g4 + 1) * 4, :],
                                                          in_=ps_tr[:].rearrange("s (j k) -> s j k", k=128)),
                     reads=[ptk], writes=["kdT" + d])
        order = {"f": fwd_order, "b": bwd_order}
        posn = {d: {c: k for k, c in enumerate(order[d])} for d in "fb"}
        groups8 = [(g0, min(8, NCHUNK - g0)) for g0 in range(0, NCHUNK, 8)]
        for d in "fb":
            for g4 in range(NCHUNK // 4):
                pu, puk = ps_upl[g4 % 2], f"ps_up{g4 % 2}"
                for j in range(4):
                    c = g4 * 4 + j
                    P.op("tensor", lambda e, d=d, c=c, j=j, pu=pu: e.matmul(pu[:, j * 128:(j + 1) * 128], lhsT=kdT[d][:, c, :], rhs=Vc[:, c, :],
                                                                   start=True, stop=True),
                         reads=["kdT" + d, "Vc"], writes=[puk], inc=(j == 3))
                P.op("scalar", lambda e, d=d, g4=g4, pu=pu: e.copy(out=upd_all[d][:, g4 * 4:(g4 + 1) * 4, :],
                                                          in_=pu[:].rearrange("p (j k) -> p j k", k=128)),
                     reads=[puk], writes=upd_keys[d])
        def p1_group(d, g0, gcnt, sci):
            psc, psk = ps_scl[sci % 2], f"ps_sc{sci % 2}"
            kA, kB = (kt[d], kf[d]) if d == "f" else (kf[d], kt[d])
            for j in range(gcnt):
                c = g0 + j
                cs = slice(c * CH, (c + 1) * CH)
                hA = slice(c * CH, c * CH + CH // 2)
                hB = slice(c * CH + CH // 2, (c + 1) * CH)
                P.op("tensor", lambda e, cs=cs, hA=hA, j=j: e.matmul(psc[:, j * CH:j * CH + CH // 2], lhsT=kA[:, cs], rhs=qt[d][:, hA], start=True, stop=True),
                     reads=["kt" + d, "kf" + d, "qt" + d], writes=[psk], inc=False)
                P.op("tensor", lambda e, cs=cs, hB=hB, j=j: e.matmul(psc[:, j * CH + CH // 2:(j + 1) * CH], lhsT=kB[:, cs], rhs=qt[d][:, hB], start=True, stop=True),
                     reads=["kt" + d, "kf" + d, "qt" + d], writes=[psk], inc=(j == gcnt - 1))
            P.op("vector", lambda e: e.tensor_tensor(
                out=scm_all[d][:, g0:g0 + gcnt, :], in0=psc[:, 0:gcnt * CH].rearrange("s (j t) -> s j t", t=CH),
                in1=msk[d][:].unsqueeze(1).to_broadcast([64, gcnt, CH]), op=ALU.mult),
                reads=[psk, "msk" + d], writes=["scm" + d])
        p1_jobs = [(d, g0, gcnt) for (g0, gcnt) in groups8 for d in "fb"]
        p1_done = 0
        for d in "fb":
            P.op("gpsimd", lambda e, d=d: e.memset(S[d][1][:], 0.0), writes=[("S", d, 1)])
            P.op("gpsimd", lambda e, d=d: e.memset(Sb_all[d][:, 0, :], 0.0), writes=[Sb_keys[d]])
        for k in range(NCHUNK - 1):
            if k % 3 == 0 and p1_done < len(p1_jobs):
                p1_group(*p1_jobs[p1_done], p1_done)
                p1_done += 1
            for d in "fb":
                c = order[d][k]
                So, Sn = S[d][(k + 1) % 2], S[d][k % 2]
                P.op("vector", lambda e, d=d, c=c, So=So, Sn=Sn: e.scalar_tensor_tensor(out=Sn[:], in0=So[:], scalar=ebl[d][:, c:c + 1],
                                                                                   in1=upd_all[d][:, c, :], op0=ALU.mult, op1=ALU.add),
                     reads=[("S", d, (k + 1) % 2), "ebl" + d] + upd_keys[d], writes=[("S", d, k % 2)])
                P.op("scalar", lambda e, d=d, k=k, Sn=Sn: e.copy(out=Sb_all[d][:, k + 1, :], in_=Sn[:]),
                     reads=[("S", d, k % 2)], writes=[Sb_keys[d]])
        while p1_done < len(p1_jobs):
            p1_group(*p1_jobs[p1_done], p1_done)
            p1_done += 1
        for gi_, (g0, gcnt) in enumerate(groups8):
            po, pk = ps_o[0], "ps_o0"
            for j in range(gcnt):
                c = g0 + j
                cs = slice(c * CH, (c + 1) * CH)
                dst = po[:, j * CH:(j + 1) * CH]
                P.op("tensor", lambda e, c=c, dst=dst: e.matmul(dst, lhsT=Vc[:, c, :], rhs=scm_all["f"][:, c, :], start=True, stop=False),
                     reads=["Vc", "scmf"], writes=[pk], inc=False)
                P.op("tensor", lambda e, c=c, dst=dst: e.matmul(dst, lhsT=Vc[:, c, :], rhs=scm_all["b"][:, c, :], start=False, stop=False),
                     reads=["Vc", "scmb"], writes=[pk], inc=False)
                P.op("tensor", lambda e, c=c, cs=cs, dst=dst: e.matmul(dst, lhsT=Sb_all["f"][:, posn["f"][c], :], rhs=qb["f"][:, cs], start=False, stop=False),
                     reads=[Sb_keys["f"], "qbf"], writes=[pk], inc=False)
                P.op("tensor", lambda e, c=c, cs=cs, dst=dst: e.matmul(dst, lhsT=Sb_all["b"][:, posn["b"][c], :], rhs=qb["b"][:, cs], start=False, stop=True),
                     reads=[Sb_keys["b"], "qbb"], writes=[pk], inc=(j == gcnt - 1))
            P.op("scalar", lambda e, po=po, g0=g0, gcnt=gcnt: e.copy(out=oT[:, g0 * CH:(g0 + gcnt) * CH], in_=po[:, 0:gcnt * CH]),
                 reads=[pk], writes=[("oT", g0)])
        okeys = [("oT", g0) for g0 in range(0, NCHUNK, 8)]
        osq = wk[0]
        P.op("scalar", lambda e: e.activation(out=osq[:], in_=oT[:], func=AF.Square), reads=okeys, writes=["wk0"])
        for (t0, tn) in [(0, 256)] + [(256 + 512 * j, 512) for j in range(4)]:
            P.op("tensor", lambda e, t0=t0, tn=tn: e.matmul(ps_ss[:, 0:tn], lhsT=ones_f[:], rhs=osq[:, t0:t0 + tn], start=True, stop=True),
                 reads=["ones_f", "wk0"], writes=["ps_ss"])
            P.op("vector", lambda e, tn=tn: e.tensor_scalar(out=rsb[:, 0:tn], in0=ps_ss[:, 0:tn], scalar1=1.0 / 128, scalar2=EPS,
                                                           op0=ALU.mult, op1=ALU.add), reads=["ps_ss"], writes=["rsb"])
            P.op("scalar", lambda e, tn=tn: e.activation(out=rsb[:, 0:tn], in_=rsb[:, 0:tn], func=AF.Ln), reads=["rsb"], writes=["rsb"])
            P.op("scalar", lambda e, tn=tn: e.activation(out=rsb[:, 0:tn], in_=rsb[:, 0:tn], func=AF.Exp, scale=-0.5),
                 reads=["rsb"], writes=["rsb"])
            P.op("vector", lambda e, t0=t0, tn=tn: e.tensor_tensor(out=rsb[:, 0:tn], in0=rsb[:, 0:tn], in1=oT[:, t0:t0 + tn], op=ALU.mult),
                 reads=["rsb"] + okeys, writes=["rsb"])
            P.op("vector", lambda e, t0=t0, tn=tn: e.scalar_tensor_tensor(out=ybt[:, t0:t0 + tn], in0=rsb[:, 0:tn], scalar=gn[:, 0:1],
                                                                        in1=ogs[:, t0:t0 + tn], op0=ALU.mult, op1=ALU.mult),
                 reads=["rsb", "gn", "ogs"], writes=["ybt"])
        P.dma("scalar", lambda e, r=r: e.dma_start(out=T["YB"][r:r + 128, :], in_=ybt[:]), reads=["ybt"], writes=["YB"])
        if T.get("dbg_o") is not None:
            P.dma("scalar", lambda e, r=r: e.dma_start(out=T["dbg_o"][r:r + 128, :], in_=oT[:]), reads=okeys, writes=["dbg_o"])
    P.end_stage()


def _load_cast_weight(P, pref, dst, src, nk, stage, nq=2, stage_keys=None):
    view = src.rearrange("(k p) n -> p k n", p=128)
    for j in range(nk // nq):
        st = stage[j % 2]
        sk = stage_keys[j % 2] if stage_keys else f"wstage{j % 2}"
        P.dma("sync", lambda e, st=st, j=j: e.dma_start(out=st[:], in_=view[:, j * nq:(j + 1) * nq, :]), writes=[sk])
        eng = "gpsimd" if j % 2 == 0 else "scalar"
        if eng == "gpsimd":
            P.op("gpsimd", lambda e, st=st, j=j: e.tensor_copy(out=dst[:, j * nq:(j + 1) * nq, :], in_=st[:]), reads=[sk], writes=[pref])
        else:
            P.op("scalar", lambda e, st=st, j=j: e.copy(out=dst[:, j * nq:(j + 1) * nq, :], in_=st[:]), reads=[sk], writes=[pref])


def stage_C(P, l, T):
    last = l == DEPTH - 1
    P.begin_stage()
    Wpa = P.sb("c_wpa", [128, 8, D], BF16)
    Wpb = P.sb("c_wpb", [128, 8, D], BF16)
    stage = [P.sb(f"c_stage{i}", [128, 2, D], F32) for i in range(2)]
    cw = P.sb("c_cw", [128, 3, 8], F32)
    cbias = P.sb("c_cb", [128, 8], F32)
    cin = {k: [P.sb(f"c_in_{k}{i}", [128, 512], F32) for i in range(2)] for k in ("cb", "cc", "cv")}
    u = P.sb("c_u", [128, 512], F32)
    y = P.sb("c_y", [128, 512], F32)
    yaTl = [P.sb(f"c_yaT{i}", [128, 8, 512], BF16) for i in range(2)]
    ybTl = [P.sb(f"c_ybT{i}", [128, 8, 512], BF16) for i in range(2)]
    sga = [P.sb(f"c_sga{i}", [128, 512], F32) for i in range(2)]
    sgb = [P.sb(f"c_sgb{i}", [128, 512], F32) for i in range(2)]
    m1 = [P.sb(f"c_m1{i}", [128, 512], F32) for i in range(2)]
    m2 = [P.sb(f"c_m2{i}", [128, 512], F32) for i in range(2)]
    mT = P.sb("c_mT", [128, 16, 512], BF16)
    ps1 = [P.ps(f"c_ps1{i}", [128, 512]) for i in range(4)]
    ps2 = [P.ps(f"c_ps2{i}", [128, 512]) for i in range(4)]
    _load_cast_weight(P, "Wpa", Wpa, T["w_pa"][l], 8, stage)
    _load_cast_weight(P, "Wpb", Wpb, T["w_pb"][l], 8, stage)
    for k in range(3):
        P.dma("sync", lambda e, k=k: e.dma_start(out=cw[:, k, :], in_=T["conv_w"][l, k, :].rearrange("(c p) -> p c", p=128),
                                                allow_slow_non_contiguous=True), writes=["cw"])
    P.dma("sync", lambda e: e.dma_start(out=cbias[:], in_=T["conv_b"][l, :].rearrange("(c p) -> p c", p=128),
                                        allow_slow_non_contiguous=True), writes=["cbias"])
    tchunks = [(0, 256, 256)] + [(256 + 512 * j, 512, 64) for j in range(4)]
    if last:
        tchunks = tchunks[1:]
    it = [0]
    def conv(ci):
        t0, tn, rowlen = tchunks[ci]
        yaT, kya = yaTl[ci % 2], f"yaT{ci % 2}"
        ybT, kyb = ybTl[ci % 2], f"ybT{ci % 2}"
        R3 = lambda ap, tn=tn, rowlen=rowlen: ap[:, 0:tn].rearrange("p (r s) -> p r s", s=rowlen)
        for ct in range(8):
            bufs = {}
            for k, base in (("cb", FM_CB), ("cc", FM_CC), ("cv", FM_CV)):
                b = cin[k][it[0] % 2]
                bufs[k] = (b, f"cin{k}{it[0] % 2}")
                P.dma("sync", lambda e, b=b, base=base, ct=ct, t0=t0, tn=tn: e.dma_start(
                    out=b[:, 0:tn], in_=T["FM"][base + ct * 128:base + (ct + 1) * 128, t0:t0 + tn]),
                    reads=["FM"], writes=[bufs[k][1]])
            it[0] += 1
            (bcb, kcb), (bcc, kcc), (bcv, kcv) = bufs["cb"], bufs["cc"], bufs["cv"]
            P.op("vector", lambda e, bcc=bcc, bcv=bcv, tn=tn: e.tensor_tensor(out=u[:, 0:tn], in0=bcc[:, 0:tn], in1=bcv[:, 0:tn], op=ALU.mult),
                 reads=[kcc, kcv], writes=["u"])
            P.op("vector", lambda e, ct=ct, tn=tn: e.tensor_scalar(out=y[:, 0:tn], in0=u[:, 0:tn], scalar1=cw[:, 1, ct:ct + 1],
                                                                 scalar2=cbias[:, ct:ct + 1], op0=ALU.mult, op1=ALU.add),
                 reads=["u", "cw", "cbias"], writes=["y"])
            P.op("vector", lambda e, ct=ct, R3=R3, rowlen=rowlen: e.scalar_tensor_tensor(
                out=R3(y)[:, :, 1:rowlen], in0=R3(u)[:, :, 0:rowlen - 1], scalar=cw[:, 0, ct:ct + 1], in1=R3(y)[:, :, 1:rowlen],
                op0=ALU.mult, op1=ALU.add), reads=["u", "cw", "y"], writes=["y"])
            P.op("vector", lambda e, ct=ct, R3=R3, rowlen=rowlen: e.scalar_tensor_tensor(
                out=R3(y)[:, :, 0:rowlen - 1], in0=R3(u)[:, :, 1:rowlen], scalar=cw[:, 2, ct:ct + 1], in1=R3(y)[:, :, 0:rowlen - 1],
                op0=ALU.mult, op1=ALU.add), reads=["u", "cw", "y"], writes=["y"])
            P.op("vector", lambda e, ct=ct, bcb=bcb, tn=tn: e.tensor_tensor(out=yaT[:, ct, 0:tn], in0=bcb[:, 0:tn], in1=y[:, 0:tn], op=ALU.mult),
                 reads=[kcb, "y"], writes=[kya])

    def mm(ci):
        t0, tn, rowlen = tchunks[ci]
        yaT, kya = yaTl[ci % 2], f"yaT{ci % 2}"
        ybT, kyb = ybTl[ci % 2], f"ybT{ci % 2}"
        P.dma("sync", lambda e, t0=t0, tn=tn: e.dma_start(out=ybT[:, :, 0:tn], in_=T["YB"][:, t0:t0 + tn].rearrange("(k p) t -> p k t", p=128)),
              reads=["YB"], writes=[kyb])
        for mc in range(16):
            j = mc % 2
            jp = mc % 4
            P.dma("sync", lambda e, mc=mc, j=j, t0=t0, tn=tn: e.dma_start(
                out=sga[j][:, 0:tn], in_=T["FM"][FM_GA + mc * 128:FM_GA + (mc + 1) * 128, t0:t0 + tn]), reads=["FM"], writes=[f"sga{j}"])
            P.dma("sync", lambda e, mc=mc, j=j, t0=t0, tn=tn: e.dma_start(
                out=sgb[j][:, 0:tn], in_=T["FM"][FM_GB + mc * 128:FM_GB + (mc + 1) * 128, t0:t0 + tn]), reads=["FM"], writes=[f"sgb{j}"])
            for k in range(8):
                P.op("tensor", lambda e, mc=mc, k=k, jp=jp, tn=tn: e.matmul(ps1[jp][:, 0:tn], lhsT=Wpa[:, k, mc * 128:(mc + 1) * 128],
                                                                       rhs=yaT[:, k, 0:tn], start=(k == 0), stop=(k == 7)),
                     reads=["Wpa", kya], writes=[f"cps1{jp}"], inc=(k == 7))
            for k in range(8):
                P.op("tensor", lambda e, mc=mc, k=k, jp=jp, tn=tn: e.matmul(ps2[jp][:, 0:tn], lhsT=Wpb[:, k, mc * 128:(mc + 1) * 128],
                                                                       rhs=ybT[:, k, 0:tn], start=(k == 0), stop=(k == 7)),
                     reads=["Wpb", kyb], writes=[f"cps2{jp}"], inc=(k == 7))
            P.op("vector", lambda e, j=j, jp=jp, tn=tn: e.tensor_tensor(out=m1[j][:, 0:tn], in0=ps1[jp][:, 0:tn], in1=sga[j][:, 0:tn], op=ALU.mult),
                 reads=[f"cps1{jp}", f"sga{j}"], writes=[f"m1{j}"])
            P.op("vector", lambda e, j=j, jp=jp, tn=tn: e.tensor_tensor(out=m2[j][:, 0:tn], in0=ps2[jp][:, 0:tn], in1=sgb[j][:, 0:tn], op=ALU.mult),
                 reads=[f"cps2{jp}", f"sgb{j}"], writes=[f"m2{j}"])
            P.op("gpsimd", lambda e, j=j, mc=mc, tn=tn: e.tensor_tensor(out=mT[:, mc, 0:tn], in0=m1[j][:, 0:tn], in1=m2[j][:, 0:tn], op=ALU.add),
                 reads=[f"m1{j}", f"m2{j}"], writes=["mT"])
        P.dma("scalar", lambda e, t0=t0, tn=tn: e.dma_start(out=T["MT"][:, t0:t0 + tn].rearrange("(k p) t -> p k t", p=128), in_=mT[:, :, 0:tn]),
              reads=["mT"], writes=["MT"])
        if T.get("dbg_ya") is not None:
            P.dma("scalar", lambda e, t0=t0, tn=tn: e.dma_start(out=T["dbg_ya"][:, t0:t0 + tn].rearrange("(k p) t -> p k t", p=128), in_=yaT[:, :, 0:tn]),
                  reads=[kya], writes=["dbg_ya"])

    conv(0)
    for ci in range(len(tchunks)):
        if ci + 1 < len(tchunks):
            conv(ci + 1)
        mm(ci)
    P.end_stage()


def _bcast_load(P, dst, key, src_row):
    P.dma("sync", lambda e: e.dma_start(out=dst[:], in_=src_row.partition_broadcast(128)), reads=["modv"], writes=[key])


def _residual_ln(P, pref, src_ps_or_sb, src_key, xt, xk, gb, gkey, lng, lnb, r, rkey, bst, mv, rs, out_ap, out_key, out_dram, aff_eng="gpsimd", res_eng="vector"):
    P.op("vector", lambda e: e.tensor_tensor(out=r[:], in0=src_ps_or_sb, in1=gb[:], op=ALU.mult), reads=[src_key, gkey], writes=[rkey])
    P.op(res_eng, lambda e: e.scalar_tensor_tensor(out=r[:], in0=xt[:], scalar=ALPHA, in1=r[:], op0=ALU.mult, op1=ALU.add),
         reads=[xk, rkey], writes=[rkey])
    _layer_stats(P, pref, r, rkey, bst, mv, rs)
    P.op("vector", lambda e: e.tensor_scalar(out=r[:], in0=r[:], scalar1=mv[:, 0:1], scalar2=rs[:, 0:1], op0=ALU.subtract, op1=ALU.mult),
         reads=[rkey, pref + "mv", pref + "rs"], writes=[rkey])
    P.op(aff_eng, lambda e: e.tensor_tensor(out=r[:], in0=r[:], in1=lng[:], op=ALU.mult), reads=[rkey, "lng"], writes=[rkey])
    P.op(aff_eng, lambda e: e.tensor_tensor(out=out_ap, in0=r[:], in1=lnb[:], op=ALU.add), reads=[rkey, "lnb"], writes=[out_key])
    P.dma("scalar", lambda e: e.dma_start(out=out_dram, in_=out_ap), reads=[out_key], writes=["xout"])


def stage_D(P, l, T):
    last = l == DEPTH - 1
    P.begin_stage()
    Wo = P.sb("d_wo", [128, 16, D], BF16)
    stage = [P.sb(f"d_stage{i}", [128, 2, D], F32) for i in range(2)]
    mTc = [P.sb(f"d_mT{i}", [128, 16, 512], BF16) for i in range(2)]
    xb = [P.sb(f"d_x{i}", [128, D], F32) for i in range(2)]
    rrb = [P.sb(f"d_r{i}", [128, D], F32) for i in range(2)]
    ob = [P.sb(f"d_o{i}", [128, D], F32) for i in range(2)]
    g1b = P.sb("d_g1", [128, D], F32)
    lng = P.sb("d_lng", [128, D], F32)
    lnb = P.sb("d_lnb", [128, D], F32)
    bst = P.sb("d_bst", [128, 24], F32)
    mv = P.sb("d_mv", [128, 2], F32)
    rs = P.sb("d_rs", [128, 1], F32)
    psy = [P.ps(f"d_psy{i}", [128, D]) for i in range(2)]
    _load_cast_weight(P, "Wo", Wo, T["w_o"][l], 16, stage)
    P.dma("sync", lambda e: e.dma_start(out=lng[:], in_=T["ln1_g"][l:l + 1, :].partition_broadcast(128)), writes=["lng"])
    P.dma("sync", lambda e: e.dma_start(out=lnb[:], in_=T["ln1_b"][l:l + 1, :].partition_broadcast(128)), writes=["lnb"])
    modv = T["modv"][l]
    tchunks = [(0, 256)] + [(256 + 512 * j, 512) for j in range(4)]
    if last:
        tchunks = tchunks[1:]
    ti = 0
    for ci, (t0, tn) in enumerate(tchunks):
        if t0 == 0:
            _bcast_load(P, g1b, "g1b", modv[1:2, 2 * D:3 * D])
        elif t0 == 256:
            _bcast_load(P, g1b, "g1b", modv[0:1, 2 * D:3 * D])
        mt = mTc[ci % 2]
        mk = f"dmT{ci % 2}"
        P.dma("sync", lambda e, mt=mt, t0=t0, tn=tn: e.dma_start(out=mt[:, :, 0:tn], in_=T["MT"][:, t0:t0 + tn].rearrange("(k p) t -> p k t", p=128)),
              reads=["MT"], writes=[mk])
        for tt in range(tn // 128):
            i = (t0 + tt * 128) // 128
            xt = xb[ti % 2]
            xk = f"dx{ti % 2}"
            ot = ob[ti % 2]
            ok = f"do{ti % 2}"
            py = psy[ti % 2]
            pk = f"dpsy{ti % 2}"
            ti += 1
            P.dma("sync", lambda e, xt=xt, i=i: e.dma_start(out=xt[:], in_=_tok_src(T, l, i)), reads=["xcur"], writes=[xk])
            for nq in range(4):
                for k in range(16):
                    P.op("tensor", lambda e, py=py, mt=mt, tt=tt, k=k, nq=nq: e.matmul(
                        py[:, nq * 512:(nq + 1) * 512], lhsT=mt[:, k, tt * 128:(tt + 1) * 128], rhs=Wo[:, k, nq * 512:(nq + 1) * 512],
                        start=(k == 0), stop=(k == 15)), reads=[mk, "Wo"], writes=[pk], inc=(k == 15 and nq == 3))
            rr, rrk = rrb[(ti - 1) % 2], f"d_r{(ti - 1) % 2}"
            _residual_ln(P, "d_", py[:], pk, xt, xk, g1b, "g1b", lng, lnb, rr, rrk, bst, mv, rs, ot[:], ok,
                         T["x1"][i * 128:(i + 1) * 128, :])
    P.end_stage()


def _top16_multi(P, jobs):
    for (src, sk, wkb, wkeys, vals, vkey, idxs, ikey) in jobs:
        P.op("vector", lambda e, vals=vals, src=src: e.max(out=vals[:, 0:8], in_=src), reads=[sk], writes=[vkey])
        yield
    for (src, sk, wkb, wkeys, vals, vkey, idxs, ikey) in jobs:
        P.op("vector", lambda e, vals=vals, src=src, idxs=idxs: e.max_index(out=idxs[:, 0:8], in_max=vals[:, 0:8], in_values=src),
             reads=[sk, vkey], writes=[ikey])
        yield
    for (src, sk, wkb, wkeys, vals, vkey, idxs, ikey) in jobs:
        P.op("vector", lambda e, vals=vals, src=src, wkb=wkb: e.match_replace(out=wkb, in_to_replace=vals[:, 0:8], in_values=src, imm_value=-1e30),
             reads=[sk, vkey], writes=wkeys)
        yield
    for (src, sk, wkb, wkeys, vals, vkey, idxs, ikey) in jobs:
        P.op("vector", lambda e, vals=vals, wkb=wkb: e.max(out=vals[:, 8:16], in_=wkb), reads=wkeys, writes=[vkey])
        yield
    for (src, sk, wkb, wkeys, vals, vkey, idxs, ikey) in jobs:
        P.op("vector", lambda e, vals=vals, wkb=wkb, idxs=idxs: e.max_index(out=idxs[:, 8:16], in_max=vals[:, 8:16], in_values=wkb),
             reads=wkeys + [vkey], writes=[ikey])
        yield


def stage_T(P, l, T):
    P.begin_stage()
    for cjob in range(256):
        tb, c = cjob % 2, cjob // 2
        src_tab = T["peer_u"] if tb == 0 else T["peer_v"]
        P.dma("gpsimd", lambda e, src_tab=src_tab, tb=tb, c=c: e.dma_start(
            out=T["UVb"][l][c * 128:(c + 1) * 128, tb, :], in_=src_tab[l, c * 128:(c + 1) * 128, :]), writes=[("UVb", cjob)])
    P.end_stage()


def stage_Q(P, l, T):
    last = l == DEPTH - 1
    P.begin_stage()
    NT = TT // 128
    Wq = P.sb("q_wq", [128, 16, D], BF16)
    keysT = P.sb("q_keysT", [128, 16, 128], BF16)
    ident_f = P.sb("q_idf", [128, 128], F32)
    ident_b = P.sb("q_idb", [128, 128], BF16)
    bc = {k: P.sb("q_bc_" + k, [128, D], F32) for k in ("sh", "s")}
    x1b = [P.sb(f"q_x1{i}", [128, D], F32) for i in range(2)]
    tmpA = P.sb("q_tmpA", [128, D], F32)
    xmbb = [P.sb(f"q_xmb{i}", [128, D], BF16) for i in range(2)]
    scb = [P.sb(f"q_sc{i}", [128, D], F32) for i in range(2)]
    xmT = P.sb("q_xmT", [128, 16, 128], BF16)
    qT = P.sb("q_qT", [128, 16, 128], BF16)
    bst = P.sb("q_bst", [128, 24], F32)
    mv = P.sb("q_mv", [128, 2], F32)
    rs = P.sb("q_rs", [128, 1], F32)
    ps_t = P.ps("q_pst", [128, D], BF16)
    ps_q = P.ps("q_psq", [128, D])
    ps_s = P.ps("q_pss", [128, D // 2])
    _make_ident(P, ident_f, ident_b)
    _load_cast_weight(P, "Wq", Wq, T["peer_wq"][l], 16, [x1b[0][:].rearrange("p (a n) -> p a n", a=1), x1b[1][:].rearrange("p (a n) -> p a n", a=1)],
                      nq=1, stage_keys=["x1b0", "x1b1"])
    kraw = scb[0][:].rearrange("p (b d) -> p b d", d=128)
    P.dma("sync", lambda e: e.dma_start(out=kraw, in_=T["peer_keys"][l].rearrange("h p k d -> k (h p) d")), writes=["sc0"])
    for blk in range(16):
        P.op("tensor", lambda e, blk=blk: e.transpose(ps_q[:, blk * 128:(blk + 1) * 128], kraw[:, blk, :], ident_f[:]),
             reads=["sc0", "ident_f"], writes=["ps_q"], inc=(blk == 15))
    P.op("vector", lambda e: e.tensor_copy(out=keysT[:], in_=ps_q[:].rearrange("p (b k) -> p b k", k=128)), reads=["ps_q"], writes=["keysT"])
    modv = T["modv"][l]
    tiles = list(range(2, NT)) if last else list(range(NT))
    def lnt(ti, i):
        par = ti % 2
        x1t, kx = x1b[par], f"x1b{par}"
        xmb, kxmb = xmbb[par], f"xmb{par}"
        xm, kxm = tmpA, "tmpA"
        if i in (0, 2):
            row = 1 if i == 0 else 0
            _bcast_load(P, bc["sh"], "bcsh", modv[row:row + 1, 3 * D:4 * D])
            _bcast_load(P, bc["s"], "bcs", modv[row:row + 1, 4 * D:5 * D])
            P.op("gpsimd", lambda e: e.tensor_scalar(out=bc["s"][:], in0=bc["s"][:], scalar1=1.0, scalar2=None, op0=ALU.add),
                 reads=["bcs"], writes=["bcs"])
        P.dma("sync", lambda e: e.dma_start(out=x1t[:], in_=T["x1"][i * 128:(i + 1) * 128, :]), reads=["x1"], writes=[kx])
        _layer_stats(P, "q_", x1t, kx, bst, mv, rs)
        P.op("vector", lambda e: e.tensor_scalar(out=xm[:], in0=x1t[:], scalar1=mv[:, 0:1], scalar2=rs[:, 0:1], op0=ALU.subtract, op1=ALU.mult),
             reads=[kx, "q_mv", "q_rs"], writes=[kxm])
        P.op("vector", lambda e: e.tensor_tensor(out=xm[:], in0=xm[:], in1=bc["s"][:], op=ALU.mult), reads=[kxm, "bcs"], writes=[kxm])
        P.op("vector", lambda e: e.tensor_tensor(out=xmb[:], in0=xm[:], in1=bc["sh"][:], op=ALU.add), reads=[kxm, "bcsh"], writes=[kxmb])
        P.dma("scalar", lambda e: e.dma_start(out=T["XMB"][i * 128:(i + 1) * 128, :], in_=xmb[:]), reads=[kxmb], writes=["XMB"])

    def tr(ti):
        xmb, kxmb = xmbb[ti % 2], f"xmb{ti % 2}"
        for k in range(16):
            P.op("tensor", lambda e, k=k: e.transpose(ps_t[:, k * 128:(k + 1) * 128], xmb[:, k * 128:(k + 1) * 128], ident_b[:]),
                 reads=[kxmb, "ident_b"], writes=["ps_t"], inc=(k == 15))
        P.op("scalar", lambda e: e.copy(out=xmT[:], in_=ps_t[:].rearrange("p (k t) -> p k t", t=128)), reads=["ps_t"], writes=["xmT"])

    lnt(0, tiles[0])
    tr(0)
    for ti, i in enumerate(tiles):
        sc, ksc = scb[ti % 2], f"sc{ti % 2}"
        if ti + 1 < len(tiles):
            lnt(ti + 1, tiles[ti + 1])
        for blk in range(16):
            for k in range(16):
                P.op("tensor", lambda e, blk=blk, k=k: e.matmul(ps_q[:, blk * 128:(blk + 1) * 128], lhsT=Wq[:, k, blk * 128:(blk + 1) * 128],
                                                              rhs=xmT[:, k, :], start=(k == 0), stop=(k == 15)),
                     reads=["Wq", "xmT"], writes=["ps_q"], inc=(k == 15 and blk == 15))
        P.op("scalar", lambda e: e.copy(out=qT[:], in_=ps_q[:].rearrange("p (b t) -> p b t", t=128)), reads=["ps_q"], writes=["qT"])
        if ti + 1 < len(tiles):
            tr(ti + 1)
        for half in range(2):
            for b8 in range(8):
                blk = half * 8 + b8
                P.op("tensor", lambda e, blk=blk, b8=b8: e.matmul(ps_s[:, b8 * 128:(b8 + 1) * 128], lhsT=qT[:, blk, :], rhs=keysT[:, blk, :],
                                                               start=True, stop=True), reads=["qT", "keysT"], writes=["ps_s"], inc=(b8 == 7))
            P.op("scalar", lambda e, sc=sc, half=half: e.copy(out=sc[:, half * 1024:(half + 1) * 1024], in_=ps_s[:]), reads=["ps_s"], writes=[ksc])
        P.dma("scalar", lambda e, i=i, sc=sc: e.dma_start(out=T["SC"][i * 128:(i + 1) * 128, :], in_=sc[:]), reads=[ksc], writes=["SC"])
    P.end_stage()


def stage_E(P, l, T):
    last = l == DEPTH - 1
    P.begin_stage()
    NT = TT // 128
    NG = 10
    ident_f = P.sb("f_idf", [128, 128], F32)
    ident_b = P.sb("f_idb", [128, 128], BF16)
    iota16 = P.sb("f_iota", [128, 16], F32)
    bc = {k: P.sb("f_bc_" + k, [128, D], F32) for k in ("g", "lng", "lnb")}
    x1b = [P.sb(f"f_x1{i}", [128, D], F32) for i in range(2)]
    rb = [P.sb(f"f_r{i}", [128, D], F32) for i in range(2)]
    xmbb = [P.sb(f"f_xmb{i}", [128, D], BF16) for i in range(2)]
    sc = P.sb("f_sc", [128, D], F32)
    wk = P.sb("f_wk", [128, D], F32)
    cand = P.sb("f_cand", [128, D], F32)
    sv = P.sb("f_sv", [128, 16, 16], F32)
    si = P.sb("f_si", [128, 16, 16], U32)
    sif = P.sb("f_sif", [128, 16, 16], F32)
    tv = P.sb("f_tv", [128, 8, 16], F32)
    pos = P.sb("f_pos", [128, 8, 16], U32)
    posu = P.sb("f_posu", [128, 8, 16], U32)
    af = P.sb("f_af", [128, 8, 16], F32)
    bf = P.sb("f_bf", [128, 8, 16], F32)
    sel0 = P.sb("f_sel0", [128, 8, 16], F32)
    sel1 = P.sb("f_sel1", [128, 8, 16], F32)
    idxf = P.sb("f_idxf", [128, 128], F32)
    zz = P.sb("f_zz", [128, 8], F32)
    idxb = [P.sb(f"f_idx{i}", [128, 128], I32) for i in range(2)]
    ggb = [P.sb(f"f_gg{i}", [128, 8, 16], F32) for i in range(2)]
    dotb = [P.sb(f"f_dot{i}", [128, 128], F32) for i in range(2)]
    gelb = [P.sb(f"f_gel{i}", [128, 128], F32) for i in range(2)]
    aab = [P.sb(f"f_aa{i}", [128, 128], F32) for i in range(2)]
    gbuf = [P.sb(f"f_g{i}", [128, 2 * D], BF16) for i in range(NG)]
    diag = [P.sb(f"f_dg{i}", [128, 128], BF16) for i in range(NG)]
    junk = P.sb("f_junk", [128, D], BF16)
    bst = P.sb("f_bst", [128, 24], F32)
    mv = P.sb("f_mv", [128, 2], F32)
    rs = P.sb("f_rs", [128, 1], F32)
    ps_f = [P.ps(f"f_psf{i}", [128, D]) for i in range(2)]
    _make_ident(P, ident_f, ident_b)
    P.op("gpsimd", lambda e: e.iota(iota16[:], pattern=[[1, 16]], base=0, channel_multiplier=0,
                                    allow_small_or_imprecise_dtypes=True), writes=["iota16"])
    P.dma("sync", lambda e: e.dma_start(out=bc["lng"][:], in_=T["ln2_g"][l:l + 1, :].partition_broadcast(128)), writes=["lng"])
    P.dma("sync", lambda e: e.dma_start(out=bc["lnb"][:], in_=T["ln2_b"][l:l + 1, :].partition_broadcast(128)), writes=["lnb"])
    modv = T["modv"][l]
    tiles = list(range(2, NT)) if last else list(range(NT))
    UVflat = T["UVb"][l].rearrange("e t d -> e (t d)")

    def route(ti, i):
        par = ti % 2
        gg, kgg = ggb[par], f"gg{par}"
        ii, kii = idxb[par], f"idx{par}"
        ksc = "sc"
        P.dma("sync", lambda e: e.dma_start(out=sc[:], in_=T["SC"][i * 128:(i + 1) * 128, :]), reads=["SC"], writes=[ksc])
        yield
        yield from _top16_multi(P, [(sc[:, blk * 128:(blk + 1) * 128], ksc, wk[:, blk * 128:(blk + 1) * 128], [("wk", blk)], sv[:, blk, :], ("sv", blk),
                                     si[:, blk, :], ("si", blk)) for blk in range(16)])
        svk = [("sv", b_) for b_ in range(16)]
        sik = [("si", b_) for b_ in range(16)]
        P.op("vector", lambda e: e.tensor_copy(out=sif[:], in_=si[:]), reads=sik, writes=["sif"])
        yield
        sv4 = sv[:].rearrange("p (h t) a -> p h t a", t=2)
        sif4 = sif[:].rearrange("p (h t) a -> p h t a", t=2)
        c4 = lambda ap: ap[:].rearrange("p (h a b) -> p h a b", a=16, b=16)
        bA = lambda v4: v4[:, :, 0, :].unsqueeze(3).to_broadcast([128, 8, 16, 16])
        bB = lambda v4: v4[:, :, 1, :].unsqueeze(2).to_broadcast([128, 8, 16, 16])
        P.op("vector", lambda e: e.tensor_tensor(out=c4(cand), in0=bA(sv4), in1=bB(sv4), op=ALU.add), reads=svk, writes=["cand"])
        yield
        yield from _top16_multi(P, [(cand[:, h * 256:(h + 1) * 256], "cand", wk[:, h * 256:(h + 1) * 256], [("wk", 2 * h), ("wk", 2 * h + 1)],
                                     tv[:, h, :], ("tv", h), pos[:, h, :], ("pos", h)) for h in range(8)])
        tvk = [("tv", h) for h in range(8)]
        posk = [("pos", h) for h in range(8)]
        P.op("vector", lambda e: e.tensor_tensor(out=gg[:], in0=tv[:], in1=tv[:, :, 0:1].to_broadcast([128, 8, 16]), op=ALU.subtract),
             reads=tvk, writes=[kgg])
        yield
        P.op("scalar", lambda e: e.activation(out=gg[:], in_=gg[:], func=AF.Exp), reads=[kgg], writes=[kgg])
        P.op("vector", lambda e: e.tensor_single_scalar(out=posu[:], in_=pos[:], scalar=4, op=ALU.logical_shift_right), reads=posk, writes=["posu"])
        yield
        P.op("vector", lambda e: e.tensor_copy(out=af[:], in_=posu[:]), reads=["posu"], writes=["af"])
        yield
        P.op("vector", lambda e: e.tensor_single_scalar(out=posu[:], in_=pos[:], scalar=15, op=ALU.bitwise_and), reads=posk + ["af"], writes=["posu"])
        yield
        P.op("vector", lambda e: e.tensor_copy(out=bf[:], in_=posu[:]), reads=["posu"], writes=["bf"])
        yield
        P.op("vector", lambda e: e.reduce_sum(out=zz[:], in_=gg[:], axis=AX.X), reads=[kgg], writes=["zz"])
        yield
        P.op("vector", lambda e: e.reciprocal(out=zz[:], in_=zz[:]), reads=["zz"], writes=["zz"])
        yield
        io4 = iota16[:].unsqueeze(1).unsqueeze(1).to_broadcast([128, 8, 16, 16])
        for (src, half, dst, dk_) in ((af, 0, sel0, "sel0"), (bf, 1, sel1, "sel1")):
            oh = c4(sc)
            P.op("vector", lambda e, src=src, oh=oh: e.tensor_tensor(out=oh, in0=io4, in1=src[:].unsqueeze(3).to_broadcast([128, 8, 16, 16]),
                                                                   op=ALU.is_equal), reads=["iota16", "af", "bf"], writes=[ksc])
            yield
            P.op("vector", lambda e, half=half, oh=oh: e.tensor_tensor(
                out=oh, in0=oh, in1=sif4[:, :, half, :].unsqueeze(2).to_broadcast([128, 8, 16, 16]), op=ALU.mult),
                reads=[ksc, "sif"], writes=[ksc])
            yield
            P.op("vector", lambda e, dst=dst, oh=oh: e.tensor_reduce(out=dst[:], in_=oh, axis=AX.X, op=ALU.add), reads=[ksc], writes=[dk_])
            yield
        P.op("vector", lambda e: e.tensor_tensor(out=gg[:], in0=gg[:], in1=zz[:].unsqueeze(2).to_broadcast([128, 8, 16]), op=ALU.mult),
             reads=[kgg, "zz"], writes=[kgg])
        yield
        P.op("vector", lambda e: e.scalar_tensor_tensor(out=idxf[:].rearrange("p (h k) -> p h k", k=16), in0=sel0[:], scalar=128.0, in1=sel1[:],
                                                       op0=ALU.mult, op1=ALU.add), reads=["sel0", "sel1"], writes=["idxf"])
        yield
        P.op("vector", lambda e: e.tensor_copy(out=ii[:], in_=idxf[:]), reads=["idxf"], writes=[kii])
        yield
        if T.get("dbg_route"):
            P.dma("scalar", lambda e: e.dma_start(out=T["IDX"][i * 128:(i + 1) * 128, :], in_=ii[:]), reads=[kii], writes=["IDX"])
            P.dma("scalar", lambda e: e.dma_start(out=T["GG"][i * 128:(i + 1) * 128, :], in_=gg[:].rearrange("p h k -> p (h k)")),
                  reads=[kgg], writes=["GG"])

    for _ in route(0, tiles[0]):
        pass
    gi = 0
    for ti, i in enumerate(tiles):
        if i in (0, 2):
            row = 1 if i == 0 else 0
            _bcast_load(P, bc["g"], "bcg", modv[row:row + 1, 5 * D:6 * D])
        par = ti % 2
        x1t, kx = x1b[par], f"x1b{par}"
        xmb, kxmb = xmbb[par], f"xmb{par}"
        idx, kidx = idxb[par], f"idx{par}"
        gg, kgg = ggb[par], f"gg{par}"
        ggf = gg[:].rearrange("p h k -> p (h k)")
        dot, gel, aa = dotb[par], gelb[par], aab[par]
        pf, kpf = ps_f[par], f"psf{par}"
        r, kr = rb[par], f"r{par}"
        P.dma("sync", lambda e, i=i, xmb=xmb: e.dma_start(out=xmb[:], in_=T["XMB"][i * 128:(i + 1) * 128, :]), reads=["XMB"], writes=[kxmb])
        P.dma("sync", lambda e, i=i, x1t=x1t: e.dma_start(out=x1t[:], in_=T["x1"][i * 128:(i + 1) * 128, :]), reads=["x1"], writes=[kx])
        gen = route(ti + 1, tiles[ti + 1]) if ti + 1 < len(tiles) else None
        for ee in range(128):
            j = gi % NG
            gi += 1
            gb_, gk = gbuf[j], ("gbuf", j)
            dg, dk_ = diag[j], ("diag", j)
            P.dma("gpsimd", lambda e, gb_=gb_, ee=ee, idx=idx: e.indirect_dma_start(
                out=gb_[:], out_offset=None, in_=UVflat,
                in_offset=bass.IndirectOffsetOnAxis(ap=idx[:, ee:ee + 1], axis=0)), reads=[kidx, "UVb"], writes=[gk])
            P.op("vector", lambda e, gb_=gb_, ee=ee, xmb=xmb, dot=dot: e.scalar_tensor_tensor(
                out=junk[:], in0=gb_[:, 0:D], scalar=1.0, in1=xmb[:], op0=ALU.mult, op1=ALU.mult, accum_out=dot[:, ee:ee + 1]),
                reads=[gk, kxmb], writes=["junk", ("dot", par, ee)])
            if gen is not None:
                for _ in range(2):
                    try:
                        next(gen)
                    except StopIteration:
                        gen = None
                        break
            P.op("scalar", lambda e, ee=ee, dot=dot, gel=gel: e.activation(out=gel[:, ee:ee + 1], in_=dot[:, ee:ee + 1], func=AF.Gelu),
                 reads=[("dot", par, ee)], writes=[("gel", par, ee)])
            P.op("scalar", lambda e, ee=ee, gel=gel, aa=aa, ggf=ggf: e.activation(out=aa[:, ee:ee + 1], in_=gel[:, ee:ee + 1], func=AF.Copy,
                                                                            scale=ggf[:, ee:ee + 1]),
                 reads=[("gel", par, ee), kgg], writes=[("aa", par, ee)])
            P.op("scalar", lambda e, ee=ee, aa=aa, dg=dg: e.activation(out=dg[:], in_=ident_b[:], func=AF.Copy, scale=aa[:, ee:ee + 1]),
                 reads=[("aa", par, ee), "ident_b"], writes=[dk_])
            for nq in range(4):
                P.op("tensor", lambda e, nq=nq, dg=dg, gb_=gb_, pf=pf, ee=ee: e.matmul(
                    pf[:, nq * 512:(nq + 1) * 512], lhsT=dg[:], rhs=gb_[:, D + nq * 512:D + (nq + 1) * 512],
                    start=(ee == 0), stop=(ee == 127)), reads=[dk_, gk], writes=[kpf], inc=(nq == 3))
        if gen is not None:
            for _ in gen:
                pass
        if T.get("dbg_f") is not None:
            P.op("vector", lambda e, pf=pf, r=r: e.tensor_copy(out=r[:], in_=pf[:]), reads=[kpf], writes=[kr])
            P.dma("scalar", lambda e, i=i, r=r: e.dma_start(out=T["dbg_f"][i * 128:(i + 1) * 128, :], in_=r[:]), reads=[kr], writes=["dbg_f"])
        if last:
            dst = T["out"][(i - 2) * 128:(i - 1) * 128, :]
        else:
            dst = T["xcur"][i * 128:(i + 1) * 128, :]
        _residual_ln(P, "f_", pf[:], kpf, x1t, kx, bc["g"], "bcg", bc["lng"], bc["lnb"], r, kr, bst, mv, rs, r[:], kr, dst, aff_eng="vector", res_eng="vector")
    P.end_stage()


def build(stop_after=None, dbg=(), only=None, ext_in=()):
    nc = bass.Bass("TRN2", target_bir_lowering=False)
    T = {}

    def inp(name, shape, dtype=F32):
        T[name] = nc.dram_tensor(name, list(shape), dtype, kind="ExternalInput").ap()

    def scr(name, shape, dtype=F32):
        kind = "ExternalOutput" if name in dbg else ("ExternalInput" if name in ext_in else "Internal")
        T[name] = nc.dram_tensor(name, list(shape), dtype, kind=kind).ap()

    inp("x", [SEQ, D]); inp("c", [D]); inp("ctx", [CTXL, D]); inp("c_ctx", [D])
    inp("w_mod", [DEPTH, D, 6 * D]); inp("b_mod", [DEPTH, 6 * D]); inp("w_in", [DEPTH, D, INW])
    inp("conv_w", [DEPTH, 3, DC]); inp("conv_b", [DEPTH, DC]); inp("lb_raw", [DEPTH, 2, DC])
    inp("hg_norm_g", [DEPTH, 128]); inp("w_pa", [DEPTH, DC, D]); inp("w_pb", [DEPTH, DC, D]); inp("w_o", [DEPTH, D, D])
    inp("ln1_g", [DEPTH, D]); inp("ln1_b", [DEPTH, D]); inp("peer_wq", [DEPTH, D, D])
    inp("peer_keys", [DEPTH, 8, 2, 128, 128]); inp("peer_u", [DEPTH, NEXP, D]); inp("peer_v", [DEPTH, NEXP, D])
    inp("ln2_g", [DEPTH, D]); inp("ln2_b", [DEPTH, D])
    T["out"] = nc.dram_tensor("out", [SEQ, D], F32, kind="ExternalOutput").ap()
    scr("modv", [DEPTH, 2, INW])
    scr("FM", [FM_ROWS, TT])
    scr("VT", [TT, DC], BF16)
    scr("xcur", [TT, D])
    scr("YB", [DC, TT], BF16)
    scr("MT", [D, TT], BF16)
    scr("IDX", [TT, 128], I32)
    scr("GG", [TT, 128])
    scr("XMB", [TT, D], BF16)
    scr("SC", [TT, D])
    T["dbg_route"] = ("IDX" in dbg)
    T["UVb"] = [nc.dram_tensor(f"UVb{l}", [NEXP, 2, D], BF16, kind="Internal").ap() for l in range(DEPTH)]
    scr("x1", [TT, D])
    for nm, shp, dt_ in (("dbg_ya", [DC, TT], BF16), ("dbg_idx", [TT, 128], I32), ("dbg_g", [TT, 128], F32), ("dbg_f", [TT, D], F32)):
        if nm in dbg:
            scr(nm, shp, dt_)
    if "dbg_h" in dbg:
        scr("dbg_h", [D, TT], BF16)
    if "dbg_o" in dbg:
        scr("dbg_o", [DC, TT])

    P = Prog(nc)
    stages = []
    for l in range(DEPTH):
        stages += [("M", l), ("A", l), ("B", l), ("C", l), ("D", l), ("Q", l), ("E", l)]
    if only is not None:
        stages = list(only)
    for (s, l) in stages:
        if only is not None and (s, l) not in only:
            continue
        if s == "M":
            stage_mod(P, l, T)
        elif s == "A":
            stage_A(P, l, T)
        elif s == "B":
            stage_B(P, l, T)
        elif s == "C":
            stage_C(P, l, T)
        elif s == "D":
            stage_D(P, l, T)
        elif s == "Q":
            stage_Q(P, l, T)
        elif s == "E":
            stage_E(P, l, T)
        elif s == "T":
            stage_T(P, l, T)
        if stop_after == (s, l):
            break
    P.close()
    return nc, P


def make_in_maps(inputs):
    shared = {k: np.ascontiguousarray(inputs[k]) for k in (
        "c_ctx", "w_mod", "b_mod", "w_in", "conv_w", "conv_b", "lb_raw", "hg_norm_g", "w_pa", "w_pb", "w_o",
        "ln1_g", "ln1_b", "peer_wq", "peer_keys", "peer_u", "peer_v", "ln2_g", "ln2_b")}
    maps = []
    for b in range(NCORE):
        m = dict(shared)
        m["x"] = np.ascontiguousarray(inputs["x"][b])
        m["c"] = np.ascontiguousarray(inputs["c"][b])
        m["ctx"] = np.ascontiguousarray(inputs["ctx"][b])
        maps.append(m)
    return maps


def kernel(**inputs):
    inputs = {k: np.asarray(v, dtype=np.float32) for k, v in inputs.items()}
    nc, _ = build()
    res = run_bass_kernel_spmd(nc, make_in_maps(inputs), core_ids=list(range(NCORE)))
    return np.stack([np.asarray(r["out"], dtype=np.float32) for r in res.results], axis=0)
```

```python
import contextlib
import numpy as np
import concourse.bass as bass
import concourse.mybir as mybir
from concourse.bass_utils import run_bass_kernel_spmd

DT = mybir.dt
F32 = DT.float32
BF16 = DT.bfloat16
I32 = DT.int32
U32 = DT.uint32
ALU = mybir.AluOpType
AF = mybir.ActivationFunctionType
AX = mybir.AxisListType

D = 2048
SEQ = 2048
CTXL = 256
TT = SEQ + CTXL
DEPTH = 2
NCORE = 8
DC = 1024
NH = 8
DK = 128
CH = 64
NCHUNK = TT // CH
INW = 12288
ALPHA = (2.0 * DEPTH) ** 0.25
EPS = 1e-6
NEXP = 16384
FM_CB, FM_CC, FM_CV, FM_Q, FM_ZF, FM_ZB, FM_OG, FM_GA, FM_GB = 0, 1024, 2048, 3072, 4096, 5120, 6144, 7168, 9216
FM_ROWS = 11264

ENGS = ("tensor", "vector", "scalar", "gpsimd", "sync")
SAME_ENG_GAP = 2


class Prog:
    def __init__(self, nc, n_dma_slots=12):
        self.nc = nc
        self.stack = contextlib.ExitStack()
        self.q = {e: [] for e in ENGS}
        self.cnt = {e: 0 for e in ENGS}
        self.pending = {e: False for e in ENGS}
        self.seen = {e: {} for e in ENGS}
        self.last_w = {}
        self.readers = {}
        self.sems = {}
        for e in ENGS:
            if e != "sync":
                self.sems[e] = self.stack.enter_context(nc.semaphore("s_" + e))
        self.dma_slots = {}
        self.dma_rr = {}
        for qn in ("sync", "gpsimd", "scalar"):
            self.dma_slots[qn] = []
            self.dma_rr[qn] = 0
            for i in range(n_dma_slots):
                key = ("dma", qn, i)
                self.sems[key] = self.stack.enter_context(nc.semaphore(f"d_{qn}_{i}"))
                self.dma_slots[qn].append([key, 0])
        self.n_inst = 0
        self.stage_stack = None

    def begin_stage(self):
        self.stage_id = getattr(self, "stage_id", 0) + 1
        self.stage_stack = contextlib.ExitStack()
        return self.stage_stack

    def sb(self, name, shape, dtype):
        return self.stage_stack.enter_context(self.nc.sbuf_tensor(f"{name}_s{self.stage_id}", list(shape), dtype))

    def ps(self, name, shape, dtype=F32):
        return self.stage_stack.enter_context(self.nc.psum_tensor(f"{name}_s{self.stage_id}", list(shape), dtype))

    def _need(self, eng, tok, waits):
        if tok is None:
            return
        key, val = tok
        if key == eng and eng == "tensor":
            return
        if key == eng and self.cnt[eng] - val >= SAME_ENG_GAP:
            return
        if self.seen[eng].get(key, 0) >= val:
            return
        self.seen[eng][key] = val
        waits.append((key, val))

    def _deps(self, eng, reads, writes):
        waits = []
        for r in reads:
            self._need(eng, self.last_w.get(r), waits)
        for w in writes:
            self._need(eng, self.last_w.get(w), waits)
            for t in self.readers.get(w, ()):
                self._need(eng, t, waits)
        return waits

    def _commit(self, tok, reads, writes):
        for r in reads:
            self.readers.setdefault(r, []).append(tok)
        for w in writes:
            self.last_w[w] = tok
            self.readers[w] = []

    def op(self, eng, fn, reads=(), writes=(), inc=True):
        waits = self._deps(eng, reads, writes)
        tok = (eng, self.cnt[eng] + 1)
        if inc:
            self.cnt[eng] += 1
            self.pending[eng] = False
        else:
            self.pending[eng] = True
        self.q[eng].append((waits, fn, eng if inc else None, 1))
        self._commit(tok, reads, writes)
        self.n_inst += 1
        return tok

    def dma(self, qn, fn, reads=(), writes=()):
        slots = self.dma_slots[qn]
        i = self.dma_rr[qn]
        self.dma_rr[qn] = (i + 1) % len(slots)
        slot = slots[i]
        waits = self._deps(qn, reads, writes)
        if slot[1] > 0:
            self._need(qn, (slot[0], slot[1]), waits)
        slot[1] += 16
        tok = (slot[0], slot[1])
        self.q[qn].append((waits, fn, slot[0], 16))
        self._commit(tok, reads, writes)
        self.n_inst += 1
        return tok

    def barrier(self):
        for e in ENGS:
            waits = []
            for x in ENGS:
                if x != "sync" and x != e and self.cnt[x] > 0:
                    self._need(e, (x, self.cnt[x]), waits)
            if e != "sync" and e != "tensor" and self.cnt[e] > 0:
                self._need(e, (e, self.cnt[e]), waits)
            for qn in self.dma_slots:
                for key, val in self.dma_slots[qn]:
                    if val > 0:
                        self._need(e, (key, val), waits)
            self.q[e].append((waits, None, None, 0))

    def end_stage(self):
        self.barrier()
        for e in ENGS:
            assert not self.pending[e], f"engine {e} ends stage with un-inc'ed instruction"
        nc = self.nc
        q = self.q
        sems = self.sems
        with nc.Block() as block:
            def run(ename, engine):
                for waits, fn, inc_key, amt in q[ename]:
                    for key, val in waits:
                        engine.wait_ge(sems[key], val)
                    if fn is None:
                        continue
                    ins = fn(engine)
                    if inc_key is not None:
                        ins.then_inc(sems[inc_key], amt)

            @block.tensor
            def _(e):
                run("tensor", e)

            @block.vector
            def _(e):
                run("vector", e)

            @block.scalar
            def _(e):
                run("scalar", e)

            @block.gpsimd
            def _(e):
                run("gpsimd", e)

            @block.sync
            def _(e):
                run("sync", e)
        self.q = {e: [] for e in ENGS}
        self.last_w = {}
        self.readers = {}
        self.stage_stack.close()
        self.stage_stack = None

    def close(self):
        self.stack.close()


def _layer_stats(P, pref, xt, xkey, bst, mv, rs):
    for j in range(4):
        P.op("vector", lambda e, j=j: e.bn_stats(out=bst[:, j * 6:(j + 1) * 6], in_=xt[:, j * 512:(j + 1) * 512]),
             reads=[xkey], writes=[pref + "bst"])
    P.op("vector", lambda e: e.bn_aggr(out=mv[:], in_=bst[:]), reads=[pref + "bst"], writes=[pref + "mv"])
    P.op("vector", lambda e: e.tensor_scalar(out=rs[:], in0=mv[:, 1:2], scalar1=EPS, scalar2=None, op0=ALU.add),
         reads=[pref + "mv"], writes=[pref + "rs"])
    P.op("scalar", lambda e: e.activation(out=rs[:], in_=rs[:], func=AF.Ln), reads=[pref + "rs"], writes=[pref + "rs"])
    P.op("scalar", lambda e: e.activation(out=rs[:], in_=rs[:], func=AF.Exp, scale=-0.5),
         reads=[pref + "rs"], writes=[pref + "rs"])


def _make_ident(P, ident_f, ident_b):
    P.op("gpsimd", lambda e: e.memset(ident_f[:], 1.0), writes=["ident_f"])
    P.op("gpsimd", lambda e: e.affine_select(out=ident_f[:], in_=ident_f[:], pattern=[[-1, 128]],
                                             compare_op=ALU.is_equal, fill=0.0, base=0, channel_multiplier=1),
         reads=["ident_f"], writes=["ident_f"])
    if ident_b is not None:
        P.op("vector", lambda e: e.tensor_copy(out=ident_b[:], in_=ident_f[:]), reads=["ident_f"], writes=["ident_b"])


def stage_mod(P, l, T):
    P.begin_stage()
    craw = P.sb("m_craw", [128, 2, 16], F32)
    sig = P.sb("m_sig", [128, 2, 16], F32)
    c2 = P.sb("m_c2", [128, 16, 2], F32)
    bm = P.sb("m_bm", [2, INW], F32)
    res = P.sb("m_res", [2, INW], F32)
    wt = [P.sb(f"m_wt{i}", [128, 16, 512], F32) for i in range(2)]
    pss = [P.ps(f"m_ps{i}", [2, 512]) for i in range(2)]
    P.dma("sync", lambda e: e.dma_start(out=craw[:, 0, :], in_=T["c"].rearrange("(p k) -> p k", k=16)), writes=["craw"])
    P.dma("sync", lambda e: e.dma_start(out=craw[:, 1, :], in_=T["c_ctx"].rearrange("(p k) -> p k", k=16)), writes=["craw"])
    P.dma("sync", lambda e: e.dma_start(out=bm[:], in_=T["b_mod"][l:l + 1, :].partition_broadcast(2)), writes=["bm"])
    P.op("scalar", lambda e: e.activation(out=sig[:], in_=craw[:], func=AF.Sigmoid), reads=["craw"], writes=["sig"])
    P.op("vector", lambda e: e.tensor_tensor(out=c2[:].rearrange("p k j -> p j k"), in0=craw[:], in1=sig[:], op=ALU.mult),
         reads=["craw", "sig"], writes=["c2"])
    wsrc = T["w_mod"][l].rearrange("(p k) n -> p k n", k=16)
    for n in range(24):
        w = wt[n % 2]
        wk = f"wt{n % 2}"
        P.dma("sync", lambda e, w=w, n=n: e.dma_start(out=w[:], in_=wsrc[:, :, n * 512:(n + 1) * 512]), writes=[wk])
        ps = pss[n % 2]
        pk = f"mps{n % 2}"
        for k in range(16):
            P.op("tensor", lambda e, ps=ps, w=w, k=k: e.matmul(ps[:], lhsT=c2[:, k, :], rhs=w[:, k, :],
                                                              start=(k == 0), stop=(k == 15)),
                 reads=["c2", wk], writes=[pk], inc=(k == 15))
        P.op("vector", lambda e, ps=ps, n=n: e.tensor_tensor(out=res[:, n * 512:(n + 1) * 512], in0=ps[:],
                                                            in1=bm[:, n * 512:(n + 1) * 512], op=ALU.add),
             reads=[pk, "bm"], writes=["res"])
    P.dma("sync", lambda e: e.dma_start(out=T["modv"][l], in_=res[:]), reads=["res"], writes=["modv"])
    P.end_stage()


def _tok_src(T, l, i):
    if l == 0:
        if i < 2:
            return T["ctx"][i * 128:(i + 1) * 128, :]
        return T["x"][(i - 2) * 128:(i - 1) * 128, :]
    return T["xcur"][i * 128:(i + 1) * 128, :]


def stage_A(P, l, T):
    P.begin_stage()
    NT = TT // 128
    hT = P.sb("a_hT", [128, 16, TT], BF16)
    ident_f = P.sb("a_idf", [128, 128], F32)
    ident_b = P.sb("a_idb", [128, 128], BF16)
    xbuf = [P.sb(f"a_x{i}", [128, D], F32) for i in range(2)]
    hb = P.sb("a_hb", [128, D], BF16)
    bc = {k: P.sb("a_bc_" + k, [128, D], F32) for k in ("s_c", "sh_c", "s_l", "sh_l")}
    bst = P.sb("a_bst", [128, 24], F32)
    mv = P.sb("a_mv", [128, 2], F32)
    rs = P.sb("a_rs", [128, 1], F32)
    ptr = P.ps("a_ptr", [128, D], BF16)
    _make_ident(P, ident_f, ident_b)
    modv = T["modv"][l]
    for nm, row, col in (("sh_c", 1, 0), ("s_c", 1, 1), ("sh_l", 0, 0), ("s_l", 0, 1)):
        P.dma("sync", lambda e, nm=nm, row=row, col=col: e.dma_start(
            out=bc[nm][:], in_=modv[row:row + 1, col * D:(col + 1) * D].partition_broadcast(128)),
            reads=["modv"], writes=["bc" + nm])
    for nm in ("s_c", "s_l"):
        P.op("gpsimd", lambda e, nm=nm: e.tensor_scalar(out=bc[nm][:], in0=bc[nm][:], scalar1=1.0, scalar2=None, op0=ALU.add),
             reads=["bc" + nm], writes=["bc" + nm])
    for i in range(NT):
        xt = xbuf[i % 2]
        xk = f"ax{i % 2}"
        P.dma("sync", lambda e, xt=xt, i=i: e.dma_start(out=xt[:], in_=_tok_src(T, l, i)), reads=["xcur"], writes=[xk])
        _layer_stats(P, "a_", xt, xk, bst, mv, rs)
        sfx = "c" if i < 2 else "l"
        P.op("vector", lambda e, xt=xt: e.tensor_scalar(out=xt[:], in0=xt[:], scalar1=mv[:, 0:1], scalar2=rs[:, 0:1],
                                                       op0=ALU.subtract, op1=ALU.mult),
             reads=[xk, "a_mv", "a_rs"], writes=[xk])
        P.op("vector", lambda e, xt=xt, sfx=sfx: e.tensor_tensor(out=xt[:], in0=xt[:], in1=bc["s_" + sfx][:], op=ALU.mult),
             reads=[xk, "bcs_" + sfx], writes=[xk])
        P.op("vector", lambda e, xt=xt, sfx=sfx: e.tensor_tensor(out=hb[:], in0=xt[:], in1=bc["sh_" + sfx][:], op=ALU.add),
             reads=[xk, "bcsh_" + sfx], writes=["hb"])
        for k in range(16):
            P.op("tensor", lambda e, k=k: e.transpose(ptr[:, k * 128:(k + 1) * 128], hb[:, k * 128:(k + 1) * 128], ident_b[:]),
                 reads=["hb", "ident_b"], writes=["ptr"], inc=(k == 15))
        P.op("scalar", lambda e, i=i: e.copy(out=hT[:, :, i * 128:(i + 1) * 128],
                                             in_=ptr[:].rearrange("p (k t) -> p k t", t=128)),
             reads=["ptr"], writes=["hT"])
    if T.get("dbg_h") is not None:
        P.dma("sync", lambda e: e.dma_start(out=T["dbg_h"].rearrange("(k p) t -> p k t", p=128), in_=hT[:]),
              reads=["hT"], writes=["dbg_h"])
    CW = 256
    wst = [P.sb(f"a_wst{i}", [128, 16, CW], F32) for i in range(2)]
    wbf = [P.sb(f"a_wbf{i}", [128, 16, CW], BF16) for i in range(2)]
    stg = [P.sb(f"a_stg{i}", [128, TT], F32) for i in range(2)]
    tmp = [P.sb(f"a_tmp{i}", [128, 512], F32) for i in range(2)]
    vst = [P.sb(f"a_vst{i}", [128, CW], BF16) for i in range(2)]
    pss = [P.ps(f"a_ps{i}", [128, 512]) for i in range(4)]
    wsrc = T["w_in"][l].rearrange("(k p) n -> p k n", p=128)
    tchunks = [(0, 256)] + [(256 + 512 * j, 512) for j in range(4)]
    fam_of_col = [("cb", FM_CB), ("cc", FM_CC), ("cv", FM_CV), ("q", FM_Q), ("zf", FM_ZF), ("zb", FM_ZB),
                  ("v", None), ("og", FM_OG), ("ga", FM_GA), ("ga", FM_GA + 1024), ("gb", FM_GB), ("gb", FM_GB + 1024)]
    pi = 0
    si = 0
    ti = 0
    for n in range(INW // CW):
        ws = wst[n % 2]
        wb = wbf[n % 2]
        wsk, wbk = f"wst{n % 2}", f"wbf{n % 2}"
        P.dma("sync", lambda e, ws=ws, n=n: e.dma_start(out=ws[:], in_=wsrc[:, :, n * CW:(n + 1) * CW]), writes=[wsk])
        P.op("gpsimd", lambda e, ws=ws, wb=wb: e.tensor_copy(out=wb[:], in_=ws[:]), reads=[wsk], writes=[wbk])
        fam, row0 = fam_of_col[(n * CW) // 1024]
        coff = (n * CW) % 1024
        if fam == "v":
            for i in range(NT):
                ps = pss[pi % 4]
                pk = f"aps{pi % 4}"
                pi += 1
                for k in range(16):
                    P.op("tensor", lambda e, ps=ps, wb=wb, k=k, i=i: e.matmul(
                        ps[:, 0:CW], lhsT=hT[:, k, i * 128:(i + 1) * 128], rhs=wb[:, k, :], start=(k == 0), stop=(k == 15)),
                        reads=["hT", wbk], writes=[pk], inc=(k == 15))
                vs = vst[i % 2]
                vk = f"vst{i % 2}"
                P.op("scalar", lambda e, ps=ps, vs=vs: e.copy(out=vs[:], in_=ps[:, 0:CW]), reads=[pk], writes=[vk])
                P.dma("scalar", lambda e, vs=vs, i=i, coff=coff: e.dma_start(
                    out=T["VT"][i * 128:(i + 1) * 128, coff:coff + CW], in_=vs[:]), reads=[vk], writes=["VT"])
            continue
        for sub in range(CW // 128):
            sg = stg[si % 2]
            sk = f"stg{si % 2}"
            si += 1
            for (t0, tn) in tchunks:
                ps = pss[pi % 4]
                pk = f"aps{pi % 4}"
                pi += 1
                for k in range(16):
                    P.op("tensor", lambda e, ps=ps, wb=wb, k=k, sub=sub, t0=t0, tn=tn: e.matmul(
                        ps[:, 0:tn], lhsT=wb[:, k, sub * 128:(sub + 1) * 128], rhs=hT[:, k, t0:t0 + tn],
                        start=(k == 0), stop=(k == 15)),
                        reads=["hT", wbk], writes=[pk], inc=(k == 15))
                dst = sg[:, t0:t0 + tn]
                if fam in ("cb", "cc", "cv"):
                    P.op("scalar", lambda e, ps=ps, dst=dst, tn=tn: e.copy(out=dst, in_=ps[:, 0:tn]), reads=[pk], writes=[sk])
                elif fam in ("zf", "zb", "ga", "gb"):
                    P.op("scalar", lambda e, ps=ps, dst=dst, tn=tn: e.activation(out=dst, in_=ps[:, 0:tn], func=AF.Sigmoid),
                         reads=[pk], writes=[sk])
                else:
                    tm = tmp[ti % 2]
                    tk = f"atmp{ti % 2}"
                    ti += 1
                    P.op("scalar", lambda e, ps=ps, tm=tm, tn=tn: e.activation(out=tm[:, 0:tn], in_=ps[:, 0:tn], func=AF.Sigmoid),
                         reads=[pk], writes=[tk])
                    scl = DK ** -0.5 if fam == "q" else 1.0
                    P.op("vector", lambda e, ps=ps, tm=tm, dst=dst, tn=tn, scl=scl: e.scalar_tensor_tensor(
                        out=dst, in0=ps[:, 0:tn], scalar=scl, in1=tm[:, 0:tn], op0=ALU.mult, op1=ALU.mult),
                        reads=[pk, tk], writes=[sk])
            r0 = row0 + coff + sub * 128
            P.dma("scalar", lambda e, sg=sg, r0=r0: e.dma_start(out=T["FM"][r0:r0 + 128, :], in_=sg[:]),
                  reads=[sk], writes=["FM"])
    P.end_stage()


def stage_B(P, l, T):
    P.begin_stage()
    W3 = lambda ap: ap.rearrange("p (c s) -> p c s", s=CH)
    ident_f = P.sb("b_idf", [128, 128], F32)
    ident_b = P.sb("b_idb", [128, 128], BF16)
    ones_f = P.sb("b_ones", [128, 128], F32)
    maskc = P.sb("b_maskc", [128, TT], F32)
    msk = {"f": P.sb("b_mskf", [64, 64], F32), "b": P.sb("b_mskb", [64, 64], F32)}
    lbr = P.sb("b_lbr", [128, 2, 2, NH], F32)
    lb = P.sb("b_lb", [128, 2, NH], F32)
    oml = P.sb("b_oml", [128, 2, NH], F32)
    noml = P.sb("b_noml", [128, 2, NH], F32)
    gn = P.sb("b_gn", [128, 1], F32)
    qs = P.sb("b_qs", [128, TT], F32)
    sgt = {"f": P.sb("b_sgf", [128, TT], F32), "b": P.sb("b_sgb", [128, TT], F32)}
    ogs = P.sb("b_ogs", [128, TT], F32)
    Vc = P.sb("b_Vc", [64, NCHUNK, 128], BF16)
    oT = P.sb("b_oT", [128, TT], F32)
    wkbig = P.sb("b_wkbig", [128, 6, TT], F32)
    wk = [wkbig[:, i, :] for i in range(6)]
    tot = P.sb("b_tot", [128, NCHUNK], F32)
    ebl = {d: P.sb("b_ebl" + d, [128, NCHUNK], F32) for d in "fb"}
    qt = {d: P.sb("b_qt" + d, [128, TT], BF16) for d in "fb"}
    kt = {d: P.sb("b_kt" + d, [128, TT], BF16) for d in "fb"}
    kf = {d: P.sb("b_kf" + d, [128, TT], BF16) for d in "fb"}
    qb = {d: P.sb("b_qb" + d, [128, TT], BF16) for d in "fb"}
    kd = {d: P.sb("b_kd" + d, [128, TT], BF16) for d in "fb"}
    kdT = {d: P.sb("b_kdT" + d, [64, NCHUNK, 128], BF16) for d in "fb"}
    S = {d: [P.sb(f"b_S{d}{i}", [128, 128], F32) for i in range(2)] for d in "fb"}
    scm_all = {d: P.sb("b_scm" + d, [64, NCHUNK, CH], BF16) for d in "fb"}
    upd_all = {"f": wkbig[:, 0:2, :].rearrange("p a t -> p (a t)").rearrange("p (c k) -> p c k", k=128),
               "b": wkbig[:, 2:4, :].rearrange("p a t -> p (a t)").rearrange("p (c k) -> p c k", k=128)}
    upd_keys = {"f": ["wk0", "wk1"], "b": ["wk2", "wk3"]}
    Sb_all = {"f": wkbig[:, 4, :].bitcast(BF16).rearrange("p (c k) -> p c k", k=128),
              "b": wkbig[:, 5, :].bitcast(BF16).rearrange("p (c k) -> p c k", k=128)}
    Sb_keys = {"f": "wk4", "b": "wk5"}
    rsb = P.sb("b_rsb", [128, 512], F32)
    ybt = P.sb("b_ybt", [128, TT], BF16)
    ps_scl = [P.ps(f"b_pssc{i}", [64, 512]) for i in range(2)]
    ps_o = [P.ps("b_pso0", [128, 512])]
    ps_upl = [P.ps(f"b_psup{i}", [128, 512]) for i in range(2)]
    ps_trl = [P.ps(f"b_pstr{i}", [64, 4 * 128], BF16) for i in range(2)]
    ps_ss = P.ps("b_psss", [128, 512])

    _make_ident(P, ident_f, ident_b)
    P.op("gpsimd", lambda e: e.memset(ones_f[:], 1.0), writes=["ones_f"])
    P.op("gpsimd", lambda e: e.memset(maskc[:], 1.0), writes=["maskc"])
    P.op("gpsimd", lambda e: e.memset(W3(maskc[:])[:, :, 0:1], 0.0), reads=["maskc"], writes=["maskc"])
    for d, pat, cm in (("f", 1, -1), ("b", -1, 1)):
        P.op("gpsimd", lambda e, d=d: e.memset(msk[d][:], 1.0), writes=["msk" + d])
        P.op("gpsimd", lambda e, d=d, pat=pat, cm=cm: e.affine_select(
            out=msk[d][:], in_=msk[d][:], pattern=[[pat, 64]], compare_op=ALU.is_ge, fill=0.0, base=0,
            channel_multiplier=cm), reads=["msk" + d], writes=["msk" + d])
    for j in range(2):
        for dd in range(2):
            P.dma("sync", lambda e, j=j, dd=dd: e.dma_start(
                out=lbr[:, j, dd, :], in_=T["lb_raw"][j, dd, :].rearrange("(h p) -> p h", p=128),
                allow_slow_non_contiguous=True), writes=["lbr"])
    P.dma("sync", lambda e: e.dma_start(out=gn[:], in_=T["hg_norm_g"][l, :].rearrange("(p o) -> p o", o=1)), writes=["gn"])
    if l == 0:
        P.op("vector", lambda e: e.memset(lb[:], 0.0), writes=["lb"])
    else:
        P.op("vector", lambda e: e.tensor_tensor(out=lb[:], in0=lbr[:, 0, :, :], in1=lbr[:, 1, :, :], op=ALU.subtract),
             reads=["lbr"], writes=["lb"])
        P.op("scalar", lambda e: e.activation(out=lb[:], in_=lb[:], func=AF.Sigmoid), reads=["lb"], writes=["lb"])
    P.op("vector", lambda e: e.tensor_scalar(out=oml[:], in0=lb[:], scalar1=-1.0, scalar2=1.0, op0=ALU.mult, op1=ALU.add),
         reads=["lb"], writes=["oml"])
    P.op("vector", lambda e: e.tensor_scalar(out=noml[:], in0=oml[:], scalar1=-1.0, scalar2=None, op0=ALU.mult),
         reads=["oml"], writes=["noml"])

    fwd_order = list(range(NCHUNK))
    bwd_order = [3, 2, 1, 0] + list(range(NCHUNK - 1, 3, -1))
    for hd in range(NH):
        r = hd * 128
        P.dma("sync", lambda e, r=r: e.dma_start(out=qs[:], in_=T["FM"][FM_Q + r:FM_Q + r + 128, :]), reads=["FM"], writes=["qs"])
        P.dma("sync", lambda e, r=r: e.dma_start(out=sgt["f"][:], in_=T["FM"][FM_ZF + r:FM_ZF + r + 128, :]), reads=["FM"], writes=["sgf"])
        P.dma("sync", lambda e, r=r: e.dma_start(out=sgt["b"][:], in_=T["FM"][FM_ZB + r:FM_ZB + r + 128, :]), reads=["FM"], writes=["sgb"])
        P.dma("sync", lambda e, r=r: e.dma_start(out=ogs[:], in_=T["FM"][FM_OG + r:FM_OG + r + 128, :]), reads=["FM"], writes=["ogs"])
        P.dma("sync", lambda e, r=r: e.dma_start(out=Vc[:], in_=T["VT"][:, r:r + 128].rearrange("(c s) v -> s c v", s=CH)),
              reads=["VT"], writes=["Vc"])
        for jj in range(32):
            cjob = hd * 32 + jj
            tb, c = cjob % 2, cjob // 2
            src_tab = T["peer_u"] if tb == 0 else T["peer_v"]
            P.dma("gpsimd", lambda e, src_tab=src_tab, tb=tb, c=c: e.dma_start(
                out=T["UVb"][l][c * 128:(c + 1) * 128, tb, :], in_=src_tab[l, c * 128:(c + 1) * 128, :]), writes=[("UVb", cjob)])
        for di, d in enumerate("fb"):
            sg = sgt[d]
            sgk = "sg" + d
            lf, bcs, cc, kk, e1, e2 = wk
            a_oml, a_lb, a_noml = oml[:, di, hd:hd + 1], lb[:, di, hd:hd + 1], noml[:, di, hd:hd + 1]
            P.op("vector", lambda e, sg=sg, a_oml=a_oml, a_lb=a_lb: e.tensor_scalar(
                out=lf[:], in0=sg[:], scalar1=a_oml, scalar2=a_lb, op0=ALU.mult, op1=ALU.add),
                reads=[sgk, "oml", "lb"], writes=["wk0"])
            P.op("scalar", lambda e: e.activation(out=lf[:], in_=lf[:], func=AF.Ln), reads=["wk0"], writes=["wk0"])
            P.op("vector", lambda e: e.tensor_tensor_scan(out=bcs[:], data0=maskc[:], data1=lf[:], initial=0.0,
                                                          op0=ALU.mult, op1=ALU.add),
                 reads=["wk0", "maskc"], writes=["wk1"])
            P.op("vector", lambda e: e.tensor_copy(out=tot[:], in_=W3(bcs[:])[:, :, CH - 1]), reads=["wk1"], writes=["tot"])
            totb = tot[:].unsqueeze(2).to_broadcast([128, NCHUNK, CH])
            if d == "f":
                c_ap, ck = bcs, "wk1"
            else:
                P.op("vector", lambda e: e.tensor_tensor(out=cc[:], in0=lf[:], in1=bcs[:], op=ALU.subtract),
                     reads=["wk0", "wk1"], writes=["wk2"])
                P.op("vector", lambda e, totb=totb: e.tensor_tensor(out=W3(cc[:]), in0=W3(cc[:]), in1=totb, op=ALU.add),
                     reads=["wk2", "tot"], writes=["wk2"])
                c_ap, ck = cc, "wk2"
            P.op("scalar", lambda e, d=d: e.activation(out=ebl[d][:], in_=tot[:], func=AF.Exp), reads=["tot"], writes=["ebl" + d])
            P.op("vector", lambda e, sg=sg, a_noml=a_noml, a_oml=a_oml: e.tensor_scalar(
                out=kk[:], in0=sg[:], scalar1=a_noml, scalar2=a_oml, op0=ALU.mult, op1=ALU.add),
                reads=[sgk, "oml", "noml"], writes=["wk3"])
            W4 = lambda ap: ap.rearrange("p (c h s) -> p c h s", h=2, s=CH // 2)
            h1, h2, i1, ib = (0, 1, 15, 31) if d == "f" else (1, 0, 47, 32)
            c4, c3 = W4(c_ap[:]), W3(c_ap[:])
            r1b = c3[:, :, i1:i1 + 1].to_broadcast([128, NCHUNK, CH // 2])
            rbb = c3[:, :, ib:ib + 1].to_broadcast([128, NCHUNK, CH // 2])
            rbb64 = c3[:, :, ib:ib + 1].to_broadcast([128, NCHUNK, CH])
            P.op("vector", lambda e, c4=c4, r1b=r1b, h1=h1: e.tensor_tensor(out=W4(e1[:])[:, :, h1, :], in0=c4[:, :, h1, :], in1=r1b, op=ALU.subtract),
                 reads=[ck], writes=["wk4"])
            P.op("vector", lambda e, c4=c4, rbb=rbb, h2=h2: e.tensor_tensor(out=W4(e1[:])[:, :, h2, :], in0=c4[:, :, h2, :], in1=rbb, op=ALU.subtract),
                 reads=[ck], writes=["wk4"])
            P.op("gpsimd", lambda e, h2=h2: e.memset(W4(e2[:])[:, :, h2, :], 0.0), writes=["wk5"])
            P.op("scalar", lambda e, h1=h1: e.activation(out=W4(e2[:])[:, :, h1, :], in_=W4(e1[:])[:, :, h1, :], func=AF.Exp, scale=-1.0),
                 reads=["wk4"], writes=["wk5"])
            P.op("vector", lambda e, d=d: e.tensor_tensor(out=kt[d][:], in0=kk[:], in1=e2[:], op=ALU.mult),
                 reads=["wk3", "wk5"], writes=["kt" + d])
            P.op("scalar", lambda e: e.activation(out=e1[:], in_=e1[:], func=AF.Exp), reads=["wk4"], writes=["wk4"])
            P.op("vector", lambda e, d=d: e.tensor_tensor(out=qt[d][:], in0=qs[:], in1=e1[:], op=ALU.mult),
                 reads=["qs", "wk4"], writes=["qt" + d])
            P.op("vector", lambda e, c3=c3, rbb64=rbb64: e.tensor_tensor(out=W3(e2[:]), in0=rbb64, in1=c3, op=ALU.subtract),
                 reads=[ck, "kt" + d], writes=["wk5"])
            P.op("scalar", lambda e: e.activation(out=e2[:], in_=e2[:], func=AF.Exp), reads=["wk5"], writes=["wk5"])
            P.op("vector", lambda e, d=d: e.tensor_tensor(out=kf[d][:], in0=kk[:], in1=e2[:], op=ALU.mult),
                 reads=["wk3", "wk5"], writes=["kf" + d])
            P.op("scalar", lambda e, c_ap=c_ap: e.activation(out=lf[:], in_=c_ap[:], func=AF.Exp), reads=[ck], writes=["wk0"])
            P.op("vector", lambda e, d=d: e.tensor_tensor(out=qb[d][:], in0=qs[:], in1=lf[:], op=ALU.mult),
                 reads=["qs", "wk0"], writes=["qb" + d])
            P.op("vector", lambda e, c_ap=c_ap, totb=totb: e.tensor_tensor(out=W3(e1[:]), in0=totb, in1=W3(c_ap[:]), op=ALU.subtract),
                 reads=[ck, "tot", "qt" + d], writes=["wk4"])
            P.op("scalar", lambda e: e.activation(out=e1[:], in_=e1[:], func=AF.Exp), reads=["wk4"], writes=["wk4"])
            P.op("vector", lambda e, d=d: e.tensor_tensor(out=kd[d][:], in0=kk[:], in1=e1[:], op=ALU.mult),
                 reads=["wk3", "wk4"], writes=["kd" + d])
            for g4 in range(NCHUNK // 4):
                ps_tr, ptk = ps_trl[g4 % 2], f"ps_tr{g4 % 2}"
                for j in range(4):
                    c = g4 * 4 + j
                    P.op("tensor", lambda e, d=d, c=c, j=j, ps_tr=ps_tr: e.transpose(ps_tr[:, j * 128:(j + 1) * 128],
                                                                       kd[d][:, c * CH:(c + 1) * CH], ident_b[:]),
                         reads=["kd" + d, "ident_b"], writes=[ptk], inc=(j == 3))
                P.op("scalar", lambda e, d=d, g4=g4, ps_tr=ps_tr: e.copy(out=kdT[d][:, g4 * 4:(g4 + 1) * 4, :],
                                                          in_=ps_tr[:].rearrange("s (j k) -> s j k", k=128)),
                     reads=[ptk], writes=["kdT" + d])
        order = {"f": fwd_order, "b": bwd_order}
        posn = {d: {c: k for k, c in enumerate(order[d])} for d in "fb"}
        groups8 = [(g0, min(8, NCHUNK - g0)) for g0 in range(0, NCHUNK, 8)]
        for d in "fb":
            for g4 in range(NCHUNK // 4):
                pu, puk = ps_upl[g4 % 2], f"ps_up{g4 % 2}"
                for j in range(4):
                    c = g4 * 4 + j
                    P.op("tensor", lambda e, d=d, c=c, j=j, pu=pu: e.matmul(pu[:, j * 128:(j + 1) * 128], lhsT=kdT[d][:, c, :], rhs=Vc[:, c, :],
                                                                   start=True, stop=True),
                         reads=["kdT" + d, "Vc"], writes=[puk], inc=(j == 3))
                P.op("scalar", lambda e, d=d, g4=g4, pu=pu: e.copy(out=upd_all[d][:, g4 * 4:(g4 + 1) * 4, :],
                                                          in_=pu[:].rearrange("p (j k) -> p j k", k=128)),
                     reads=[puk], writes=upd_keys[d])
        def p1_group(d, g0, gcnt, sci):
            psc, psk = ps_scl[sci % 2], f"ps_sc{sci % 2}"
            kA, kB = (kt[d], kf[d]) if d == "f" else (kf[d], kt[d])
            for j in range(gcnt):
                c = g0 + j
                cs = slice(c * CH, (c + 1) * CH)
                hA = slice(c * CH, c * CH + CH // 2)
                hB = slice(c * CH + CH // 2, (c + 1) * CH)
                P.op("tensor", lambda e, cs=cs, hA=hA, j=j: e.matmul(psc[:, j * CH:j * CH + CH // 2], lhsT=kA[:, cs], rhs=qt[d][:, hA], start=True, stop=True),
                     reads=["kt" + d, "kf" + d, "qt" + d], writes=[psk], inc=False)
                P.op("tensor", lambda e, cs=cs, hB=hB, j=j: e.matmul(psc[:, j * CH + CH // 2:(j + 1) * CH], lhsT=kB[:, cs], rhs=qt[d][:, hB], start=True, stop=True),
                     reads=["kt" + d, "kf" + d, "qt" + d], writes=[psk], inc=(j == gcnt - 1))
            P.op("vector", lambda e: e.tensor_tensor(
                out=scm_all[d][:, g0:g0 + gcnt, :], in0=psc[:, 0:gcnt * CH].rearrange("s (j t) -> s j t", t=CH),
                in1=msk[d][:].unsqueeze(1).to_broadcast([64, gcnt, CH]), op=ALU.mult),
                reads=[psk, "msk" + d], writes=["scm" + d])
        p1_jobs = [(d, g0, gcnt) for (g0, gcnt) in groups8 for d in "fb"]
        p1_done = 0
        for d in "fb":
            P.op("gpsimd", lambda e, d=d: e.memset(S[d][1][:], 0.0), writes=[("S", d, 1)])
            P.op("gpsimd", lambda e, d=d: e.memset(Sb_all[d][:, 0, :], 0.0), writes=[Sb_keys[d]])
        for k in range(NCHUNK - 1):
            if k % 3 == 0 and p1_done < len(p1_jobs):
                p1_group(*p1_jobs[p1_done], p1_done)
                p1_done += 1
            for d in "fb":
                c = order[d][k]
                So, Sn = S[d][(k + 1) % 2], S[d][k % 2]
                P.op("vector", lambda e, d=d, c=c, So=So, Sn=Sn: e.scalar_tensor_tensor(out=Sn[:], in0=So[:], scalar=ebl[d][:, c:c + 1],
                                                                                   in1=upd_all[d][:, c, :], op0=ALU.mult, op1=ALU.add),
                     reads=[("S", d, (k + 1) % 2), "ebl" + d] + upd_keys[d], writes=[("S", d, k % 2)])
                P.op("scalar", lambda e, d=d, k=k, Sn=Sn: e.copy(out=Sb_all[d][:, k + 1, :], in_=Sn[:]),
                     reads=[("S", d, k % 2)], writes=[Sb_keys[d]])
        while p1_done < len(p1_jobs):
            p1_group(*p1_jobs[p1_done], p1_done)
            p1_done += 1
        for gi_, (g0, gcnt) in enumerate(groups8):
            po, pk = ps_o[0], "ps_o0"
            for j in range(gcnt):
                c = g0 + j
                cs = slice(c * CH, (c + 1) * CH)
                dst = po[:, j * CH:(j + 1) * CH]
                P.op("tensor", lambda e, c=c, dst=dst: e.matmul(dst, lhsT=Vc[:, c, :], rhs=scm_all["f"][:, c, :], start=True, stop=False),
                     reads=["Vc", "scmf"], writes=[pk], inc=False)
                P.op("tensor", lambda e, c=c, dst=dst: e.matmul(dst, lhsT=Vc[:, c, :], rhs=scm_all["b"][:, c, :], start=False, stop=False),
                     reads=["Vc", "scmb"], writes=[pk], inc=False)
                P.op("tensor", lambda e, c=c, cs=cs, dst=dst: e.matmul(dst, lhsT=Sb_all["f"][:, posn["f"][c], :], rhs=qb["f"][:, cs], start=False, stop=False),
                     reads=[Sb_keys["f"], "qbf"], writes=[pk], inc=False)
                P.op("tensor", lambda e, c=c, cs=cs, dst=dst: e.matmul(dst, lhsT=Sb_all["b"][:, posn["b"][c], :], rhs=qb["b"][:, cs], start=False, stop=True),
                     reads=[Sb_keys["b"], "qbb"], writes=[pk], inc=(j == gcnt - 1))
            P.op("scalar", lambda e, po=po, g0=g0, gcnt=gcnt: e.copy(out=oT[:, g0 * CH:(g0 + gcnt) * CH], in_=po[:, 0:gcnt * CH]),
                 reads=[pk], writes=[("oT", g0)])
        okeys = [("oT", g0) for g0 in range(0, NCHUNK, 8)]
        osq = wk[0]
        P.op("scalar", lambda e: e.activation(out=osq[:], in_=oT[:], func=AF.Square), reads=okeys, writes=["wk0"])
        for (t0, tn) in [(0, 256)] + [(256 + 512 * j, 512) for j in range(4)]:
            P.op("tensor", lambda e, t0=t0, tn=tn: e.matmul(ps_ss[:, 0:tn], lhsT=ones_f[:], rhs=osq[:, t0:t0 + tn], start=True, stop=True),
                 reads=["ones_f", "wk0"], writes=["ps_ss"])
            P.op("vector", lambda e, tn=tn: e.tensor_scalar(out=rsb[:, 0:tn], in0=ps_ss[:, 0:tn], scalar1=1.0 / 128, scalar2=EPS,
                                                           op0=ALU.mult, op1=ALU.add), reads=["ps_ss"], writes=["rsb"])
            P.op("scalar", lambda e, tn=tn: e.activation(out=rsb[:, 0:tn], in_=rsb[:, 0:tn], func=AF.Ln), reads=["rsb"], writes=["rsb"])
            P.op("scalar", lambda e, tn=tn: e.activation(out=rsb[:, 0:tn], in_=rsb[:, 0:tn], func=AF.Exp, scale=-0.5),
                 reads=["rsb"], writes=["rsb"])
            P.op("vector", lambda e, t0=t0, tn=tn: e.tensor_tensor(out=rsb[:, 0:tn], in0=rsb[:, 0:tn], in1=oT[:, t0:t0 + tn], op=ALU.mult),
                 reads=["rsb"] + okeys, writes=["rsb"])
            P.op("vector", lambda e, t0=t0, tn=tn: e.scalar_tensor_tensor(out=ybt[:, t0:t0 + tn], in0=rsb[:, 0:tn], scalar=gn[:, 0:1],
                                                                        in1=ogs[:, t0:t0 + tn], op0=ALU.mult, op1=ALU.mult),
                 reads=["rsb", "gn", "ogs"], writes=["ybt"])
        P.dma("scalar", lambda e, r=r: e.dma_start(out=T["YB"][r:r + 128, :], in_=ybt[:]), reads=["ybt"], writes=["YB"])
        if T.get("dbg_o") is not None:
            P.dma("scalar", lambda e, r=r: e.dma_start(out=T["dbg_o"][r:r + 128, :], in_=oT[:]), reads=okeys, writes=["dbg_o"])
    P.end_stage()


def _load_cast_weight(P, pref, dst, src, nk, stage, nq=2, stage_keys=None):
    view = src.rearrange("(k p) n -> p k n", p=128)
    for j in range(nk // nq):
        st = stage[j % 2]
        sk = stage_keys[j % 2] if stage_keys else f"wstage{j % 2}"
        P.dma("sync", lambda e, st=st, j=j: e.dma_start(out=st[:], in_=view[:, j * nq:(j + 1) * nq, :]), writes=[sk])
        eng = "gpsimd" if j % 2 == 0 else "scalar"
        if eng == "gpsimd":
            P.op("gpsimd", lambda e, st=st, j=j: e.tensor_copy(out=dst[:, j * nq:(j + 1) * nq, :], in_=st[:]), reads=[sk], writes=[pref])
        else:
            P.op("scalar", lambda e, st=st, j=j: e.copy(out=dst[:, j * nq:(j + 1) * nq, :], in_=st[:]), reads=[sk], writes=[pref])


def stage_C(P, l, T):
    last = l == DEPTH - 1
    P.begin_stage()
    Wpa = P.sb("c_wpa", [128, 8, D], BF16)
    Wpb = P.sb("c_wpb", [128, 8, D], BF16)
    stage = [P.sb(f"c_stage{i}", [128, 2, D], F32) for i in range(2)]
    cw = P.sb("c_cw", [128, 3, 8], F32)
    cbias = P.sb("c_cb", [128, 8], F32)
    cin = {k: [P.sb(f"c_in_{k}{i}", [128, 512], F32) for i in range(2)] for k in ("cb", "cc", "cv")}
    u = P.sb("c_u", [128, 512], F32)
    y = P.sb("c_y", [128, 512], F32)
    yaTl = [P.sb(f"c_yaT{i}", [128, 8, 512], BF16) for i in range(2)]
    ybTl = [P.sb(f"c_ybT{i}", [128, 8, 512], BF16) for i in range(2)]
    sga = [P.sb(f"c_sga{i}", [128, 512], F32) for i in range(2)]
    sgb = [P.sb(f"c_sgb{i}", [128, 512], F32) for i in range(2)]
    m1 = [P.sb(f"c_m1{i}", [128, 512], F32) for i in range(2)]
    m2 = [P.sb(f"c_m2{i}", [128, 512], F32) for i in range(2)]
    mT = P.sb("c_mT", [128, 16, 512], BF16)
    ps1 = [P.ps(f"c_ps1{i}", [128, 512]) for i in range(4)]
    ps2 = [P.ps(f"c_ps2{i}", [128, 512]) for i in range(4)]
    _load_cast_weight(P, "Wpa", Wpa, T["w_pa"][l], 8, stage)
    _load_cast_weight(P, "Wpb", Wpb, T["w_pb"][l], 8, stage)
    for k in range(3):
        P.dma("sync", lambda e, k=k: e.dma_start(out=cw[:, k, :], in_=T["conv_w"][l, k, :].rearrange("(c p) -> p c", p=128),
                                                allow_slow_non_contiguous=True), writes=["cw"])
    P.dma("sync", lambda e: e.dma_start(out=cbias[:], in_=T["conv_b"][l, :].rearrange("(c p) -> p c", p=128),
                                        allow_slow_non_contiguous=True), writes=["cbias"])
    tchunks = [(0, 256, 256)] + [(256 + 512 * j, 512, 64) for j in range(4)]
    if last:
        tchunks = tchunks[1:]
    it = [0]
    def conv(ci):
        t0, tn, rowlen = tchunks[ci]
        yaT, kya = yaTl[ci % 2], f"yaT{ci % 2}"
        ybT, kyb = ybTl[ci % 2], f"ybT{ci % 2}"
        R3 = lambda ap, tn=tn, rowlen=rowlen: ap[:, 0:tn].rearrange("p (r s) -> p r s", s=rowlen)
        for ct in range(8):
            bufs = {}
            for k, base in (("cb", FM_CB), ("cc", FM_CC), ("cv", FM_CV)):
                b = cin[k][it[0] % 2]
                bufs[k] = (b, f"cin{k}{it[0] % 2}")
                P.dma("sync", lambda e, b=b, base=base, ct=ct, t0=t0, tn=tn: e.dma_start(
                    out=b[:, 0:tn], in_=T["FM"][base + ct * 128:base + (ct + 1) * 128, t0:t0 + tn]),
                    reads=["FM"], writes=[bufs[k][1]])
            it[0] += 1
            (bcb, kcb), (bcc, kcc), (bcv, kcv) = bufs["cb"], bufs["cc"], bufs["cv"]
            P.op("vector", lambda e, bcc=bcc, bcv=bcv, tn=tn: e.tensor_tensor(out=u[:, 0:tn], in0=bcc[:, 0:tn], in1=bcv[:, 0:tn], op=ALU.mult),
                 reads=[kcc, kcv], writes=["u"])
            P.op("vector", lambda e, ct=ct, tn=tn: e.tensor_scalar(out=y[:, 0:tn], in0=u[:, 0:tn], scalar1=cw[:, 1, ct:ct + 1],
                                                                 scalar2=cbias[:, ct:ct + 1], op0=ALU.mult, op1=ALU.add),
                 reads=["u", "cw", "cbias"], writes=["y"])
            P.op("vector", lambda e, ct=ct, R3=R3, rowlen=rowlen: e.scalar_tensor_tensor(
                out=R3(y)[:, :, 1:rowlen], in0=R3(u)[:, :, 0:rowlen - 1], scalar=cw[:, 0, ct:ct + 1], in1=R3(y)[:, :, 1:rowlen],
                op0=ALU.mult, op1=ALU.add), reads=["u", "cw", "y"], writes=["y"])
            P.op("vector", lambda e, ct=ct, R3=R3, rowlen=rowlen: e.scalar_tensor_tensor(
                out=R3(y)[:, :, 0:rowlen - 1], in0=R3(u)[:, :, 1:rowlen], scalar=cw[:, 2, ct:ct + 1], in1=R3(y)[:, :, 0:rowlen - 1],
                op0=ALU.mult, op1=ALU.add), reads=["u", "cw", "y"], writes=["y"])
            P.op("vector", lambda e, ct=ct, bcb=bcb, tn=tn: e.tensor_tensor(out=yaT[:, ct, 0:tn], in0=bcb[:, 0:tn], in1=y[:, 0:tn], op=ALU.mult),
                 reads=[kcb, "y"], writes=[kya])

    def mm(ci):
        t0, tn, rowlen = tchunks[ci]
        yaT, kya = yaTl[ci % 2], f"yaT{ci % 2}"
        ybT, kyb = ybTl[ci % 2], f"ybT{ci % 2}"
        P.dma("sync", lambda e, t0=t0, tn=tn: e.dma_start(out=ybT[:, :, 0:tn], in_=T["YB"][:, t0:t0 + tn].rearrange("(k p) t -> p k t", p=128)),
              reads=["YB"], writes=[kyb])
        for mc in range(16):
            j = mc % 2
            jp = mc % 4
            P.dma("sync", lambda e, mc=mc, j=j, t0=t0, tn=tn: e.dma_start(
                out=sga[j][:, 0:tn], in_=T["FM"][FM_GA + mc * 128:FM_GA + (mc + 1) * 128, t0:t0 + tn]), reads=["FM"], writes=[f"sga{j}"])
            P.dma("sync", lambda e, mc=mc, j=j, t0=t0, tn=tn: e.dma_start(
                out=sgb[j][:, 0:tn], in_=T["FM"][FM_GB + mc * 128:FM_GB + (mc + 1) * 128, t0:t0 + tn]), reads=["FM"], writes=[f"sgb{j}"])
            for k in range(8):
                P.op("tensor", lambda e, mc=mc, k=k, jp=jp, tn=tn: e.matmul(ps1[jp][:, 0:tn], lhsT=Wpa[:, k, mc * 128:(mc + 1) * 128],
                                                                       rhs=yaT[:, k, 0:tn], start=(k == 0), stop=(k == 7)),
                     reads=["Wpa", kya], writes=[f"cps1{jp}"], inc=(k == 7))
            for k in range(8):
                P.op("tensor", lambda e, mc=mc, k=k, jp=jp, tn=tn: e.matmul(ps2[jp][:, 0:tn], lhsT=Wpb[:, k, mc * 128:(mc + 1) * 128],
                                                                       rhs=ybT[:, k, 0:tn], start=(k == 0), stop=(k == 7)),
                     reads=["Wpb", kyb], writes=[f"cps2{jp}"], inc=(k == 7))
            P.op("vector", lambda e, j=j, jp=jp, tn=tn: e.tensor_tensor(out=m1[j][:, 0:tn], in0=ps1[jp][:, 0:tn], in1=sga[j][:, 0:tn], op=ALU.mult),
                 reads=[f"cps1{jp}", f"sga{j}"], writes=[f"m1{j}"])
            P.op("vector", lambda e, j=j, jp=jp, tn=tn: e.tensor_tensor(out=m2[j][:, 0:tn], in0=ps2[jp][:, 0:tn], in1=sgb[j][:, 0:tn], op=ALU.mult),
                 reads=[f"cps2{jp}", f"sgb{j}"], writes=[f"m2{j}"])
            P.op("gpsimd", lambda e, j=j, mc=mc, tn=tn: e.tensor_tensor(out=mT[:, mc, 0:tn], in0=m1[j][:, 0:tn], in1=m2[j][:, 0:tn], op=ALU.add),
                 reads=[f"m1{j}", f"m2{j}"], writes=["mT"])
        P.dma("scalar", lambda e, t0=t0, tn=tn: e.dma_start(out=T["MT"][:, t0:t0 + tn].rearrange("(k p) t -> p k t", p=128), in_=mT[:, :, 0:tn]),
              reads=["mT"], writes=["MT"])
        if T.get("dbg_ya") is not None:
            P.dma("scalar", lambda e, t0=t0, tn=tn: e.dma_start(out=T["dbg_ya"][:, t0:t0 + tn].rearrange("(k p) t -> p k t", p=128), in_=yaT[:, :, 0:tn]),
                  reads=[kya], writes=["dbg_ya"])

    conv(0)
    for ci in range(len(tchunks)):
        if ci + 1 < len(tchunks):
            conv(ci + 1)
        mm(ci)
    P.end_stage()


def _bcast_load(P, dst, key, src_row):
    P.dma("sync", lambda e: e.dma_start(out=dst[:], in_=src_row.partition_broadcast(128)), reads=["modv"], writes=[key])


def _residual_ln(P, pref, src_ps_or_sb, src_key, xt, xk, gb, gkey, lng, lnb, r, rkey, bst, mv, rs, out_ap, out_key, out_dram, aff_eng="gpsimd", res_eng="vector"):
    P.op("vector", lambda e: e.tensor_tensor(out=r[:], in0=src_ps_or_sb, in1=gb[:], op=ALU.mult), reads=[src_key, gkey], writes=[rkey])
    P.op(res_eng, lambda e: e.scalar_tensor_tensor(out=r[:], in0=xt[:], scalar=ALPHA, in1=r[:], op0=ALU.mult, op1=ALU.add),
         reads=[xk, rkey], writes=[rkey])
    _layer_stats(P, pref, r, rkey, bst, mv, rs)
    P.op("vector", lambda e: e.tensor_scalar(out=r[:], in0=r[:], scalar1=mv[:, 0:1], scalar2=rs[:, 0:1], op0=ALU.subtract, op1=ALU.mult),
         reads=[rkey, pref + "mv", pref + "rs"], writes=[rkey])
    P.op(aff_eng, lambda e: e.tensor_tensor(out=r[:], in0=r[:], in1=lng[:], op=ALU.mult), reads=[rkey, "lng"], writes=[rkey])
    P.op(aff_eng, lambda e: e.tensor_tensor(out=out_ap, in0=r[:], in1=lnb[:], op=ALU.add), reads=[rkey, "lnb"], writes=[out_key])
    P.dma("scalar", lambda e: e.dma_start(out=out_dram, in_=out_ap), reads=[out_key], writes=["xout"])


def stage_D(P, l, T):
    last = l == DEPTH - 1
    P.begin_stage()
    Wo = P.sb("d_wo", [128, 16, D], BF16)
    stage = [P.sb(f"d_stage{i}", [128, 2, D], F32) for i in range(2)]
    mTc = [P.sb(f"d_mT{i}", [128, 16, 512], BF16) for i in range(2)]
    xb = [P.sb(f"d_x{i}", [128, D], F32) for i in range(2)]
    rrb = [P.sb(f"d_r{i}", [128, D], F32) for i in range(2)]
    ob = [P.sb(f"d_o{i}", [128, D], F32) for i in range(2)]
    g1b = P.sb("d_g1", [128, D], F32)
    lng = P.sb("d_lng", [128, D], F32)
    lnb = P.sb("d_lnb", [128, D], F32)
    bst = P.sb("d_bst", [128, 24], F32)
    mv = P.sb("d_mv", [128, 2], F32)
    rs = P.sb("d_rs", [128, 1], F32)
    psy = [P.ps(f"d_psy{i}", [128, D]) for i in range(2)]
    _load_cast_weight(P, "Wo", Wo, T["w_o"][l], 16, stage)
    P.dma("sync", lambda e: e.dma_start(out=lng[:], in_=T["ln1_g"][l:l + 1, :].partition_broadcast(128)), writes=["lng"])
    P.dma("sync", lambda e: e.dma_start(out=lnb[:], in_=T["ln1_b"][l:l + 1, :].partition_broadcast(128)), writes=["lnb"])
    modv = T["modv"][l]
    tchunks = [(0, 256)] + [(256 + 512 * j, 512) for j in range(4)]
    if last:
        tchunks = tchunks[1:]
    ti = 0
    for ci, (t0, tn) in enumerate(tchunks):
        if t0 == 0:
            _bcast_load(P, g1b, "g1b", modv[1:2, 2 * D:3 * D])
        elif t0 == 256:
            _bcast_load(P, g1b, "g1b", modv[0:1, 2 * D:3 * D])
        mt = mTc[ci % 2]
        mk = f"dmT{ci % 2}"
        P.dma("sync", lambda e, mt=mt, t0=t0, tn=tn: e.dma_start(out=mt[:, :, 0:tn], in_=T["MT"][:, t0:t0 + tn].rearrange("(k p) t -> p k t", p=128)),
              reads=["MT"], writes=[mk])
        for tt in range(tn // 128):
            i = (t0 + tt * 128) // 128
            xt = xb[ti % 2]
            xk = f"dx{ti % 2}"
            ot = ob[ti % 2]
            ok = f"do{ti % 2}"
            py = psy[ti % 2]
            pk = f"dpsy{ti % 2}"
            ti += 1
            P.dma("sync", lambda e, xt=xt, i=i: e.dma_start(out=xt[:], in_=_tok_src(T, l, i)), reads=["xcur"], writes=[xk])
            for nq in range(4):
                for k in range(16):
                    P.op("tensor", lambda e, py=py, mt=mt, tt=tt, k=k, nq=nq: e.matmul(
                        py[:, nq * 512:(nq + 1) * 512], lhsT=mt[:, k, tt * 128:(tt + 1) * 128], rhs=Wo[:, k, nq * 512:(nq + 1) * 512],
                        start=(k == 0), stop=(k == 15)), reads=[mk, "Wo"], writes=[pk], inc=(k == 15 and nq == 3))
            rr, rrk = rrb[(ti - 1) % 2], f"d_r{(ti - 1) % 2}"
            _residual_ln(P, "d_", py[:], pk, xt, xk, g1b, "g1b", lng, lnb, rr, rrk, bst, mv, rs, ot[:], ok,
                         T["x1"][i * 128:(i + 1) * 128, :])
    P.end_stage()


def _top16_multi(P, jobs):
    for (src, sk, wkb, wkeys, vals, vkey, idxs, ikey) in jobs:
        P.op("vector", lambda e, vals=vals, src=src: e.max(out=vals[:, 0:8], in_=src), reads=[sk], writes=[vkey])
        yield
    for (src, sk, wkb, wkeys, vals, vkey, idxs, ikey) in jobs:
        P.op("vector", lambda e, vals=vals, src=src, idxs=idxs: e.max_index(out=idxs[:, 0:8], in_max=vals[:, 0:8], in_values=src),
             reads=[sk, vkey], writes=[ikey])
        yield
    for (src, sk, wkb, wkeys, vals, vkey, idxs, ikey) in jobs:
        P.op("vector", lambda e, vals=vals, src=src, wkb=wkb: e.match_replace(out=wkb, in_to_replace=vals[:, 0:8], in_values=src, imm_value=-1e30),
             reads=[sk, vkey], writes=wkeys)
        yield
    for (src, sk, wkb, wkeys, vals, vkey, idxs, ikey) in jobs:
        P.op("vector", lambda e, vals=vals, wkb=wkb: e.max(out=vals[:, 8:16], in_=wkb), reads=wkeys, writes=[vkey])
        yield
    for (src, sk, wkb, wkeys, vals, vkey, idxs, ikey) in jobs:
        P.op("vector", lambda e, vals=vals, wkb=wkb, idxs=idxs: e.max_index(out=idxs[:, 8:16], in_max=vals[:, 8:16], in_values=wkb),
             reads=wkeys + [vkey], writes=[ikey])
        yield


def stage_T(P, l, T):
    P.begin_stage()
    for cjob in range(256):
        tb, c = cjob % 2, cjob // 2
        src_tab = T["peer_u"] if tb == 0 else T["peer_v"]
        P.dma("gpsimd", lambda e, src_tab=src_tab, tb=tb, c=c: e.dma_start(
            out=T["UVb"][l][c * 128:(c + 1) * 128, tb, :], in_=src_tab[l, c * 128:(c + 1) * 128, :]), writes=[("UVb", cjob)])
    P.end_stage()


def stage_Q(P, l, T):
    last = l == DEPTH - 1
    P.begin_stage()
    NT = TT // 128
    Wq = P.sb("q_wq", [128, 16, D], BF16)
    keysT = P.sb("q_keysT", [128, 16, 128], BF16)
    ident_f = P.sb("q_idf", [128, 128], F32)
    ident_b = P.sb("q_idb", [128, 128], BF16)
    bc = {k: P.sb("q_bc_" + k, [128, D], F32) for k in ("sh", "s")}
    x1b = [P.sb(f"q_x1{i}", [128, D], F32) for i in range(2)]
    tmpA = P.sb("q_tmpA", [128, D], F32)
    xmbb = [P.sb(f"q_xmb{i}", [128, D], BF16) for i in range(2)]
    scb = [P.sb(f"q_sc{i}", [128, D], F32) for i in range(2)]
    xmT = P.sb("q_xmT", [128, 16, 128], BF16)
    qT = P.sb("q_qT", [128, 16, 128], BF16)
    bst = P.sb("q_bst", [128, 24], F32)
    mv = P.sb("q_mv", [128, 2], F32)
    rs = P.sb("q_rs", [128, 1], F32)
    ps_t = P.ps("q_pst", [128, D], BF16)
    ps_q = P.ps("q_psq", [128, D])
    ps_s = P.ps("q_pss", [128, D // 2])
    _make_ident(P, ident_f, ident_b)
    _load_cast_weight(P, "Wq", Wq, T["peer_wq"][l], 16, [x1b[0][:].rearrange("p (a n) -> p a n", a=1), x1b[1][:].rearrange("p (a n) -> p a n", a=1)],
                      nq=1, stage_keys=["x1b0", "x1b1"])
    kraw = scb[0][:].rearrange("p (b d) -> p b d", d=128)
    P.dma("sync", lambda e: e.dma_start(out=kraw, in_=T["peer_keys"][l].rearrange("h p k d -> k (h p) d")), writes=["sc0"])
    for blk in range(16):
        P.op("tensor", lambda e, blk=blk: e.transpose(ps_q[:, blk * 128:(blk + 1) * 128], kraw[:, blk, :], ident_f[:]),
             reads=["sc0", "ident_f"], writes=["ps_q"], inc=(blk == 15))
    P.op("vector", lambda e: e.tensor_copy(out=keysT[:], in_=ps_q[:].rearrange("p (b k) -> p b k", k=128)), reads=["ps_q"], writes=["keysT"])
    modv = T["modv"][l]
    tiles = list(range(2, NT)) if last else list(range(NT))
    def lnt(ti, i):
        par = ti % 2
        x1t, kx = x1b[par], f"x1b{par}"
        xmb, kxmb = xmbb[par], f"xmb{par}"
        xm, kxm = tmpA, "tmpA"
        if i in (0, 2):
            row = 1 if i == 0 else 0
            _bcast_load(P, bc["sh"], "bcsh", modv[row:row + 1, 3 * D:4 * D])
            _bcast_load(P, bc["s"], "bcs", modv[row:row + 1, 4 * D:5 * D])
            P.op("gpsimd", lambda e: e.tensor_scalar(out=bc["s"][:], in0=bc["s"][:], scalar1=1.0, scalar2=None, op0=ALU.add),
                 reads=["bcs"], writes=["bcs"])
        P.dma("sync", lambda e: e.dma_start(out=x1t[:], in_=T["x1"][i * 128:(i + 1) * 128, :]), reads=["x1"], writes=[kx])
        _layer_stats(P, "q_", x1t, kx, bst, mv, rs)
        P.op("vector", lambda e: e.tensor_scalar(out=xm[:], in0=x1t[:], scalar1=mv[:, 0:1], scalar2=rs[:, 0:1], op0=ALU.subtract, op1=ALU.mult),
             reads=[kx, "q_mv", "q_rs"], writes=[kxm])
        P.op("vector", lambda e: e.tensor_tensor(out=xm[:], in0=xm[:], in1=bc["s"][:], op=ALU.mult), reads=[kxm, "bcs"], writes=[kxm])
        P.op("vector", lambda e: e.tensor_tensor(out=xmb[:], in0=xm[:], in1=bc["sh"][:], op=ALU.add), reads=[kxm, "bcsh"], writes=[kxmb])
        P.dma("scalar", lambda e: e.dma_start(out=T["XMB"][i * 128:(i + 1) * 128, :], in_=xmb[:]), reads=[kxmb], writes=["XMB"])

    def tr(ti):
        xmb, kxmb = xmbb[ti % 2], f"xmb{ti % 2}"
        for k in range(16):
            P.op("tensor", lambda e, k=k: e.transpose(ps_t[:, k * 128:(k + 1) * 128], xmb[:, k * 128:(k + 1) * 128], ident_b[:]),
                 reads=[kxmb, "ident_b"], writes=["ps_t"], inc=(k == 15))
        P.op("scalar", lambda e: e.copy(out=xmT[:], in_=ps_t[:].rearrange("p (k t) -> p k t", t=128)), reads=["ps_t"], writes=["xmT"])

    lnt(0, tiles[0])
    tr(0)
    for ti, i in enumerate(tiles):
        sc, ksc = scb[ti % 2], f"sc{ti % 2}"
        if ti + 1 < len(tiles):
            lnt(ti + 1, tiles[ti + 1])
        for blk in range(16):
            for k in range(16):
                P.op("tensor", lambda e, blk=blk, k=k: e.matmul(ps_q[:, blk * 128:(blk + 1) * 128], lhsT=Wq[:, k, blk * 128:(blk + 1) * 128],
                                                              rhs=xmT[:, k, :], start=(k == 0), stop=(k == 15)),
                     reads=["Wq", "xmT"], writes=["ps_q"], inc=(k == 15 and blk == 15))
        P.op("scalar", lambda e: e.copy(out=qT[:], in_=ps_q[:].rearrange("p (b t) -> p b t", t=128)), reads=["ps_q"], writes=["qT"])
        if ti + 1 < len(tiles):
            tr(ti + 1)
        for half in range(2):
            for b8 in range(8):
                blk = half * 8 + b8
                P.op("tensor", lambda e, blk=blk, b8=b8: e.matmul(ps_s[:, b8 * 128:(b8 + 1) * 128], lhsT=qT[:, blk, :], rhs=keysT[:, blk, :],
                                                               start=True, stop=True), reads=["qT", "keysT"], writes=["ps_s"], inc=(b8 == 7))
            P.op("scalar", lambda e, sc=sc, half=half: e.copy(out=sc[:, half * 1024:(half + 1) * 1024], in_=ps_s[:]), reads=["ps_s"], writes=[ksc])
        P.dma("scalar", lambda e, i=i, sc=sc: e.dma_start(out=T["SC"][i * 128:(i + 1) * 128, :], in_=sc[:]), reads=[ksc], writes=["SC"])
    P.end_stage()


def stage_E(P, l, T):
    last = l == DEPTH - 1
    P.begin_stage()
    NT = TT // 128
    NG = 10
    ident_f = P.sb("f_idf", [128, 128], F32)
    ident_b = P.sb("f_idb", [128, 128], BF16)
    iota16 = P.sb("f_iota", [128, 16], F32)
    bc = {k: P.sb("f_bc_" + k, [128, D], F32) for k in ("g", "lng", "lnb")}
    x1b = [P.sb(f"f_x1{i}", [128, D], F32) for i in range(2)]
    rb = [P.sb(f"f_r{i}", [128, D], F32) for i in range(2)]
    xmbb = [P.sb(f"f_xmb{i}", [128, D], BF16) for i in range(2)]
    sc = P.sb("f_sc", [128, D], F32)
    wk = P.sb("f_wk", [128, D], F32)
    cand = P.sb("f_cand", [128, D], F32)
    sv = P.sb("f_sv", [128, 16, 16], F32)
    si = P.sb("f_si", [128, 16, 16], U32)
    sif = P.sb("f_sif", [128, 16, 16], F32)
    tv = P.sb("f_tv", [128, 8, 16], F32)
    pos = P.sb("f_pos", [128, 8, 16], U32)
    posu = P.sb("f_posu", [128, 8, 16], U32)
    af = P.sb("f_af", [128, 8, 16], F32)
    bf = P.sb("f_bf", [128, 8, 16], F32)
    sel0 = P.sb("f_sel0", [128, 8, 16], F32)
    sel1 = P.sb("f_sel1", [128, 8, 16], F32)
    idxf = P.sb("f_idxf", [128, 128], F32)
    zz = P.sb("f_zz", [128, 8], F32)
    idxb = [P.sb(f"f_idx{i}", [128, 128], I32) for i in range(2)]
    ggb = [P.sb(f"f_gg{i}", [128, 8, 16], F32) for i in range(2)]
    dotb = [P.sb(f"f_dot{i}", [128, 128], F32) for i in range(2)]
    gelb = [P.sb(f"f_gel{i}", [128, 128], F32) for i in range(2)]
    aab = [P.sb(f"f_aa{i}", [128, 128], F32) for i in range(2)]
    gbuf = [P.sb(f"f_g{i}", [128, 2 * D], BF16) for i in range(NG)]
    diag = [P.sb(f"f_dg{i}", [128, 128], BF16) for i in range(NG)]
    junkl = [P.sb(f"f_junk{i}", [128, D], BF16) for i in range(3)]
    bst = P.sb("f_bst", [128, 24], F32)
    mv = P.sb("f_mv", [128, 2], F32)
    rs = P.sb("f_rs", [128, 1], F32)
    ps_f = [P.ps(f"f_psf{i}", [128, D]) for i in range(2)]
    _make_ident(P, ident_f, ident_b)
    P.op("gpsimd", lambda e: e.iota(iota16[:], pattern=[[1, 16]], base=0, channel_multiplier=0,
                                    allow_small_or_imprecise_dtypes=True), writes=["iota16"])
    P.dma("sync", lambda e: e.dma_start(out=bc["lng"][:], in_=T["ln2_g"][l:l + 1, :].partition_broadcast(128)), writes=["lng"])
    P.dma("sync", lambda e: e.dma_start(out=bc["lnb"][:], in_=T["ln2_b"][l:l + 1, :].partition_broadcast(128)), writes=["lnb"])
    modv = T["modv"][l]
    tiles = list(range(2, NT)) if last else list(range(NT))
    UVflat = T["UVb"][l].rearrange("e t d -> e (t d)")

    def route(ti, i):
        par = ti % 2
        gg, kgg = ggb[par], f"gg{par}"
        ii, kii = idxb[par], f"idx{par}"
        ksc = "sc"
        P.dma("sync", lambda e: e.dma_start(out=sc[:], in_=T["SC"][i * 128:(i + 1) * 128, :]), reads=["SC"], writes=[ksc])
        yield
        yield from _top16_multi(P, [(sc[:, blk * 128:(blk + 1) * 128], ksc, wk[:, blk * 128:(blk + 1) * 128], [("wk", blk)], sv[:, blk, :], ("sv", blk),
                                     si[:, blk, :], ("si", blk)) for blk in range(16)])
        svk = [("sv", b_) for b_ in range(16)]
        sik = [("si", b_) for b_ in range(16)]
        P.op("vector", lambda e: e.tensor_copy(out=sif[:], in_=si[:]), reads=sik, writes=["sif"])
        yield
        sv4 = sv[:].rearrange("p (h t) a -> p h t a", t=2)
        sif4 = sif[:].rearrange("p (h t) a -> p h t a", t=2)
        c4 = lambda ap: ap[:].rearrange("p (h a b) -> p h a b", a=16, b=16)
        bA = lambda v4: v4[:, :, 0, :].unsqueeze(3).to_broadcast([128, 8, 16, 16])
        bB = lambda v4: v4[:, :, 1, :].unsqueeze(2).to_broadcast([128, 8, 16, 16])
        P.op("vector", lambda e: e.tensor_tensor(out=c4(cand), in0=bA(sv4), in1=bB(sv4), op=ALU.add), reads=svk, writes=["cand"])
        yield
        yield from _top16_multi(P, [(cand[:, h * 256:(h + 1) * 256], "cand", wk[:, h * 256:(h + 1) * 256], [("wk", 2 * h), ("wk", 2 * h + 1)],
                                     tv[:, h, :], ("tv", h), pos[:, h, :], ("pos", h)) for h in range(8)])
        tvk = [("tv", h) for h in range(8)]
        posk = [("pos", h) for h in range(8)]
        P.op("vector", lambda e: e.tensor_tensor(out=gg[:], in0=tv[:], in1=tv[:, :, 0:1].to_broadcast([128, 8, 16]), op=ALU.subtract),
             reads=tvk, writes=[kgg])
        yield
        P.op("scalar", lambda e: e.activation(out=gg[:], in_=gg[:], func=AF.Exp), reads=[kgg], writes=[kgg])
        P.op("vector", lambda e: e.tensor_single_scalar(out=posu[:], in_=pos[:], scalar=4, op=ALU.logical_shift_right), reads=posk, writes=["posu"])
        yield
        P.op("vector", lambda e: e.tensor_copy(out=af[:], in_=posu[:]), reads=["posu"], writes=["af"])
        yield
        P.op("vector", lambda e: e.tensor_single_scalar(out=posu[:], in_=pos[:], scalar=15, op=ALU.bitwise_and), reads=posk + ["af"], writes=["posu"])
        yield
        P.op("vector", lambda e: e.tensor_copy(out=bf[:], in_=posu[:]), reads=["posu"], writes=["bf"])
        yield
        P.op("vector", lambda e: e.reduce_sum(out=zz[:], in_=gg[:], axis=AX.X), reads=[kgg], writes=["zz"])
        yield
        P.op("vector", lambda e: e.reciprocal(out=zz[:], in_=zz[:]), reads=["zz"], writes=["zz"])
        yield
        io4 = iota16[:].unsqueeze(1).unsqueeze(1).to_broadcast([128, 8, 16, 16])
        for (src, half, dst, dk_) in ((af, 0, sel0, "sel0"), (bf, 1, sel1, "sel1")):
            oh = c4(sc)
            P.op("vector", lambda e, src=src, oh=oh: e.tensor_tensor(out=oh, in0=io4, in1=src[:].unsqueeze(3).to_broadcast([128, 8, 16, 16]),
                                                                   op=ALU.is_equal), reads=["iota16", "af", "bf"], writes=[ksc])
            yield
            P.op("vector", lambda e, half=half, oh=oh: e.tensor_tensor(
                out=oh, in0=oh, in1=sif4[:, :, half, :].unsqueeze(2).to_broadcast([128, 8, 16, 16]), op=ALU.mult),
                reads=[ksc, "sif"], writes=[ksc])
            yield
            P.op("vector", lambda e, dst=dst, oh=oh: e.tensor_reduce(out=dst[:], in_=oh, axis=AX.X, op=ALU.add), reads=[ksc], writes=[dk_])
            yield
        P.op("vector", lambda e: e.tensor_tensor(out=gg[:], in0=gg[:], in1=zz[:].unsqueeze(2).to_broadcast([128, 8, 16]), op=ALU.mult),
             reads=[kgg, "zz"], writes=[kgg])
        yield
        P.op("vector", lambda e: e.scalar_tensor_tensor(out=idxf[:].rearrange("p (h k) -> p h k", k=16), in0=sel0[:], scalar=128.0, in1=sel1[:],
                                                       op0=ALU.mult, op1=ALU.add), reads=["sel0", "sel1"], writes=["idxf"])
        yield
        P.op("vector", lambda e: e.tensor_copy(out=ii[:], in_=idxf[:]), reads=["idxf"], writes=[kii])
        yield
        if T.get("dbg_route"):
            P.dma("scalar", lambda e: e.dma_start(out=T["IDX"][i * 128:(i + 1) * 128, :], in_=ii[:]), reads=[kii], writes=["IDX"])
            P.dma("scalar", lambda e: e.dma_start(out=T["GG"][i * 128:(i + 1) * 128, :], in_=gg[:].rearrange("p h k -> p (h k)")),
                  reads=[kgg], writes=["GG"])

    for _ in route(0, tiles[0]):
        pass
    gi = 0
    for ti, i in enumerate(tiles):
        if i in (0, 2):
            row = 1 if i == 0 else 0
            _bcast_load(P, bc["g"], "bcg", modv[row:row + 1, 5 * D:6 * D])
        par = ti % 2
        x1t, kx = x1b[par], f"x1b{par}"
        xmb, kxmb = xmbb[par], f"xmb{par}"
        idx, kidx = idxb[par], f"idx{par}"
        gg, kgg = ggb[par], f"gg{par}"
        ggf = gg[:].rearrange("p h k -> p (h k)")
        dot, gel, aa = dotb[par], gelb[par], aab[par]
        pf, kpf = ps_f[par], f"psf{par}"
        r, kr = rb[par], f"r{par}"
        P.dma("sync", lambda e, i=i, xmb=xmb: e.dma_start(out=xmb[:], in_=T["XMB"][i * 128:(i + 1) * 128, :]), reads=["XMB"], writes=[kxmb])
        P.dma("sync", lambda e, i=i, x1t=x1t: e.dma_start(out=x1t[:], in_=T["x1"][i * 128:(i + 1) * 128, :]), reads=["x1"], writes=[kx])
        gen = route(ti + 1, tiles[ti + 1]) if ti + 1 < len(tiles) else None
        for ee in range(128):
            j = gi % NG
            gi += 1
            gb_, gk = gbuf[j], ("gbuf", j)
            dg, dk_ = diag[j], ("diag", j)
            P.dma("gpsimd", lambda e, gb_=gb_, ee=ee, idx=idx: e.indirect_dma_start(
                out=gb_[:], out_offset=None, in_=UVflat,
                in_offset=bass.IndirectOffsetOnAxis(ap=idx[:, ee:ee + 1], axis=0)), reads=[kidx, "UVb"], writes=[gk])
            junk, jk = junkl[gi % 3], f"junk{gi % 3}"
            P.op("vector", lambda e, gb_=gb_, ee=ee, xmb=xmb, dot=dot, junk=junk: e.scalar_tensor_tensor(
                out=junk[:], in0=gb_[:, 0:D], scalar=1.0, in1=xmb[:], op0=ALU.mult, op1=ALU.mult, accum_out=dot[:, ee:ee + 1]),
                reads=[gk, kxmb], writes=[jk, ("dot", par, ee)])
            if gen is not None:
                for _ in range(2):
                    try:
                        next(gen)
                    except StopIteration:
                        gen = None
                        break
            P.op("scalar", lambda e, ee=ee, dot=dot, gel=gel: e.activation(out=gel[:, ee:ee + 1], in_=dot[:, ee:ee + 1], func=AF.Gelu),
                 reads=[("dot", par, ee)], writes=[("gel", par, ee)])
            P.op("scalar", lambda e, ee=ee, gel=gel, aa=aa, ggf=ggf: e.activation(out=aa[:, ee:ee + 1], in_=gel[:, ee:ee + 1], func=AF.Copy,
                                                                            scale=ggf[:, ee:ee + 1]),
                 reads=[("gel", par, ee), kgg], writes=[("aa", par, ee)])
            P.op("scalar", lambda e, ee=ee, aa=aa, dg=dg: e.activation(out=dg[:], in_=ident_b[:], func=AF.Copy, scale=aa[:, ee:ee + 1]),
                 reads=[("aa", par, ee), "ident_b"], writes=[dk_])
            for nq in range(4):
                P.op("tensor", lambda e, nq=nq, dg=dg, gb_=gb_, pf=pf, ee=ee: e.matmul(
                    pf[:, nq * 512:(nq + 1) * 512], lhsT=dg[:], rhs=gb_[:, D + nq * 512:D + (nq + 1) * 512],
                    start=(ee == 0), stop=(ee == 127)), reads=[dk_, gk], writes=[kpf], inc=(nq == 3))
        if gen is not None:
            for _ in gen:
                pass
        if T.get("dbg_f") is not None:
            P.op("vector", lambda e, pf=pf, r=r: e.tensor_copy(out=r[:], in_=pf[:]), reads=[kpf], writes=[kr])
            P.dma("scalar", lambda e, i=i, r=r: e.dma_start(out=T["dbg_f"][i * 128:(i + 1) * 128, :], in_=r[:]), reads=[kr], writes=["dbg_f"])
        if last:
            dst = T["out"][(i - 2) * 128:(i - 1) * 128, :]
        else:
            dst = T["xcur"][i * 128:(i + 1) * 128, :]
        _residual_ln(P, "f_", pf[:], kpf, x1t, kx, bc["g"], "bcg", bc["lng"], bc["lnb"], r, kr, bst, mv, rs, r[:], kr, dst, aff_eng="vector", res_eng="vector")
    P.end_stage()


def build(stop_after=None, dbg=(), only=None, ext_in=()):
    nc = bass.Bass("TRN2", target_bir_lowering=False)
    T = {}

    def inp(name, shape, dtype=F32):
        T[name] = nc.dram_tensor(name, list(shape), dtype, kind="ExternalInput").ap()

    def scr(name, shape, dtype=F32):
        kind = "ExternalOutput" if name in dbg else ("ExternalInput" if name in ext_in else "Internal")
        T[name] = nc.dram_tensor(name, list(shape), dtype, kind=kind).ap()

    inp("x", [SEQ, D]); inp("c", [D]); inp("ctx", [CTXL, D]); inp("c_ctx", [D])
    inp("w_mod", [DEPTH, D, 6 * D]); inp("b_mod", [DEPTH, 6 * D]); inp("w_in", [DEPTH, D, INW])
    inp("conv_w", [DEPTH, 3, DC]); inp("conv_b", [DEPTH, DC]); inp("lb_raw", [DEPTH, 2, DC])
    inp("hg_norm_g", [DEPTH, 128]); inp("w_pa", [DEPTH, DC, D]); inp("w_pb", [DEPTH, DC, D]); inp("w_o", [DEPTH, D, D])
    inp("ln1_g", [DEPTH, D]); inp("ln1_b", [DEPTH, D]); inp("peer_wq", [DEPTH, D, D])
    inp("peer_keys", [DEPTH, 8, 2, 128, 128]); inp("peer_u", [DEPTH, NEXP, D]); inp("peer_v", [DEPTH, NEXP, D])
    inp("ln2_g", [DEPTH, D]); inp("ln2_b", [DEPTH, D])
    T["out"] = nc.dram_tensor("out", [SEQ, D], F32, kind="ExternalOutput").ap()
    scr("modv", [DEPTH, 2, INW])
    scr("FM", [FM_ROWS, TT])
    scr("VT", [TT, DC], BF16)
    scr("xcur", [TT, D])
    scr("YB", [DC, TT], BF16)
    scr("MT", [D, TT], BF16)
    scr("IDX", [TT, 128], I32)
    scr("GG", [TT, 128])
    scr("XMB", [TT, D], BF16)
    scr("SC", [TT, D])
    T["dbg_route"] = ("IDX" in dbg)
    T["UVb"] = [nc.dram_tensor(f"UVb{l}", [NEXP, 2, D], BF16, kind="Internal").ap() for l in range(DEPTH)]
    scr("x1", [TT, D])
    for nm, shp, dt_ in (("dbg_ya", [DC, TT], BF16), ("dbg_idx", [TT, 128], I32), ("dbg_g", [TT, 128], F32), ("dbg_f", [TT, D], F32)):
        if nm in dbg:
            scr(nm, shp, dt_)
    if "dbg_h" in dbg:
        scr("dbg_h", [D, TT], BF16)
    if "dbg_o" in dbg:
        scr("dbg_o", [DC, TT])

    P = Prog(nc)
    stages = []
    for l in range(DEPTH):
        stages += [("M", l), ("A", l), ("B", l), ("C", l), ("D", l), ("Q", l), ("E", l)]
    if only is not None:
        stages = list(only)
    for (s, l) in stages:
        if only is not None and (s, l) not in only:
            continue
        if s == "M":
            stage_mod(P, l, T)
        elif s == "A":
            stage_A(P, l, T)
        elif s == "B":
            stage_B(P, l, T)
        elif s == "C":
            stage_C(P, l, T)
        elif s == "D":
            stage_D(P, l, T)
        elif s == "Q":
            stage_Q(P, l, T)
        elif s == "E":
            stage_E(P, l, T)
        elif s == "T":
            stage_T(P, l, T)
        if stop_after == (s, l):
            break
    P.close()
    return nc, P


def make_in_maps(inputs):
    shared = {k: np.ascontiguousarray(inputs[k]) for k in (
        "c_ctx", "w_mod", "b_mod", "w_in", "conv_w", "conv_b", "lb_raw", "hg_norm_g", "w_pa", "w_pb", "w_o",
        "ln1_g", "ln1_b", "peer_wq", "peer_keys", "peer_u", "peer_v", "ln2_g", "ln2_b")}
    maps = []
    for b in range(NCORE):
        m = dict(shared)
        m["x"] = np.ascontiguousarray(inputs["x"][b])
        m["c"] = np.ascontiguousarray(inputs["c"][b])
        m["ctx"] = np.ascontiguousarray(inputs["ctx"][b])
        maps.append(m)
    return maps


def kernel(**inputs):
    inputs = {k: np.asarray(v, dtype=np.float32) for k, v in inputs.items()}
    nc, _ = build()
    res = run_bass_kernel_spmd(nc, make_in_maps(inputs), core_ids=list(range(NCORE)))
    return np.stack([np.asarray(r["out"], dtype=np.float32) for r in res.results], axis=0)
```
